# Optimizing a Trainium2 kernel written in Bass

```python
import math
import jax, jax.numpy as jnp
from jax import lax
import numpy as np

D_MODEL = 1024
BATCH = 8
SEQ = 4096
DEPTH = 4
DEC_BATCH = 2
DEC_SEQ = 8192
PAST_LEN = 128

N_EVEN = (DEPTH + 1) // 2
N_ODD = DEPTH // 2
MIX_WIDTH = D_MODEL
D_FF = 2816
RMS_EPS = 1e-6

D_A = MIX_WIDTH // 2
S5_GROUP = 16
S5_GROUPS = D_A // S5_GROUP
S5_STATE = 64

D_B = MIX_WIDTH - D_A
RW_HEAD = 64
RW_HEADS = D_B // RW_HEAD
RW_DECAY_LORA = D_MODEL // 32
RW_A_LORA = D_MODEL // 32
RW_GATE_LORA = D_MODEL // 16
RW_IN = 3 * D_B + 2 * RW_DECAY_LORA + RW_A_LORA + RW_GATE_LORA
RW_GN_EPS = 64e-5
AB_IN = D_A + RW_IN

D_C = MIX_WIDTH // 2
LRU_HEADS = 8
LRU_HEAD_DIM = D_C // LRU_HEADS
LRU_CONV = 4
LRU_PAD_LO = LRU_CONV // 2
LRU_PAD_HI = LRU_CONV - 1 - LRU_PAD_LO
LRU_C = 8.0

D_D = MIX_WIDTH - D_C
HY_ORDER = 2
HY_SHORT = 3
HY_POS_EMB = 33
HY_FILTER_HIDDEN = 64
HY_DECAY_PCT_SHORT = 0.3
HY_DECAY_PCT_LONG = 1.5
HY_DECAY_TARGET = 1e-2
CD_IN = 2 * D_C + (HY_ORDER + 1) * D_D

kernel_name = 'hybrid_bidir_s5_rwkv7_rglru_hyena'


def _rmsnorm(x, g):
    x32 = x.astype(jnp.float32)
    y = x32 * lax.rsqrt(jnp.mean(x32 * x32, axis=-1, keepdims=True) + RMS_EPS)
    return (y * g.astype(jnp.float32)).astype(x.dtype)


def _swiglu(x, w_gate, w_up, w_down):
    return (jax.nn.silu(x @ w_gate) * (x @ w_up)) @ w_down


def _depthwise_conv(x, w, b, pad_lo, pad_hi):
    c = x.shape[-1]
    y = lax.conv_general_dilated(x, w[:, None, :].astype(x.dtype), window_strides=(1,),
                                 padding=[(pad_lo, pad_hi)],
                                 dimension_numbers=('NWC', 'WIO', 'NWC'),
                                 feature_group_count=c)
    return y + b.astype(x.dtype)


def _linear_combine(e1, e2):
    a1, b1 = e1
    a2, b2 = e2
    return a1 * a2, a2 * b1 + b2


def _complex_linear_combine(e1, e2):
    ar1, ai1, br1, bi1 = e1
    ar2, ai2, br2, bi2 = e2
    ar = ar2 * ar1 - ai2 * ai1
    ai = ar2 * ai1 + ai2 * ar1
    br = ar2 * br1 - ai2 * bi1 + br2
    bi = ar2 * bi1 + ai2 * br1 + bi2
    return ar, ai, br, bi


def _s5_direction(u, lam_re, lam_im, log_step, b_re, b_im, c_re, c_im, reverse):
    f32 = jnp.float32
    lam_re = jnp.minimum(lam_re.astype(f32), -1e-4)
    lam_im = lam_im.astype(f32)
    step = jnp.exp(log_step.astype(f32))[:, None]
    mag = jnp.exp(lam_re * step)
    ang = lam_im * step
    abar_re = mag * jnp.cos(ang)
    abar_im = mag * jnp.sin(ang)
    den = lam_re * lam_re + lam_im * lam_im
    num_re = abar_re - 1.0
    coef_re = (num_re * lam_re + abar_im * lam_im) / den
    coef_im = (abar_im * lam_re - num_re * lam_im) / den
    b_re = b_re.astype(f32)
    b_im = b_im.astype(f32)
    bbar_re = coef_re[..., None] * b_re - coef_im[..., None] * b_im
    bbar_im = coef_re[..., None] * b_im + coef_im[..., None] * b_re
    bu_re = jnp.einsum('blgc,gnc->blgn', u, bbar_re)
    bu_im = jnp.einsum('blgc,gnc->blgn', u, bbar_im)
    a_re = jnp.broadcast_to(abar_re, bu_re.shape)
    a_im = jnp.broadcast_to(abar_im, bu_re.shape)
    _, _, h_re, h_im = lax.associative_scan(_complex_linear_combine, (a_re, a_im, bu_re, bu_im),
                                            reverse=reverse, axis=1)
    return (jnp.einsum('blgn,gcn->blgc', h_re, c_re.astype(f32))
            - jnp.einsum('blgn,gcn->blgc', h_im, c_im.astype(f32)))


def _s5_mixer(u, lam_re, lam_im, log_step, b_re, b_im, c_re, c_im, d, glu_w, glu_b):
    bsz, seq_len, _ = u.shape
    u32 = u.astype(jnp.float32)
    ug = u32.reshape(bsz, seq_len, S5_GROUPS, S5_GROUP)
    y = _s5_direction(ug, lam_re[0], lam_im[0], log_step[0], b_re[0], b_im[0], c_re[0], c_im[0], False)
    y = y + _s5_direction(ug, lam_re[1], lam_im[1], log_step[1], b_re[1], b_im[1], c_re[1], c_im[1], True)
    y = y.reshape(bsz, seq_len, D_A) + d.astype(jnp.float32) * u32
    y = jax.nn.gelu(y).astype(u.dtype)
    return y * jax.nn.sigmoid(y @ glu_w + glu_b)


def _centred_shift(p):
    prev = jnp.pad(p[:, :-1], ((0, 0), (1, 0), (0, 0)))
    nxt = jnp.pad(p[:, 1:], ((0, 0), (0, 1), (0, 0)))
    return 0.5 * (prev + nxt)


def _rwkv7_step(state, inp):
    r, w, k, v, kk, a = inp
    sa = jnp.einsum('bhvk,bhk->bhv', state, -kk)
    state = (state * w[:, :, None, :] + sa[..., None] * (kk * a)[:, :, None, :]
             + v[..., None] * k[:, :, None, :])
    y = jnp.einsum('bhvk,bhk->bhv', state, r)
    return state, y


def _rwkv7_mixer(p, mu, w0, w_up, a0, a_up, g_up, k_k, k_a, r_k, ln_w, ln_b):
    f32 = jnp.float32
    bsz, seq_len, _ = p.shape
    out_dtype = p.dtype
    p = (p + (_centred_shift(p) - p) * mu).astype(f32)
    s0 = 3 * D_B
    splits = [D_B, 2 * D_B, s0, s0 + RW_DECAY_LORA, s0 + 2 * RW_DECAY_LORA,
              s0 + 2 * RW_DECAY_LORA + RW_A_LORA]
    r, k, v, xw_f, xw_b, xa, xg = jnp.split(p, splits, axis=-1)
    a = jax.nn.sigmoid(a0 + xa @ a_up)
    g = jax.nn.sigmoid(xg) @ g_up

    def heads(t):
        return t.astype(f32).reshape(bsz, seq_len, RW_HEADS, RW_HEAD)

    kk = heads(k * k_k)
    kk = kk / jnp.maximum(jnp.sqrt(jnp.sum(kk * kk, axis=-1, keepdims=True)), 1e-12)
    k = k * (1.0 + (a - 1.0) * k_a)
    r_h, k_h, v_h, a_h = heads(r), heads(k), heads(v), heads(a)

    def tm(t):
        return jnp.swapaxes(t, 0, 1)

    state0 = jnp.zeros((bsz, RW_HEADS, RW_HEAD, RW_HEAD), f32)
    y = None
    for direction, xw in enumerate((xw_f, xw_b)):
        wl = -jax.nn.softplus(-(w0[direction] + jnp.tanh(xw) @ w_up[direction])) - 0.5
        decay = heads(jnp.exp(-jnp.exp(wl)))
        _, yd = lax.scan(_rwkv7_step, state0,
                         (tm(r_h), tm(decay), tm(k_h), tm(v_h), tm(kk), tm(a_h)),
                         reverse=(direction == 1))
        y = yd if y is None else y + yd
    y = jnp.swapaxes(y, 0, 1)
    mean = jnp.mean(y, axis=-1, keepdims=True)
    var = jnp.mean((y - mean) ** 2, axis=-1, keepdims=True)
    y = (y - mean) * lax.rsqrt(var + RW_GN_EPS)
    y = y * ln_w.reshape(RW_HEADS, RW_HEAD) + ln_b.reshape(RW_HEADS, RW_HEAD)
    y = y + jnp.sum(r_h * k_h * r_k, axis=-1, keepdims=True) * v_h
    return (y.reshape(bsz, seq_len, D_B) * g).astype(out_dtype)


def _rglru_mixer(xb, gb, conv_w, conv_b, lam, wa, ba, wx, bx):
    f32 = jnp.float32
    bsz, seq_len, _ = xb.shape
    xc = _depthwise_conv(xb, conv_w, conv_b, LRU_PAD_LO, LRU_PAD_HI).astype(f32)
    xh = xc.reshape(bsz, seq_len, LRU_HEADS, LRU_HEAD_DIM)
    h_sum = None
    for direction in range(2):
        gate_r = jax.nn.sigmoid(jnp.einsum('blhi,hij->blhj', xh, wa[direction].astype(f32))
                                .reshape(bsz, seq_len, D_C) + ba[direction])
        gate_i = jax.nn.sigmoid(jnp.einsum('blhi,hij->blhj', xh, wx[direction].astype(f32))
                                .reshape(bsz, seq_len, D_C) + bx[direction])
        log_a = -LRU_C * gate_r * jax.nn.softplus(-lam[direction].astype(f32))
        a = jnp.exp(log_a)
        mult = jnp.sqrt(-jnp.expm1(2.0 * log_a))
        _, h = lax.associative_scan(_linear_combine, (a, mult * gate_i * xc),
                                    reverse=(direction == 1), axis=1)
        h_sum = h if h_sum is None else h_sum + h
    return (h_sum * jax.nn.gelu(gb.astype(f32))).astype(xb.dtype)


def _hyena_filters(seq_len, f_w1, f_b1, f_w2, f_b2, f_freq, f_w3):
    f32 = jnp.float32
    t = jnp.linspace(0.0, 1.0, seq_len, dtype=f32)[:, None]
    bands = (HY_POS_EMB - 1) // 2
    freqs = jnp.linspace(1e-4, bands - 1, bands, dtype=f32)[None, :]
    wpos = (2.0 * math.pi / seq_len) * jnp.arange(seq_len, dtype=f32)[:, None]
    z = jnp.concatenate([t, jnp.cos(freqs * wpos), -jnp.sin(freqs * wpos)], axis=-1)
    h = jnp.sin(f_freq[0].astype(f32) * (z @ f_w1.astype(f32) + f_b1.astype(f32)))
    h = jnp.sin(f_freq[1].astype(f32) * (h @ f_w2.astype(f32) + f_b2.astype(f32)))
    h = (h @ f_w3.astype(f32)).reshape(seq_len, 2, HY_ORDER, D_D)
    max_decay = math.log(HY_DECAY_TARGET) / HY_DECAY_PCT_SHORT
    min_decay = math.log(HY_DECAY_TARGET) / HY_DECAY_PCT_LONG
    deltas = jnp.abs(jnp.linspace(min_decay, max_decay, D_D, dtype=f32))
    h = h * jnp.exp(-t * deltas)[:, None, None, :]
    h_fwd, h_bwd = h[:, 0], h[:, 1]
    k = jnp.concatenate([h_fwd, jnp.zeros((1, HY_ORDER, D_D), f32), h_bwd[:0:-1]], axis=0)
    k = k / jnp.sum(jnp.abs(k), axis=0, keepdims=True)
    return jnp.fft.rfft(k, axis=0)


def _hyena_mixer(p, conv_w, conv_b, filt_f, bias):
    f32 = jnp.float32
    bsz, seq_len, _ = p.shape
    pc = _depthwise_conv(p, conv_w, conv_b, 1, 1).astype(f32)
    v, x1, x2 = jnp.split(pc, 3, axis=-1)
    z = v
    for order, gate in enumerate((x1, x2)):
        zf = jnp.fft.rfft(z, n=2 * seq_len, axis=1)
        zc = jnp.fft.irfft(zf * filt_f[None, :, order, :], n=2 * seq_len, axis=1)[:, :seq_len]
        z = gate * (zc + bias[order].astype(f32) * z)
    return z.astype(p.dtype)


def setup_inputs(seed: int = 0) -> dict:
    key = jax.random.key(seed)
    ks = iter(jax.random.split(key, 64))
    f32 = jnp.float32

    def nrm(shape, scale):
        return scale * jax.random.normal(next(ks), shape, f32)

    def unif(shape, lo, hi):
        return jax.random.uniform(next(ks), shape, f32, lo, hi)

    NE, NO = N_EVEN, N_ODD
    inp = {}
    inp['x_prompt'] = nrm((BATCH, SEQ, D_MODEL), 1.0)
    inp['x_sample'] = nrm((DEC_BATCH, DEC_SEQ, D_MODEL), 1.0)
    inp['ffn1_norm'] = 1.0 + nrm((DEPTH, D_MODEL), 0.02)
    inp['ffn1_w_gate'] = nrm((DEPTH, D_MODEL, D_FF), D_MODEL ** -0.5)
    inp['ffn1_w_up'] = nrm((DEPTH, D_MODEL, D_FF), D_MODEL ** -0.5)
    inp['ffn1_w_down'] = nrm((DEPTH, D_FF, D_MODEL), D_FF ** -0.5)
    inp['mix_norm'] = 1.0 + nrm((DEPTH, D_MODEL), 0.02)
    inp['ffn2_norm'] = 1.0 + nrm((DEPTH, D_MODEL), 0.02)
    inp['ffn2_w_gate'] = nrm((DEPTH, D_MODEL, D_FF), D_MODEL ** -0.5)
    inp['ffn2_w_up'] = nrm((DEPTH, D_MODEL, D_FF), D_MODEL ** -0.5)
    inp['ffn2_w_down'] = nrm((DEPTH, D_FF, D_MODEL), D_FF ** -0.5)
    inp['ab_w_in'] = nrm((NE, D_MODEL, AB_IN), D_MODEL ** -0.5)
    inp['ab_w_out'] = nrm((NE, MIX_WIDTH, D_MODEL), MIX_WIDTH ** -0.5)
    inp['s5_lambda_re'] = -0.5 + nrm((NE, 2, S5_GROUPS, S5_STATE), 0.01)
    inp['s5_lambda_im'] = math.pi * jnp.arange(S5_STATE, dtype=f32) + nrm((NE, 2, S5_GROUPS, S5_STATE), 0.01)
    inp['s5_log_step'] = unif((NE, 2, S5_GROUPS), math.log(1e-3), math.log(1e-1))
    inp['s5_b_re'] = nrm((NE, 2, S5_GROUPS, S5_STATE, S5_GROUP), (2 * S5_GROUP) ** -0.5)
    inp['s5_b_im'] = nrm((NE, 2, S5_GROUPS, S5_STATE, S5_GROUP), (2 * S5_GROUP) ** -0.5)
    inp['s5_c_re'] = nrm((NE, 2, S5_GROUPS, S5_GROUP, S5_STATE), S5_STATE ** -0.5)
    inp['s5_c_im'] = nrm((NE, 2, S5_GROUPS, S5_GROUP, S5_STATE), S5_STATE ** -0.5)
    inp['s5_d'] = nrm((NE, D_A), 1.0)
    inp['s5_glu_w'] = nrm((NE, D_A, D_A), D_A ** -0.5)
    inp['s5_glu_b'] = nrm((NE, D_A), 0.01)
    inp['rw_mu'] = unif((NE, RW_IN), 0.0, 1.0)
    inp['rw_w0'] = jnp.linspace(-6.0, -1.0, D_B, dtype=f32) + nrm((NE, 2, D_B), 0.1)
    inp['rw_w_up'] = nrm((NE, 2, RW_DECAY_LORA, D_B), 0.1)
    inp['rw_a0'] = nrm((NE, D_B), 0.1)
    inp['rw_a_up'] = nrm((NE, RW_A_LORA, D_B), RW_A_LORA ** -0.5)
    inp['rw_g_up'] = nrm((NE, RW_GATE_LORA, D_B), RW_GATE_LORA ** -0.5)
    inp['rw_k_k'] = 0.85 + nrm((NE, D_B), 0.05)
    inp['rw_k_a'] = 1.0 + nrm((NE, D_B), 0.05)
    inp['rw_r_k'] = -0.04 + nrm((NE, RW_HEADS, RW_HEAD), 0.02)
    inp['rw_ln_w'] = 1.0 + nrm((NE, D_B), 0.02)
    inp['rw_ln_b'] = nrm((NE, D_B), 0.01)
    inp['cd_w_in'] = nrm((NO, D_MODEL, CD_IN), D_MODEL ** -0.5)
    inp['cd_w_out'] = nrm((NO, MIX_WIDTH, D_MODEL), MIX_WIDTH ** -0.5)
    inp['lru_conv_w'] = nrm((NO, LRU_CONV, D_C), LRU_CONV ** -0.5)
    inp['lru_conv_b'] = nrm((NO, D_C), 0.01)
    a_c = unif((NO, 2, D_C), 0.9, 0.999)
    a_base = a_c ** (1.0 / LRU_C)
    inp['lru_lambda'] = jnp.log(a_base) - jnp.log1p(-a_base)
    inp['lru_wa'] = nrm((NO, 2, LRU_HEADS, LRU_HEAD_DIM, LRU_HEAD_DIM), LRU_HEAD_DIM ** -0.5)
    inp['lru_ba'] = nrm((NO, 2, D_C), 0.01)
    inp['lru_wx'] = nrm((NO, 2, LRU_HEADS, LRU_HEAD_DIM, LRU_HEAD_DIM), LRU_HEAD_DIM ** -0.5)
    inp['lru_bx'] = nrm((NO, 2, D_C), 0.01)
    inp['hy_conv_w'] = nrm((NO, HY_SHORT, (HY_ORDER + 1) * D_D), HY_SHORT ** -0.5)
    inp['hy_conv_b'] = nrm((NO, (HY_ORDER + 1) * D_D), 0.01)
    inp['hy_f_w1'] = nrm((NO, HY_POS_EMB, HY_FILTER_HIDDEN), HY_POS_EMB ** -0.5)
    inp['hy_f_b1'] = nrm((NO, HY_FILTER_HIDDEN), 0.1)
    inp['hy_f_w2'] = nrm((NO, HY_FILTER_HIDDEN, HY_FILTER_HIDDEN), HY_FILTER_HIDDEN ** -0.5)
    inp['hy_f_b2'] = nrm((NO, HY_FILTER_HIDDEN), 0.1)
    inp['hy_f_freq'] = 1.0 + nrm((NO, 2, HY_FILTER_HIDDEN), 0.1)
    inp['hy_f_w3'] = nrm((NO, HY_FILTER_HIDDEN, 2 * HY_ORDER * D_D), HY_FILTER_HIDDEN ** -0.5)
    inp['hy_bias'] = nrm((NO, HY_ORDER, D_D), 1.0)
    inp['final_norm'] = 1.0 + nrm((D_MODEL,), 0.02)
    return inp


def reference(x_prompt, x_sample, ffn1_norm, ffn1_w_gate, ffn1_w_up, ffn1_w_down, mix_norm,
              ffn2_norm, ffn2_w_gate, ffn2_w_up, ffn2_w_down, ab_w_in, ab_w_out,
              s5_lambda_re, s5_lambda_im, s5_log_step, s5_b_re, s5_b_im, s5_c_re, s5_c_im,
              s5_d, s5_glu_w, s5_glu_b, rw_mu, rw_w0, rw_w_up, rw_a0, rw_a_up, rw_g_up,
              rw_k_k, rw_k_a, rw_r_k, rw_ln_w, rw_ln_b, cd_w_in, cd_w_out, lru_conv_w,
              lru_conv_b, lru_lambda, lru_wa, lru_ba, lru_wx, lru_bx, hy_conv_w, hy_conv_b,
              hy_f_w1, hy_f_b1, hy_f_w2, hy_f_b2, hy_f_freq, hy_f_w3, hy_bias, final_norm):

    def trunk(x):
        seq_len = x.shape[1]
        for layer in range(DEPTH):
            j = layer // 2
            x = x + 0.5 * _swiglu(_rmsnorm(x, ffn1_norm[layer]), ffn1_w_gate[layer],
                                  ffn1_w_up[layer], ffn1_w_down[layer])
            h = _rmsnorm(x, mix_norm[layer])
            if layer % 2 == 0:
                proj = h @ ab_w_in[j]
                y_a = _s5_mixer(proj[..., :D_A], s5_lambda_re[j], s5_lambda_im[j], s5_log_step[j],
                                s5_b_re[j], s5_b_im[j], s5_c_re[j], s5_c_im[j], s5_d[j],
                                s5_glu_w[j], s5_glu_b[j])
                y_b = _rwkv7_mixer(proj[..., D_A:], rw_mu[j], rw_w0[j], rw_w_up[j], rw_a0[j],
                                   rw_a_up[j], rw_g_up[j], rw_k_k[j], rw_k_a[j], rw_r_k[j],
                                   rw_ln_w[j], rw_ln_b[j])
                x = x + jnp.concatenate([y_a, y_b], axis=-1) @ ab_w_out[j]
            else:
                proj = h @ cd_w_in[j]
                y_c = _rglru_mixer(proj[..., :D_C], proj[..., D_C:2 * D_C], lru_conv_w[j],
                                   lru_conv_b[j], lru_lambda[j], lru_wa[j], lru_ba[j],
                                   lru_wx[j], lru_bx[j])
                filt_f = _hyena_filters(seq_len, hy_f_w1[j], hy_f_b1[j], hy_f_w2[j], hy_f_b2[j],
                                        hy_f_freq[j], hy_f_w3[j])
                y_d = _hyena_mixer(proj[..., 2 * D_C:], hy_conv_w[j], hy_conv_b[j], filt_f, hy_bias[j])
                x = x + jnp.concatenate([y_c, y_d], axis=-1) @ cd_w_out[j]
            x = x + 0.5 * _swiglu(_rmsnorm(x, ffn2_norm[layer]), ffn2_w_gate[layer],
                                  ffn2_w_up[layer], ffn2_w_down[layer])
        return _rmsnorm(x, final_norm)

    y_prompt = trunk(x_prompt)
    y_sample = trunk(x_sample)
    return (y_prompt, y_sample)
```

```python
import math
from contextlib import ExitStack
import numpy as np
import concourse.bass as bass
import concourse.mybir as mybir
from concourse.bass_utils import run_bass_kernel_spmd

F32 = mybir.dt.float32
BF16 = mybir.dt.bfloat16
I32 = mybir.dt.int32
ALU = mybir.AluOpType
AF = mybir.ActivationFunctionType
AX = mybir.AxisListType

D = 1024
DFF = 2816
DEPTH = 4
AB_IN = 2208
CD_IN = 2560
TT = 1024
EPS = 1e-6
MAGIC = 12582912.0
TWO_PI = 2.0 * math.pi
TBMAX = 2048


class Prog:
    ENG = ("pe", "act", "dve", "pool", "sp")
    NRING = 8
    LOOKBACK = 6

    def __init__(self, nc, stack):
        self.nc = nc
        self.cnt = {e: 0 for e in self.ENG}
        self.ringn = {e: 0 for e in self.ENG}
        self.waited = {e: {} for e in self.ENG}
        self.lastw = {}
        self.readers = {}
        self.nins = 0
        self.engobj = {"pe": nc.tensor, "act": nc.scalar, "dve": nc.vector, "pool": nc.gpsimd, "sp": nc.sync}
        self.sem = {e: stack.enter_context(nc.semaphore("s_" + e)) for e in self.ENG}
        self.ring = {e: [stack.enter_context(nc.semaphore("r_%s%d" % (e, i))) for i in range(self.NRING)]
                     for e in ("sp", "act", "pool")}

    def _semh(self, semk):
        if semk[0] == "eng":
            return self.sem[semk[1]]
        return self.ring[semk[1]][semk[2]]

    def _need(self, eng, tok, waits):
        if tok is None:
            return
        semk, val, teng, isdma = tok
        if (not isdma) and teng == eng and val <= self.cnt[eng] - self.LOOKBACK:
            return
        if self.waited[eng].get(semk, 0) >= val:
            return
        if waits.get(semk, 0) < val:
            waits[semk] = val

    def op(self, eng, fn, reads=(), writes=(), dma=False):
        waits = {}
        for k in reads:
            self._need(eng, self.lastw.get(k), waits)
        for k in writes:
            self._need(eng, self.lastw.get(k), waits)
            for t in self.readers.get(k, {}).values():
                self._need(eng, t, waits)
        if dma:
            i = self.ringn[eng]
            self.ringn[eng] += 1
            slot = i % self.NRING
            val = 16 * (i // self.NRING + 1)
            semk = ("ring", eng, slot)
            if i >= self.NRING:
                self._need(eng, (semk, val - 16, eng, True), waits)
            tok = (semk, val, eng, True)
        else:
            self.cnt[eng] += 1
            tok = (("eng", eng), self.cnt[eng], eng, False)
        e = self.engobj[eng]
        for semk, val in waits.items():
            self.waited[eng][semk] = val
            e.wait_ge(self._semh(semk), val)
        ins = fn(e)
        ins.then_inc(self._semh(tok[0]), 16 if dma else 1)
        self.nins += 1 + len(waits)
        for k in reads:
            self.readers.setdefault(k, {})[(eng, tok[0] if dma else 0)] = tok
        for k in writes:
            self.lastw[k] = tok
            self.readers[k] = {}
        return tok

    def _all_tokens(self):
        toks = []
        for q in self.ENG:
            if self.cnt[q] > 0:
                toks.append((("eng", q), self.cnt[q], q, False))
        for q in ("sp", "act", "pool"):
            n = self.ringn[q]
            for slot in range(min(n, self.NRING)):
                c = (n - 1 - slot) // self.NRING + 1
                toks.append((("ring", q, slot), 16 * c, q, True))
        return toks

    def barrier(self, engs=None):
        toks = self._all_tokens()
        for eng in (engs or self.ENG):
            e = self.engobj[eng]
            for semk, val, teng, isdma in toks:
                if self.waited[eng].get(semk, 0) >= val:
                    continue
                if (not isdma) and teng == eng:
                    continue
                self.waited[eng][semk] = val
                e.wait_ge(self._semh(semk), val)
                self.nins += 1
        if engs is None:
            self.lastw = {}
            self.readers = {}


class Rot:
    def __init__(self, bufs, name):
        self.bufs = bufs
        self.name = name
        self.i = 0

    def next(self):
        j = self.i % len(self.bufs)
        self.i += 1
        return self.bufs[j], (self.name, j)


def _wlayout(nco, R, cw):
    return [nco, 128, R // 128, cw]


class Builder:
    def __init__(self, LP, LS, enable, depth=DEPTH):
        self.LP, self.LS, self.enable, self.depth = LP, LS, enable, depth
        self.Lmax = max(LP, LS)
        self.nc = bass.Bass("TRN2", target_bir_lowering=False)
        self.inputs = {}
        self.scr = {}
        self.uid = 0

    def sbt(self, stack, name, shape, dt=F32):
        self.uid += 1
        return stack.enter_context(self.nc.sbuf_tensor("%s_u%d" % (name, self.uid), list(shape), dt))

    def din(self, name, shape, dt=F32):
        t = self.nc.dram_tensor(name, list(shape), dt, kind="ExternalInput").ap()
        self.inputs[name] = t
        return t

    def dscr(self, name, shape, dt=F32):
        t = self.nc.dram_tensor(name, list(shape), dt).ap()
        self.scr[name] = t
        return t

    def build(self):
        nc = self.nc
        LP, LS, Lmax = self.LP, self.LS, self.Lmax
        self.xp = self.din("xp", [LP, D])
        self.xs = self.din("xs", [LS, D])
        self.yp = nc.dram_tensor("yp", [LP, D], F32, kind="ExternalOutput").ap()
        self.ys = nc.dram_tensor("ys", [LS, D], F32, kind="ExternalOutput").ap()
        self.w_raw = {}
        for nm in ("ffn1_w_gate", "ffn1_w_up", "ffn2_w_gate", "ffn2_w_up"):
            self.w_raw[nm] = self.din(nm, [DEPTH, D, DFF])
        for nm in ("ffn1_w_down", "ffn2_w_down"):
            self.w_raw[nm] = self.din(nm, [DEPTH, DFF, D])
        self.w_raw["ab_w_in"] = self.din("ab_w_in", [2, D, AB_IN])
        self.w_raw["cd_w_in"] = self.din("cd_w_in", [2, D, CD_IN])
        self.w_raw["ab_w_out"] = self.din("ab_w_out", [2, D, D])
        self.w_raw["cd_w_out"] = self.din("cd_w_out", [2, D, D])
        self.norms = self.din("norms", [128, 3 * DEPTH + 1, 8])
        self.lru_cols = self.din("lru_cols", [2, 128, 4, 11])
        self.s5_prm = self.din("s5_prm", [2, 128, 32, 3])
        self.rw_mu = self.din("rw_mu", [2, 128, 14])
        self.hy_cols = self.din("hy_cols", [2, 128, 12, 4])
        self.hy_w1 = self.din("hy_w1", [2, 33, 64])
        self.hy_w2 = self.din("hy_w2", [2, 64, 64])
        self.hy_w3 = self.din("hy_w3", [2, 64, 2048])
        self.hy_fcols = self.din("hy_fcols", [2, 64, 6])
        self.hy_bias = self.din("hy_bias", [2, 128, 4, 2])
        self.hy_drow = self.din("hy_drow", [128, 512])
        self.hy_zpos = {}
        self.hy_ntcol = {}
        for LL in sorted(set((LP, LS))):
            self.hy_zpos[LL] = self.din("hy_zpos_%d" % LL, [33, LL])
            self.hy_ntcol[LL] = self.din("hy_ntcol_%d" % LL, [128, LL // 128])
        self.rw_pc = self.din("rw_pc", [2, 128, 4, 8])
        self.rw_lw = self.din("rw_lw", [2, 128, 2, 512])
        self.s5_bt = self.din("s5_bt", [2, 16, 128, 2, 2, 128])
        self.s5_ct = self.din("s5_ct", [2, 16, 2, 128, 2, 32])
        self.s5_pc = self.din("s5_pc", [2, 128, 4, 2])
        self.w_raw["s5_glu_w"] = self.din("s5_glu_w", [2, 512, 512])
        self.lru_wbd = self.din("lru_wbd", [2, 4, 128, 4, 128])
        self.wb = {}
        for l in range(DEPTH):
            for f in (1, 2):
                self.wb[("g", f, l)] = self.dscr("wg%d_%d" % (f, l), _wlayout(11, D, 256), BF16)
                self.wb[("u", f, l)] = self.dscr("wu%d_%d" % (f, l), _wlayout(11, D, 256), BF16)
                self.wb[("d", f, l)] = self.dscr("wd%d_%d" % (f, l), _wlayout(8, DFF, 128), BF16)
            cin = AB_IN if l % 2 == 0 else CD_IN
            self.wb[("in", l)] = self.dscr("win_%d" % l, _wlayout((cin + 127) // 128, D, 128), BF16)
            self.wb[("out", l)] = self.dscr("wout_%d" % l, _wlayout(8, D, 128), BF16)
        for jj in range(2):
            self.wb[("glu", jj)] = self.dscr("wglu_%d" % jj, _wlayout(4, 512, 128), BF16)
        self.YS5 = self.dscr("YS5", [2, 512, Lmax])
        self.RWTM = self.dscr("RWTM", [7, Lmax, 512])
        self.HYC = self.dscr("HYC", [3, 512, Lmax])
        self.HYZ = self.dscr("HYZ", [512, Lmax])
        self.HYK = self.dscr("HYK", [Lmax, 2048], BF16)
        self.HYF = self.dscr("HYF", [2, 2, Lmax // 128, 128, 2, 256])
        self.HYRN = self.dscr("HYRN", [128, 1024])
        self.dft = {}
        self.YTM = self.dscr("YTM", [2, Lmax, 512])
        self.RWG = self.dscr("RWG", [512, Lmax])
        self.RWB = self.dscr("RWB", [512, Lmax])
        self.XT = self.dscr("XT", [D, Lmax])
        self.PJ = self.dscr("PJ", [CD_IN, Lmax])
        self.YM = self.dscr("YM", [D, Lmax], BF16)

        with ExitStack() as st:
            self.P = Prog(nc, st)
            self.st = st
            self.ident = st.enter_context(nc.sbuf_tensor("ident", [128, 128], F32))
            self.ones_bf = st.enter_context(nc.sbuf_tensor("ones_bf", [128, 128], BF16))
            self.gcols = st.enter_context(nc.sbuf_tensor("gcols", [128, 3 * DEPTH + 1, 8], F32))
            self.psum = [st.enter_context(nc.psum_tensor("ps%d" % i, [128, 512], F32)) for i in range(8)]
            self.setup_consts()
            self.cast_weights()
            self.P.barrier()
            if self.enable.get("hyena"):
                for LL in sorted(set((LP, LS))):
                    self.gen_dft(LL)
            for (xin, yout, L, tag) in ((self.xp, self.yp, LP, "P"), (self.xs, self.ys, LS, "S")):
                self.trunk(xin, yout, L, tag)
                self.P.barrier()
            self.P.barrier(engs=["sp"])
        return nc

    def setup_consts(self):
        nc, P = self.nc, self.P
        with ExitStack() as s2:
            io = self.sbt(s2, "c_io", [128, 128], I32)
            iof = self.sbt(s2, "c_iof", [128, 128], F32)
            P.op("pool", lambda e: e.iota(io[:], pattern=[[1, 128]], base=0, channel_multiplier=-1), writes=["c_io"])
            P.op("dve", lambda e: e.tensor_copy(out=iof[:], in_=io[:]), reads=["c_io"], writes=["c_iof"])
            P.op("dve", lambda e: e.tensor_scalar(out=self.ident[:], in0=iof[:], scalar1=0.0, scalar2=None,
                                                  op0=ALU.is_equal), reads=["c_iof"], writes=["ident"])
            P.op("dve", lambda e: e.memset(self.ones_bf[:], 1.0), writes=["ones_bf"])
            P.op("sp", lambda e: e.dma_start(out=self.gcols[:], in_=self.norms[:, :, :]), writes=["gcols"], dma=True)
            P.barrier()

    def cast_weights(self):
        nc, P = self.nc, self.P
        with ExitStack() as s2:
            CW = 2048
            stg = Rot([self.sbt(s2, "cw_s%d" % i, [128, CW], F32) for i in range(3)], "cw_s")
            stb = Rot([self.sbt(s2, "cw_b%d" % i, [128, CW], BF16) for i in range(3)], "cw_b")
            self._cast_n = 0

            def cast(src, dst, R, C, cw):
                nco = dst.shape[0]
                for kc in range(R // 128):
                    c0 = 0
                    while c0 < C:
                        wc = min(CW, C - c0)
                        wc_pad = ((wc + cw - 1) // cw) * cw
                        a, ka = stg.next()
                        b, kb = stb.next()
                        P.op("sp", lambda e: e.dma_start(out=a[:, 0:wc], in_=src[kc * 128:(kc + 1) * 128, c0:c0 + wc]),
                             writes=[ka], dma=True)
                        if wc_pad != wc:
                            P.op("pool", lambda e: e.memset(b[:, wc:wc_pad], 0.0), writes=[kb])
                        eng = ("act", "dve", "pool")[self._cast_n % 3]
                        self._cast_n += 1
                        if eng == "act":
                            P.op("act", lambda e: e.copy(out=b[:, 0:wc], in_=a[:, 0:wc]), reads=[ka], writes=[kb])
                        else:
                            P.op(eng, lambda e: e.tensor_copy(out=b[:, 0:wc], in_=a[:, 0:wc]), reads=[ka], writes=[kb])
                        co0 = c0 // cw
                        nn = wc_pad // cw
                        dview = dst.rearrange("co p kc j -> p co kc j")[:, co0:co0 + nn, kc, :]
                        P.op("pool", lambda e: e.dma_start(out=dview, in_=b[:, 0:wc_pad].rearrange("p (co j) -> p co j", j=cw)),
                             reads=[kb], dma=True)
                        c0 += wc

            for l in range(self.depth):
                for f in (1, 2):
                    cast(self.w_raw["ffn%d_w_gate" % f][l], self.wb[("g", f, l)], D, DFF, 256)
                    cast(self.w_raw["ffn%d_w_up" % f][l], self.wb[("u", f, l)], D, DFF, 256)
                    cast(self.w_raw["ffn%d_w_down" % f][l], self.wb[("d", f, l)], DFF, D, 128)
                j = l // 2
                if l % 2 == 0:
                    cast(self.w_raw["s5_glu_w"][j], self.wb[("glu", j)], 512, 512, 128)
                    cast(self.w_raw["ab_w_in"][j], self.wb[("in", l)], D, AB_IN, 128)
                    cast(self.w_raw["ab_w_out"][j], self.wb[("out", l)], D, D, 128)
                else:
                    cast(self.w_raw["cd_w_in"][j], self.wb[("in", l)], D, CD_IN, 128)
                    cast(self.w_raw["cd_w_out"][j], self.wb[("out", l)], D, D, 128)

    def trunk(self, xin, yout, L, tag):
        for l in range(self.depth + 1):
            self.dense_pass(xin, yout, L, l)
            self.P.barrier()
            if l < self.depth:
                self.mixer_pass(L, l)
                self.P.barrier()

    def zero_rows(self, L, r0, r1):
        nc, P = self.nc, self.P
        with ExitStack() as s2:
            z = self.sbt(s2, "mz", [128, 2048], BF16)
            P.op("dve", lambda e: e.memset(z[:], 0.0), writes=["mz"])
            for rc in range(r0 // 128, r1 // 128):
                for t0 in range(0, L, 2048):
                    w = min(2048, L - t0)
                    P.op("pool", lambda e: e.dma_start(out=self.YM[rc * 128:(rc + 1) * 128, t0:t0 + w], in_=z[:, 0:w]),
                         reads=["mz"], dma=True)
            P.barrier()

    def mixer_pass(self, L, l):
        j = l // 2
        if l % 2 == 0:
            if self.enable.get("s5"):
                self.s5(L, j)
            else:
                self.zero_rows(L, 0, 512)
            self.P.barrier()
            if self.enable.get("rwkv"):
                self.rwkv(L, j)
            else:
                self.zero_rows(L, 512, 1024)
        else:
            if self.enable.get("lru"):
                self.lru(L, j)
            else:
                self.zero_rows(L, 0, 512)
            self.P.barrier()
            if self.enable.get("hyena"):
                self.hyena(L, j)
            else:
                self.zero_rows(L, 512, 1024)

    def lru(self, L, j):
        nc, P = self.nc, self.P
        TB = min(L, TBMAX)
        nb = L // TB
        with ExitStack() as s2:
            def sb(name, shape, dt=F32):
                return self.sbt(s2, name, shape, dt)
            cols = sb("l_cols", [128, 4, 11])
            cs = sb("l_cs", [128, 4, 2])
            wbd = sb("l_wbd", [128, 4, 128])
            HS = sb("l_HS", [128, L])
            xpad = sb("l_xpad", [128, TB + 3])
            xc = sb("l_xc", [128, TB])
            gr = sb("l_gr", [128, TB])
            gi = sb("l_gi", [128, TB])
            a_ = sb("l_a", [128, TB])
            om = sb("l_om", [128, TB])
            hR = Rot([sb("l_h%d" % i, [128, TB]) for i in range(2)], "l_h")
            gb = sb("l_gb", [128, TB])
            yo = sb("l_yo", [128, TB], BF16)
            psA = Rot(self.psum[0:4], "psA")
            P.op("sp", lambda e: e.dma_start(out=cols[:], in_=self.lru_cols[j]), writes=["l_cols"], dma=True)
            for d in range(2):
                P.op("act", lambda e: e.activation(out=cs[:, :, d], in_=cols[:, :, 5 + 3 * d], func=AF.Exp, scale=-1.0),
                     reads=["l_cols"], writes=["l_cs"])
                P.op("act", lambda e: e.activation(out=cs[:, :, d], in_=cs[:, :, d], func=AF.Ln, bias=1.0),
                     reads=["l_cs"], writes=["l_cs"])
            P.op("dve", lambda e: e.tensor_scalar(out=cs[:].rearrange("p a b -> p (a b)"), in0=cs[:].rearrange("p a b -> p (a b)"),
                                                  scalar1=-8.0, scalar2=None, op0=ALU.mult), reads=["l_cs"], writes=["l_cs"])
            for ct in range(4):
                P.op("sp", lambda e: e.dma_start(out=wbd[:], in_=self.lru_wbd[j, ct]), writes=["l_wbd"], dma=True)
                for d in range(2):
                    carry = 0.0
                    ckey = None
                    blocks = list(range(nb)) if d == 0 else list(range(nb - 1, -1, -1))
                    for bi in blocks:
                        t0 = bi * TB
                        lo = max(t0 - 2, 0)
                        hi = min(t0 + TB + 1, L)
                        if t0 == 0:
                            P.op("pool", lambda e: e.memset(xpad[:, 0:2], 0.0), writes=["l_xpad"])
                        if t0 + TB == L:
                            P.op("pool", lambda e: e.memset(xpad[:, TB + 2:TB + 3], 0.0), writes=["l_xpad"])
                        P.op("sp", lambda e: e.dma_start(out=xpad[:, lo - (t0 - 2):hi - (t0 - 2)],
                                                         in_=self.PJ[ct * 128:(ct + 1) * 128, lo:hi]), writes=["l_xpad"], dma=True)
                        P.op("dve", lambda e: e.tensor_scalar(out=xc[:], in0=xpad[:, 0:TB], scalar1=cols[:, ct, 0:1],
                                                              scalar2=cols[:, ct, 4:5], op0=ALU.mult, op1=ALU.add),
                             reads=["l_xpad", "l_cols"], writes=["l_xc"])
                        for q in range(1, 4):
                            P.op("dve", lambda e: e.scalar_tensor_tensor(out=xc[:], in0=xpad[:, q:q + TB], scalar=cols[:, ct, q:q + 1],
                                                                         op0=ALU.mult, in1=xc[:], op1=ALU.add),
                                 reads=["l_xpad", "l_xc"], writes=["l_xc"])
                        for sbk in range(TB // 512):
                            sl = slice(sbk * 512, (sbk + 1) * 512)
                            for which, dst, bcol, key in ((0, gr, 6 + 3 * d, "l_gr"), (1, gi, 7 + 3 * d, "l_gi")):
                                pa, kpa = psA.next()
                                P.op("pe", lambda e: e.matmul(pa[:], lhsT=wbd[:, 2 * d + which, :], rhs=xc[:, sl], start=True, stop=True),
                                     reads=["l_wbd", "l_xc"], writes=[kpa])
                                P.op("act", lambda e: e.activation(out=dst[:, sl], in_=pa[:], func=AF.Sigmoid,
                                                                   bias=cols[:, ct, bcol:bcol + 1]),
                                     reads=[kpa, "l_cols"], writes=[key])
                        P.op("act", lambda e: e.activation(out=a_[:], in_=gr[:], func=AF.Exp, scale=cs[:, ct, d:d + 1]),
                             reads=["l_gr", "l_cs"], writes=["l_a"])
                        P.op("pool", lambda e: e.tensor_tensor(out=om[:], in0=a_[:], in1=a_[:], op=ALU.mult), reads=["l_a"], writes=["l_om"])
                        P.op("pool", lambda e: e.tensor_scalar(out=om[:], in0=om[:], scalar1=-1.0, scalar2=1.0, op0=ALU.mult, op1=ALU.add),
                             reads=["l_om"], writes=["l_om"])
                        P.op("pool", lambda e: e.tensor_scalar(out=om[:], in0=om[:], scalar1=1e-30, scalar2=None, op0=ALU.max),
                             reads=["l_om"], writes=["l_om"])
                        P.op("act", lambda e: e.activation(out=om[:], in_=om[:], func=AF.Sqrt), reads=["l_om"], writes=["l_om"])
                        P.op("dve", lambda e: e.tensor_tensor(out=om[:], in0=om[:], in1=gi[:], op=ALU.mult), reads=["l_om", "l_gi"], writes=["l_om"])
                        P.op("dve", lambda e: e.tensor_tensor(out=om[:], in0=om[:], in1=xc[:], op=ALU.mult), reads=["l_om", "l_xc"], writes=["l_om"])
                        rk = ["l_a", "l_om"] + ([ckey] if ckey else [])
                        if d == 0:
                            P.op("dve", lambda e: e.tensor_tensor_scan(out=HS[:, t0:t0 + TB], data0=a_[:], data1=om[:], initial=carry,
                                                                       op0=ALU.mult, op1=ALU.add), reads=rk, writes=[("l_HS", bi)])
                            carry = HS[:, t0 + TB - 1:t0 + TB]
                            ckey = ("l_HS", bi)
                        else:
                            h, kh = hR.next()
                            P.op("dve", lambda e: e.tensor_tensor_scan(out=h[:, ::-1], data0=a_[:, ::-1], data1=om[:, ::-1], initial=carry,
                                                                       op0=ALU.mult, op1=ALU.add), reads=rk, writes=[kh])
                            carry = h[:, 0:1]
                            ckey = kh
                            P.op("sp", lambda e: e.dma_start(out=gb[:], in_=self.PJ[512 + ct * 128:512 + (ct + 1) * 128, t0:t0 + TB]),
                                 writes=["l_gb"], dma=True)
                            P.op("act", lambda e: e.activation(out=gb[:], in_=gb[:], func=AF.Gelu_apprx_tanh), reads=["l_gb"], writes=["l_gb"])
                            P.op("pool", lambda e: e.tensor_tensor(out=gr[:], in0=h[:], in1=HS[:, t0:t0 + TB], op=ALU.add),
                                 reads=[kh, ("l_HS", bi)], writes=["l_gr"])
                            P.op("pool", lambda e: e.tensor_tensor(out=yo[:], in0=gr[:], in1=gb[:], op=ALU.mult),
                                 reads=["l_gr", "l_gb"], writes=["l_yo"])
                            P.op("pool", lambda e: e.dma_start(out=self.YM[ct * 128:(ct + 1) * 128, t0:t0 + TB], in_=yo[:]),
                                 reads=["l_yo"], dma=True)

    def _range_reduce(self, t, tmp, key, tkey, n):
        P = self.P
        P.op("dve", lambda e: e.tensor_scalar(out=tmp, in0=t, scalar1=1.0 / TWO_PI, scalar2=MAGIC, op0=ALU.mult, op1=ALU.add),
             reads=[key], writes=[tkey])
        P.op("dve", lambda e: e.tensor_scalar(out=tmp, in0=tmp, scalar1=MAGIC, scalar2=-TWO_PI, op0=ALU.subtract, op1=ALU.mult),
             reads=[tkey], writes=[tkey])
        P.op("dve", lambda e: e.tensor_tensor(out=t, in0=t, in1=tmp, op=ALU.add), reads=[key, tkey], writes=[key])
        P.op("dve", lambda e: e.tensor_scalar(out=t, in0=t, scalar1=3.14159, scalar2=-3.14159, op0=ALU.min, op1=ALU.max),
             reads=[key], writes=[key])

    def s5(self, L, j):
        nc, P = self.nc, self.P
        TB = min(L, TBMAX)
        nb = L // TB
        YS = self.YS5
        with ExitStack() as s2:
            def sb(name, shape, dt=F32):
                return self.sbt(s2, name, shape, dt)
            NC = 32
            prm = sb("s_prm", [128, NC, 3])
            lr = sb("s_lr", [128, NC]); li = sb("s_li", [128, NC]); stp = sb("s_stp", [128, NC])
            mag = sb("s_mag", [128, NC]); ang = sb("s_ang", [128, NC]); angc = sb("s_angc", [128, NC]); tmpc = sb("s_tmpc", [128, NC])
            cth = sb("s_cth", [128, NC]); sth = sb("s_sth", [128, NC]); nsth = sb("s_nsth", [128, NC])
            are = sb("s_are", [128, NC]); aim = sb("s_aim", [128, NC]); den = sb("s_den", [128, NC])
            cre = sb("s_cre", [128, NC]); cim = sb("s_cim", [128, NC]); t1 = sb("s_t1", [128, NC]); t2 = sb("s_t2", [128, NC])
            ones = sb("s_ones", [128, TB])
            ucm = sb("s_ucm", [128, L])
            bt = sb("s_bt", [128, 2, 2, 128])
            ctl = sb("s_ct", [128, 2, 32])
            cpr = sb("s_cpr", [128, 2, 32])
            Ere = sb("s_Ere", [128, TB]); Eim = sb("s_Eim", [128, TB]); rt = sb("s_rt", [128, TB])
            xre = sb("s_xre", [128, TB]); xim = sb("s_xim", [128, TB])
            gre = sb("s_gre", [128, TB]); gim = sb("s_gim", [128, TB])
            ta = sb("s_ta", [128, TB]); tb = sb("s_tb", [128, TB])
            cw = sb("s_cw", [128, 2]); cw2 = sb("s_cw2", [128, 2]); ini = sb("s_ini", [128, 2]); tin = sb("s_tin", [128, 2])
            yev = Rot([sb("s_yev%d" % i, [32, 512]) for i in range(2)], "s_yev")
            psW = Rot(self.psum[0:4], "psW")
            psY = Rot(self.psum[4:6], "psY")
            P.op("sp", lambda e: e.dma_start(out=prm[:], in_=self.s5_prm[j]), writes=["s_prm"], dma=True)
            P.op("dve", lambda e: e.memset(ones[:], 1.0), writes=["s_ones"])
            P.op("dve", lambda e: e.tensor_scalar(out=lr[:], in0=prm[:, :, 0], scalar1=-1e-4, scalar2=None, op0=ALU.min), reads=["s_prm"], writes=["s_lr"])
            P.op("dve", lambda e: e.tensor_copy(out=li[:], in_=prm[:, :, 1]), reads=["s_prm"], writes=["s_li"])
            P.op("act", lambda e: e.activation(out=stp[:], in_=prm[:, :, 2], func=AF.Exp), reads=["s_prm"], writes=["s_stp"])
            P.op("dve", lambda e: e.tensor_tensor(out=mag[:], in0=lr[:], in1=stp[:], op=ALU.mult), reads=["s_lr", "s_stp"], writes=["s_mag"])
            P.op("act", lambda e: e.activation(out=mag[:], in_=mag[:], func=AF.Exp), reads=["s_mag"], writes=["s_mag"])
            P.op("dve", lambda e: e.tensor_tensor(out=ang[:], in0=li[:], in1=stp[:], op=ALU.mult), reads=["s_li", "s_stp"], writes=["s_ang"])
            P.op("dve", lambda e: e.tensor_scalar(out=angc[:], in0=ang[:], scalar1=0.5 * math.pi, scalar2=None, op0=ALU.add), reads=["s_ang"], writes=["s_angc"])
            self._range_reduce(ang[:], tmpc[:], "s_ang", "s_tmpc", NC)
            self._range_reduce(angc[:], tmpc[:], "s_angc", "s_tmpc", NC)
            P.op("act", lambda e: e.activation(out=sth[:], in_=ang[:], func=AF.Sin), reads=["s_ang"], writes=["s_sth"])
            P.op("act", lambda e: e.activation(out=cth[:], in_=angc[:], func=AF.Sin), reads=["s_angc"], writes=["s_cth"])
            P.op("dve", lambda e: e.tensor_scalar(out=nsth[:], in0=sth[:], scalar1=-1.0, scalar2=None, op0=ALU.mult), reads=["s_sth"], writes=["s_nsth"])
            P.op("dve", lambda e: e.tensor_tensor(out=are[:], in0=mag[:], in1=cth[:], op=ALU.mult), reads=["s_mag", "s_cth"], writes=["s_are"])
            P.op("dve", lambda e: e.tensor_tensor(out=aim[:], in0=mag[:], in1=sth[:], op=ALU.mult), reads=["s_mag", "s_sth"], writes=["s_aim"])
            P.op("dve", lambda e: e.tensor_tensor(out=den[:], in0=lr[:], in1=lr[:], op=ALU.mult), reads=["s_lr"], writes=["s_den"])
            P.op("dve", lambda e: e.tensor_tensor(out=t1[:], in0=li[:], in1=li[:], op=ALU.mult), reads=["s_li"], writes=["s_t1"])
            P.op("dve", lambda e: e.tensor_tensor(out=den[:], in0=den[:], in1=t1[:], op=ALU.add), reads=["s_den", "s_t1"], writes=["s_den"])
            P.op("dve", lambda e: e.reciprocal(out=den[:], in_=den[:]), reads=["s_den"], writes=["s_den"])
            P.op("dve", lambda e: e.tensor_scalar(out=are[:], in0=are[:], scalar1=-1.0, scalar2=None, op0=ALU.add), reads=["s_are"], writes=["s_are"])
            P.op("dve", lambda e: e.tensor_tensor(out=t1[:], in0=are[:], in1=lr[:], op=ALU.mult), reads=["s_are", "s_lr"], writes=["s_t1"])
            P.op("dve", lambda e: e.tensor_tensor(out=t2[:], in0=aim[:], in1=li[:], op=ALU.mult), reads=["s_aim", "s_li"], writes=["s_t2"])
            P.op("dve", lambda e: e.tensor_tensor(out=t1[:], in0=t1[:], in1=t2[:], op=ALU.add), reads=["s_t1", "s_t2"], writes=["s_t1"])
            P.op("dve", lambda e: e.tensor_tensor(out=cre[:], in0=t1[:], in1=den[:], op=ALU.mult), reads=["s_t1", "s_den"], writes=["s_cre"])
            P.op("dve", lambda e: e.tensor_tensor(out=t1[:], in0=aim[:], in1=lr[:], op=ALU.mult), reads=["s_aim", "s_lr"], writes=["s_t1"])
            P.op("dve", lambda e: e.tensor_tensor(out=t2[:], in0=are[:], in1=li[:], op=ALU.mult), reads=["s_are", "s_li"], writes=["s_t2"])
            P.op("dve", lambda e: e.tensor_tensor(out=t1[:], in0=t1[:], in1=t2[:], op=ALU.subtract), reads=["s_t1", "s_t2"], writes=["s_t1"])
            P.op("dve", lambda e: e.tensor_tensor(out=cim[:], in0=t1[:], in1=den[:], op=ALU.mult), reads=["s_t1", "s_den"], writes=["s_cim"])

            for ut in range(4):
                P.op("sp", lambda e: e.dma_start(out=ucm[:], in_=self.PJ[ut * 128:(ut + 1) * 128, 0:L]), writes=["s_ucm"], dma=True)
                for gp in range(4):
                    T = ut * 4 + gp
                    rows = slice(0, 128)
                    P.op("sp", lambda e: e.dma_start(out=bt[:], in_=self.s5_bt[j, T]), writes=["s_bt"], dma=True)
                    for d in range(2):
                        ci = T * 2 + d
                        cc = slice(ci, ci + 1)
                        P.op("sp", lambda e: e.dma_start(out=ctl[:], in_=self.s5_ct[j, T, d]), writes=["s_ct"], dma=True)
                        P.op("dve", lambda e: e.tensor_scalar(out=cpr[:, 0, :], in0=ctl[:, 0, :], scalar1=cre[:, cc], scalar2=None, op0=ALU.mult),
                             reads=["s_ct", "s_cre"], writes=["s_cpr"])
                        P.op("dve", lambda e: e.scalar_tensor_tensor(out=cpr[:, 0, :], in0=ctl[:, 1, :], scalar=cim[:, cc], op0=ALU.mult,
                                                                     in1=cpr[:, 0, :], op1=ALU.subtract),
                             reads=["s_ct", "s_cim", "s_cpr"], writes=["s_cpr"])
                        P.op("dve", lambda e: e.tensor_scalar(out=cpr[:, 0, :], in0=cpr[:, 0, :], scalar1=-1.0, scalar2=None, op0=ALU.mult),
                             reads=["s_cpr"], writes=["s_cpr"])
                        P.op("dve", lambda e: e.tensor_scalar(out=cpr[:, 1, :], in0=ctl[:, 0, :], scalar1=cim[:, cc], scalar2=None, op0=ALU.mult),
                             reads=["s_ct", "s_cim"], writes=["s_cpr1"])
                        P.op("dve", lambda e: e.scalar_tensor_tensor(out=cpr[:, 1, :], in0=ctl[:, 1, :], scalar=cre[:, cc], op0=ALU.mult,
                                                                     in1=cpr[:, 1, :], op1=ALU.add),
                             reads=["s_ct", "s_cre", "s_cpr1"], writes=["s_cpr1"])
                        P.op("dve", lambda e: e.tensor_scalar(out=cpr[:, 1, :], in0=cpr[:, 1, :], scalar1=-1.0, scalar2=None, op0=ALU.mult),
                             reads=["s_cpr1"], writes=["s_cpr1"])
                        P.op("dve", lambda e: e.tensor_scalar(out=rt[:], in0=ones[:], scalar1=mag[:, cc], scalar2=None, op0=ALU.mult),
                             reads=["s_ones", "s_mag"], writes=["s_rt"])
                        P.op("dve", lambda e: e.memset(Ere[:, 0:1], 1.0), writes=["s_E"])
                        P.op("dve", lambda e: e.memset(Eim[:, 0:1], 0.0), writes=["s_E"])
                        P.op("dve", lambda e: e.tensor_copy(out=cw[:, 0:1], in_=cth[:, cc]), reads=["s_cth"], writes=["s_cw"])
                        P.op("dve", lambda e: e.tensor_copy(out=cw[:, 1:2], in_=nsth[:, cc]), reads=["s_nsth"], writes=["s_cw"])
                        m = 1
                        while m < TB:
                            P.op("dve", lambda e: e.tensor_scalar(out=ta[:, 0:m], in0=Eim[:, 0:m], scalar1=cw[:, 1:2], scalar2=None, op0=ALU.mult),
                                 reads=["s_E", "s_cw"], writes=["s_ta"])
                            P.op("dve", lambda e: e.scalar_tensor_tensor(out=Ere[:, m:2 * m], in0=Ere[:, 0:m], scalar=cw[:, 0:1], op0=ALU.mult,
                                                                         in1=ta[:, 0:m], op1=ALU.subtract),
                                 reads=["s_E", "s_cw", "s_ta"], writes=["s_E"])
                            P.op("dve", lambda e: e.tensor_scalar(out=tb[:, 0:m], in0=Ere[:, 0:m], scalar1=cw[:, 1:2], scalar2=None, op0=ALU.mult),
                                 reads=["s_E", "s_cw"], writes=["s_tb"])
                            P.op("dve", lambda e: e.scalar_tensor_tensor(out=Eim[:, m:2 * m], in0=Eim[:, 0:m], scalar=cw[:, 0:1], op0=ALU.mult,
                                                                         in1=tb[:, 0:m], op1=ALU.add),
                                 reads=["s_E", "s_cw", "s_tb"], writes=["s_E"])
                            m *= 2
                            if m < TB:
                                P.op("dve", lambda e: e.tensor_tensor(out=cw2[:, 0:1], in0=cw[:, 1:2], in1=cw[:, 1:2], op=ALU.mult), reads=["s_cw"], writes=["s_cw2"])
                                P.op("dve", lambda e: e.tensor_tensor(out=cw2[:, 1:2], in0=cw[:, 0:1], in1=cw[:, 1:2], op=ALU.mult), reads=["s_cw"], writes=["s_cw2"])
                                P.op("dve", lambda e: e.scalar_tensor_tensor(out=cw[:, 0:1], in0=cw[:, 0:1], scalar=cw[:, 0:1], op0=ALU.mult,
                                                                             in1=cw2[:, 0:1], op1=ALU.subtract), reads=["s_cw", "s_cw2"], writes=["s_cw"])
                                P.op("dve", lambda e: e.tensor_scalar(out=cw[:, 1:2], in0=cw2[:, 1:2], scalar1=2.0, scalar2=None, op0=ALU.mult),
                                     reads=["s_cw2"], writes=["s_cw"])
                        rv = (lambda ap: ap[:, ::-1]) if d == 1 else (lambda ap: ap[:])
                        blocks = list(range(nb)) if d == 0 else list(range(nb - 1, -1, -1))
                        first_blk = True
                        for bi in blocks:
                            t0 = bi * TB
                            for sbk in range(TB // 512):
                                c0 = sbk * 512
                                sl = slice(c0, c0 + 512)
                                if d == 0:
                                    Er, Ei = Ere[:, sl], Eim[:, sl]
                                else:
                                    Er, Ei = Ere[:, TB - c0 - 512:TB - c0][:, ::-1], Eim[:, TB - c0 - 512:TB - c0][:, ::-1]
                                pr, kpr = psW.next()
                                pi_, kpi = psW.next()
                                P.op("pe", lambda e: e.matmul(pr[:], lhsT=bt[rows, d, 0, :], rhs=ucm[rows, t0 + c0:t0 + c0 + 512], start=True, stop=True),
                                     reads=["s_bt", "s_ucm"], writes=[kpr])
                                P.op("pe", lambda e: e.matmul(pi_[:], lhsT=bt[rows, d, 1, :], rhs=ucm[rows, t0 + c0:t0 + c0 + 512], start=True, stop=True),
                                     reads=["s_bt", "s_ucm"], writes=[kpi])
                                P.op("dve", lambda e: e.tensor_tensor(out=xre[:, sl], in0=pr[:], in1=Er, op=ALU.mult), reads=[kpr, "s_E"], writes=["s_xre"])
                                P.op("dve", lambda e: e.tensor_tensor(out=ta[:, sl], in0=pi_[:], in1=Ei, op=ALU.mult), reads=[kpi, "s_E"], writes=["s_ta"])
                                P.op("dve", lambda e: e.tensor_tensor(out=xre[:, sl], in0=xre[:, sl], in1=ta[:, sl], op=ALU.subtract),
                                     reads=["s_xre", "s_ta"], writes=["s_xre"])
                                P.op("dve", lambda e: e.tensor_tensor(out=xim[:, sl], in0=pr[:], in1=Ei, op=ALU.mult), reads=[kpr, "s_E"], writes=["s_xim"])
                                P.op("dve", lambda e: e.tensor_tensor(out=tb[:, sl], in0=pi_[:], in1=Er, op=ALU.mult), reads=[kpi, "s_E"], writes=["s_tb"])
                                P.op("dve", lambda e: e.tensor_tensor(out=xim[:, sl], in0=xim[:, sl], in1=tb[:, sl], op=ALU.add),
                                     reads=["s_xim", "s_tb"], writes=["s_xim"])
                            if first_blk:
                                i_re, i_im = 0.0, 0.0
                            else:
                                P.op("dve", lambda e: e.tensor_scalar(out=ini[:, 0:1], in0=tin[:, 1:2], scalar1=sth[:, cc], scalar2=None, op0=ALU.mult),
                                     reads=["s_tin", "s_sth"], writes=["s_ini"])
                                P.op("dve", lambda e: e.scalar_tensor_tensor(out=ini[:, 0:1], in0=tin[:, 0:1], scalar=cth[:, cc], op0=ALU.mult,
                                                                             in1=ini[:, 0:1], op1=ALU.subtract),
                                     reads=["s_tin", "s_cth", "s_ini"], writes=["s_ini"])
                                P.op("dve", lambda e: e.tensor_scalar(out=ini[:, 1:2], in0=tin[:, 0:1], scalar1=sth[:, cc], scalar2=None, op0=ALU.mult),
                                     reads=["s_tin", "s_sth"], writes=["s_ini"])
                                P.op("dve", lambda e: e.scalar_tensor_tensor(out=ini[:, 1:2], in0=tin[:, 1:2], scalar=cth[:, cc], op0=ALU.mult,
                                                                             in1=ini[:, 1:2], op1=ALU.add),
                                     reads=["s_tin", "s_cth", "s_ini"], writes=["s_ini"])
                                i_re, i_im = ini[:, 0:1], ini[:, 1:2]
                            first_blk = False
                            P.op("dve", lambda e: e.tensor_tensor_scan(out=rv(gre), data0=rv(rt), data1=rv(xre), initial=i_re, op0=ALU.mult, op1=ALU.add),
                                 reads=["s_rt", "s_xre", "s_ini"], writes=["s_gre"])
                            P.op("dve", lambda e: e.tensor_tensor_scan(out=rv(gim), data0=rv(rt), data1=rv(xim), initial=i_im, op0=ALU.mult, op1=ALU.add),
                                 reads=["s_rt", "s_xim", "s_ini"], writes=["s_gim"])
                            Erf = Ere[:, ::-1] if d == 1 else Ere[:]
                            Eif = Eim[:, ::-1] if d == 1 else Eim[:]
                            P.op("dve", lambda e: e.tensor_tensor(out=xre[:], in0=gre[:], in1=Erf, op=ALU.mult), reads=["s_gre", "s_E"], writes=["s_xre"])
                            P.op("dve", lambda e: e.tensor_tensor(out=ta[:], in0=gim[:], in1=Eif, op=ALU.mult), reads=["s_gim", "s_E"], writes=["s_ta"])
                            P.op("dve", lambda e: e.tensor_tensor(out=xre[:], in0=xre[:], in1=ta[:], op=ALU.add), reads=["s_xre", "s_ta"], writes=["s_xre"])
                            P.op("dve", lambda e: e.tensor_tensor(out=xim[:], in0=gim[:], in1=Erf, op=ALU.mult), reads=["s_gim", "s_E"], writes=["s_xim"])
                            P.op("dve", lambda e: e.tensor_tensor(out=tb[:], in0=gre[:], in1=Eif, op=ALU.mult), reads=["s_gre", "s_E"], writes=["s_tb"])
                            P.op("dve", lambda e: e.tensor_tensor(out=xim[:], in0=xim[:], in1=tb[:], op=ALU.subtract), reads=["s_xim", "s_tb"], writes=["s_xim"])
                            edge = 0 if d == 1 else TB - 1
                            P.op("dve", lambda e: e.tensor_copy(out=tin[:, 0:1], in_=xre[:, edge:edge + 1]), reads=["s_xre"], writes=["s_tin"])
                            P.op("dve", lambda e: e.tensor_copy(out=tin[:, 1:2], in_=xim[:, edge:edge + 1]), reads=["s_xim"], writes=["s_tin"])
                            for sbk in range(TB // 512):
                                sl = slice(sbk * 512, (sbk + 1) * 512)
                                py, kpy = psY.next()
                                P.op("pe", lambda e: e.matmul(py[0:32, :], lhsT=cpr[:, 0, :], rhs=xre[:, sl], start=True, stop=False),
                                     reads=["s_cpr", "s_cpr1", "s_xre"], writes=[kpy])
                                P.op("pe", lambda e: e.matmul(py[0:32, :], lhsT=cpr[:, 1, :], rhs=xim[:, sl], start=False, stop=True),
                                     reads=["s_cpr", "s_cpr1", "s_xim"], writes=[kpy])
                                yv, kyv = yev.next()
                                P.op("act", lambda e: e.copy(out=yv[:], in_=py[0:32, :]), reads=[kpy], writes=[kyv])
                                P.op("pool", lambda e: e.dma_start(out=YS[d, 32 * T:32 * T + 32, t0 + sbk * 512:t0 + (sbk + 1) * 512], in_=yv[:]),
                                     reads=[kyv], dma=True)
            P.barrier()
        with ExitStack() as s2:
            def sb(name, shape, dt=F32):
                return self.sbt(s2, name, shape, dt)
            pc = sb("s_pc", [128, 4, 2])
            gw = sb("s_gw", [128, 4, 4, 128], BF16)
            yf = sb("s_yf", [128, 4, 512]); yb = sb("s_yb", [128, 4, 512]); uu = sb("s_uu", [128, 4, 512])
            yg = sb("s_yg", [128, 4, 512], BF16)
            sgm = Rot([sb("s_sg%d" % i, [128, 512]) for i in range(2)], "s_sg")
            yo = sb("s_yo", [128, 4, 512], BF16)
            psA = Rot(self.psum[0:4], "psA")
            P.op("sp", lambda e: e.dma_start(out=pc[:], in_=self.s5_pc[j]), writes=["s_pc"], dma=True)
            P.op("sp", lambda e: e.dma_start(out=gw[:], in_=self.wb[("glu", j)].rearrange("jc p ic w -> p jc ic w")), writes=["s_gw"], dma=True)
            for t0 in range(0, L, 512):
                rowv = lambda ap: ap.rearrange("(c p) t -> p c t", p=128)
                P.op("sp", lambda e: e.dma_start(out=yf[:], in_=rowv(YS[0, :, t0:t0 + 512])), writes=["s_yf"], dma=True)
                P.op("sp", lambda e: e.dma_start(out=yb[:], in_=rowv(YS[1, :, t0:t0 + 512])), writes=["s_yb"], dma=True)
                P.op("sp", lambda e: e.dma_start(out=uu[:], in_=rowv(self.PJ[0:512, t0:t0 + 512])), writes=["s_uu"], dma=True)
                P.op("pool", lambda e: e.tensor_tensor(out=yf[:], in0=yf[:], in1=yb[:], op=ALU.add), reads=["s_yf", "s_yb"], writes=["s_yf"])
                for c in range(4):
                    P.op("dve", lambda e: e.scalar_tensor_tensor(out=yf[:, c, :], in0=uu[:, c, :], scalar=pc[:, c, 0:1], op0=ALU.mult,
                                                                 in1=yf[:, c, :], op1=ALU.add), reads=["s_uu", "s_yf", "s_pc"], writes=["s_yf"])
                P.op("act", lambda e: e.activation(out=yg[:].rearrange("p a b -> p (a b)"), in_=yf[:].rearrange("p a b -> p (a b)"),
                                                   func=AF.Gelu_apprx_tanh), reads=["s_yf"], writes=["s_yg"])
                for jc in range(4):
                    pa, kpa = psA.next()
                    for ic in range(4):
                        P.op("pe", lambda e: e.matmul(pa[:], lhsT=gw[:, jc, ic, :], rhs=yg[:, ic, :], start=(ic == 0), stop=(ic == 3)),
                             reads=["s_gw", "s_yg"], writes=[kpa])
                    sg_, ksg = sgm.next()
                    P.op("act", lambda e: e.activation(out=sg_[:], in_=pa[:], func=AF.Sigmoid, bias=pc[:, jc, 1:2]), reads=[kpa, "s_pc"], writes=[ksg])
                    P.op("dve", lambda e: e.tensor_tensor(out=yo[:, jc, :], in0=sg_[:], in1=yg[:, jc, :], op=ALU.mult), reads=[ksg, "s_yg"], writes=["s_yo"])
                P.op("pool", lambda e: e.dma_start(out=self.YM[0:512, t0:t0 + 512].rearrange("(c p) t -> p c t", p=128), in_=yo[:]),
                     reads=["s_yo"], dma=True)

    def rwkv(self, L, j):
        nc, P = self.nc, self.P
        TM = self.RWTM
        YT = self.YTM
        GC, BC = self.RWG, self.RWB
        with ExitStack() as s2:
            def sb(name, shape, dt=F32):
                return self.sbt(s2, name, shape, dt)
            mu = sb("r_mu", [128, 14]); omu = sb("r_omu", [128, 14]); hmu = sb("r_hmu", [128, 14])
            pc = sb("r_pc", [128, 4, 8]); omka = sb("r_omka", [128, 4]); nw0 = sb("r_nw0", [128, 4, 2])
            lw = sb("r_lw", [128, 2, 512])
            bones = sb("r_bones", [128, 128])
            ppad = Rot([sb("r_ppad%d" % i, [128, 514]) for i in range(3)], "r_ppad")
            nsum = Rot([sb("r_nsum%d" % i, [128, 512]) for i in range(2)], "r_nsum")
            rr = sb("r_r", [128, 4, 512]); kk_ = sb("r_k", [128, 4, 512]); vv = sb("r_v", [128, 4, 512])
            lo = sb("r_lo", [128, 512]); xg = sb("r_xg", [128, 512]); tw = sb("r_tw", [128, 512]); sgx = sb("r_sgx", [128, 512])
            aa = sb("r_a", [128, 4, 512]); gg = sb("r_g", [128, 4, 512]); wf = sb("r_wf", [128, 4, 512]); wbk = sb("r_wb", [128, 4, 512])
            An = sb("r_An", [128, 4, 512]); Bn = sb("r_Bn", [128, 4, 512]); k2 = sb("r_k2", [128, 4, 512]); bon = sb("r_bon", [128, 4, 512])
            t1 = sb("r_t1", [128, 512]); t2 = sb("r_t2", [128, 512]); t3 = sb("r_t3", [128, 512])
            tok = Rot([sb("r_tok%d" % i, [128, 512]) for i in range(3)], "r_tok")
            psA = Rot(self.psum[0:6], "psA")
            psT = Rot(self.psum[6:8], "psT")
            P.op("sp", lambda e: e.dma_start(out=mu[:], in_=self.rw_mu[j]), writes=["r_mu"], dma=True)
            P.op("sp", lambda e: e.dma_start(out=pc[:], in_=self.rw_pc[j]), writes=["r_pc"], dma=True)
            P.op("sp", lambda e: e.dma_start(out=lw[:], in_=self.rw_lw[j]), writes=["r_lw"], dma=True)
            P.op("dve", lambda e: e.tensor_scalar(out=omu[:], in0=mu[:], scalar1=-1.0, scalar2=1.0, op0=ALU.mult, op1=ALU.add), reads=["r_mu"], writes=["r_omu"])
            P.op("dve", lambda e: e.tensor_scalar(out=hmu[:], in0=mu[:], scalar1=0.5, scalar2=None, op0=ALU.mult), reads=["r_mu"], writes=["r_hmu"])
            P.op("dve", lambda e: e.tensor_scalar(out=omka[:], in0=pc[:, :, 4], scalar1=-1.0, scalar2=1.0, op0=ALU.mult, op1=ALU.add), reads=["r_pc"], writes=["r_omka"])
            P.op("dve", lambda e: e.tensor_scalar(out=nw0[:], in0=pc[:, :, 0:2], scalar1=-1.0, scalar2=None, op0=ALU.mult), reads=["r_pc"], writes=["r_nw0"])
            P.op("dve", lambda e: e.memset(bones[:], 0.0), writes=["r_bones"])
            P.op("dve", lambda e: e.memset(bones[0:64, 0:64], 1.0), writes=["r_bones"])
            P.op("dve", lambda e: e.memset(bones[64:128, 64:128], 1.0), writes=["r_bones"])
            row_tiles = [(512 + 128 * i, 128) for i in range(12)] + [(512 + 1536, 96), (512 + 1632, 64)]
            dests = [(rr, i) for i in range(4)] + [(kk_, i) for i in range(4)] + [(vv, i) for i in range(4)] + [(lo, None), (xg, None)]
            for t0 in range(0, L, 512):
                lo_t = max(t0 - 1, 0)
                hi_t = min(t0 + 513, L)
                for ti, ((r0, nr), (dst, di)) in enumerate(zip(row_tiles, dests)):
                    pp, kp = ppad.next()
                    if t0 == 0:
                        P.op("pool", lambda e: e.memset(pp[:, 0:1], 0.0), writes=[kp])
                    if t0 + 512 == L:
                        P.op("pool", lambda e: e.memset(pp[:, 513:514], 0.0), writes=[kp])
                    P.op("sp", lambda e: e.dma_start(out=pp[0:nr, lo_t - (t0 - 1):hi_t - (t0 - 1)], in_=self.PJ[r0:r0 + nr, lo_t:hi_t]),
                         writes=[kp], dma=True)
                    ns, kn = nsum.next()
                    P.op("pool", lambda e: e.tensor_tensor(out=ns[0:nr, :], in0=pp[0:nr, 0:512], in1=pp[0:nr, 2:514], op=ALU.add), reads=[kp], writes=[kn])
                    P.op("pool", lambda e: e.tensor_scalar(out=ns[0:nr, :], in0=ns[0:nr, :], scalar1=hmu[0:nr, ti:ti + 1], scalar2=None, op0=ALU.mult),
                         reads=[kn, "r_hmu"], writes=[kn])
                    dap = dst[0:nr, :] if di is None else dst[0:nr, di, :]
                    dkey = "r_dst%d" % ti
                    P.op("dve", lambda e: e.scalar_tensor_tensor(out=dap, in0=pp[0:nr, 1:513], scalar=omu[0:nr, ti:ti + 1], op0=ALU.mult,
                                                                 in1=ns[0:nr, :], op1=ALU.add), reads=[kp, kn, "r_omu"], writes=[dkey])
                allk = ["r_dst%d" % i for i in range(14)]
                P.op("act", lambda e: e.activation(out=tw[0:64, :], in_=lo[0:64, :], func=AF.Tanh), reads=allk, writes=["r_tw"])
                P.op("act", lambda e: e.activation(out=sgx[0:64, :], in_=xg[0:64, :], func=AF.Sigmoid), reads=allk, writes=["r_sgx"])
                for ct in range(4):
                    cs_ = slice(ct * 128, (ct + 1) * 128)
                    pa, kpa = psA.next()
                    P.op("pe", lambda e: e.matmul(pa[:], lhsT=lw[64:96, 0, cs_], rhs=lo[64:96, :], start=True, stop=True), reads=["r_lw"] + allk, writes=[kpa])
                    P.op("act", lambda e: e.activation(out=aa[:, ct, :], in_=pa[:], func=AF.Sigmoid, bias=pc[:, ct, 2:3]), reads=[kpa, "r_pc"], writes=[("r_a", ct)])
                    pa, kpa = psA.next()
                    P.op("pe", lambda e: e.matmul(pa[:], lhsT=lw[0:64, 1, cs_], rhs=sgx[0:64, :], start=True, stop=True), reads=["r_lw", "r_sgx"], writes=[kpa])
                    P.op("act", lambda e: e.copy(out=gg[:, ct, :], in_=pa[:]), reads=[kpa], writes=[("r_g", ct)])
                    for d, wdst in ((0, wf), (1, wbk)):
                        pa, kpa = psA.next()
                        P.op("pe", lambda e: e.matmul(pa[:], lhsT=lw[32 * d:32 * d + 32, 0, cs_], rhs=tw[32 * d:32 * d + 32, :], start=True, stop=True),
                             reads=["r_lw", "r_tw"], writes=[kpa])
                        wk = ("r_w", d, ct)
                        P.op("act", lambda e: e.activation(out=wdst[:, ct, :], in_=pa[:], func=AF.Exp, scale=-1.0, bias=nw0[:, ct, d:d + 1]), reads=[kpa, "r_nw0"], writes=[wk])
                        P.op("act", lambda e: e.activation(out=wdst[:, ct, :], in_=wdst[:, ct, :], func=AF.Ln, bias=1.0), reads=[wk], writes=[wk])
                        P.op("act", lambda e: e.activation(out=wdst[:, ct, :], in_=wdst[:, ct, :], func=AF.Exp, scale=-1.0, bias=-0.5), reads=[wk], writes=[wk])
                        P.op("act", lambda e: e.activation(out=wdst[:, ct, :], in_=wdst[:, ct, :], func=AF.Exp, scale=-1.0), reads=[wk], writes=[wk])
                    P.op("dve", lambda e: e.tensor_scalar(out=t1[:], in0=kk_[:, ct, :], scalar1=pc[:, ct, 3:4], scalar2=None, op0=ALU.mult), reads=allk + ["r_pc"], writes=["r_t1"])
                    P.op("pool", lambda e: e.tensor_tensor(out=t2[:], in0=t1[:], in1=t1[:], op=ALU.mult), reads=["r_t1"], writes=["r_t2"])
                    pa, kpa = psA.next()
                    P.op("pe", lambda e: e.matmul(pa[:], lhsT=bones[:], rhs=t2[:], start=True, stop=True), reads=["r_bones", "r_t2"], writes=[kpa])
                    P.op("act", lambda e: e.activation(out=t3[:], in_=pa[:], func=AF.Sqrt), reads=[kpa], writes=["r_t3"])
                    P.op("dve", lambda e: e.tensor_scalar(out=t3[:], in0=t3[:], scalar1=1e-12, scalar2=None, op0=ALU.max), reads=["r_t3"], writes=["r_t3"])
                    P.op("dve", lambda e: e.reciprocal(out=t3[:], in_=t3[:]), reads=["r_t3"], writes=["r_t3"])
                    P.op("dve", lambda e: e.tensor_tensor(out=t1[:], in0=t1[:], in1=t3[:], op=ALU.mult), reads=["r_t1", "r_t3"], writes=["r_t1"])
                    P.op("pool", lambda e: e.tensor_scalar(out=An[:, ct, :], in0=t1[:], scalar1=-1.0, scalar2=None, op0=ALU.mult), reads=["r_t1"], writes=[("r_An", ct)])
                    P.op("dve", lambda e: e.tensor_tensor(out=Bn[:, ct, :], in0=t1[:], in1=aa[:, ct, :], op=ALU.mult), reads=["r_t1", ("r_a", ct)], writes=[("r_Bn", ct)])
                    P.op("dve", lambda e: e.tensor_scalar(out=t2[:], in0=aa[:, ct, :], scalar1=pc[:, ct, 4:5], scalar2=omka[:, ct:ct + 1], op0=ALU.mult, op1=ALU.add),
                         reads=[("r_a", ct), "r_pc", "r_omka"], writes=["r_t2"])
                    P.op("dve", lambda e: e.tensor_tensor(out=k2[:, ct, :], in0=kk_[:, ct, :], in1=t2[:], op=ALU.mult), reads=allk + ["r_t2"], writes=[("r_k2", ct)])
                    P.op("pool", lambda e: e.tensor_tensor(out=t3[:], in0=rr[:, ct, :], in1=k2[:, ct, :], op=ALU.mult), reads=allk + [("r_k2", ct)], writes=["r_t3"])
                    P.op("dve", lambda e: e.tensor_scalar(out=t3[:], in0=t3[:], scalar1=pc[:, ct, 5:6], scalar2=None, op0=ALU.mult), reads=["r_t3", "r_pc"], writes=["r_t3"])
                    pa, kpa = psA.next()
                    P.op("pe", lambda e: e.matmul(pa[:], lhsT=bones[:], rhs=t3[:], start=True, stop=True), reads=["r_bones", "r_t3"], writes=[kpa])
                    P.op("dve", lambda e: e.tensor_tensor(out=bon[:, ct, :], in0=pa[:], in1=vv[:, ct, :], op=ALU.mult), reads=[kpa] + allk, writes=[("r_bon", ct)])
                cm = lambda ap: ap.rearrange("(c p) t -> p c t", p=128)
                P.op("pool", lambda e: e.dma_start(out=cm(GC[:, t0:t0 + 512]), in_=gg[:]), reads=[("r_g", c) for c in range(4)], dma=True)
                P.op("pool", lambda e: e.dma_start(out=cm(BC[:, t0:t0 + 512]), in_=bon[:]), reads=[("r_bon", c) for c in range(4)], dma=True)
                srcs = [(rr, allk), (k2, [("r_k2", c) for c in range(4)]), (vv, allk), (An, [("r_An", c) for c in range(4)]),
                        (Bn, [("r_Bn", c) for c in range(4)]), (wf, [("r_w", 0, c) for c in range(4)]), (wbk, [("r_w", 1, c) for c in range(4)])]
                for ai, (src, skeys) in enumerate(srcs):
                    for sbk in range(4):
                        pt, kpt = psT.next()
                        for ct in range(4):
                            P.op("pe", lambda e: e.matmul(pt[:, ct * 128:(ct + 1) * 128], lhsT=src[:, ct, sbk * 128:(sbk + 1) * 128], rhs=self.ident[:],
                                                          start=True, stop=True), reads=skeys + ["ident"], writes=[kpt])
                        tk, ktk = tok.next()
                        P.op("act", lambda e: e.copy(out=tk[:], in_=pt[:]), reads=[kpt], writes=[ktk])
                        P.op("pool", lambda e: e.dma_start(out=TM[ai, t0 + sbk * 128:t0 + (sbk + 1) * 128, :], in_=tk[:]), reads=[ktk], dma=True)
            P.barrier()
        NS = 16
        with ExitStack() as s2:
            def sb(name, shape, dt=F32):
                return self.sbt(s2, name, shape, dt)
            rep = sb("q_rep", [16, 128]); iot = sb("q_iot", [16, 128], I32); iof = sb("q_iof", [16, 128]); iog = sb("q_iog", [16, 128])
            S = sb("q_S", [128, 8, 64]); tmp2 = sb("q_tmp2", [128, 8, 64]); U = sb("q_U", [128, 8])
            tmpR = Rot([sb("q_tmp%d" % i, [128, 8, 64]) for i in range(4)], "q_tmp")
            pend = None
            cmpR = Rot([sb("q_cmp%d" % i, [16, 5, NS * 64]) for i in range(2)], "q_cmp")
            opR = Rot([sb("q_op%d" % i, [128, 5, NS, 64]) for i in range(2)], "q_op")
            vR = Rot([sb("q_v%d" % i, [128, NS, 8]) for i in range(2)], "q_v")
            yR = Rot([sb("q_y%d" % i, [128, NS, 8]) for i in range(2)], "q_y")
            vkR = Rot([sb("q_vk%d" % i, [128, 8, 64]) for i in range(4)], "q_vk")
            psR = Rot(self.psum[0:8], "psR")
            P.op("pool", lambda e: e.iota(iot[:], pattern=[[1, 128]], base=0, channel_multiplier=-8), writes=["q_iot"])
            P.op("dve", lambda e: e.tensor_copy(out=iof[:], in_=iot[:]), reads=["q_iot"], writes=["q_iof"])
            P.op("dve", lambda e: e.tensor_scalar(out=iog[:], in0=iof[:], scalar1=0.0, scalar2=None, op0=ALU.is_ge), reads=["q_iof"], writes=["q_iog"])
            P.op("dve", lambda e: e.tensor_scalar(out=iof[:], in0=iof[:], scalar1=8.0, scalar2=None, op0=ALU.is_lt), reads=["q_iof"], writes=["q_iof"])
            P.op("dve", lambda e: e.tensor_tensor(out=rep[:], in0=iof[:], in1=iog[:], op=ALU.mult), reads=["q_iof", "q_iog"], writes=["q_rep"])
            P.op("dve", lambda e: e.memset(S[:].rearrange("p a b -> p (a b)"), 0.0), writes=["q_S"])
            arr_of = [3, None, 4, 1, 0]
            for i0 in range(0, L, NS):
                cmp_, kc_ = cmpR.next()
                opt, ko = opR.next()
                vt, kv = vR.next()
                yt_, ky = yR.next()
                for d in range(2):
                    for oi in range(5):
                        ai = arr_of[oi] if oi != 1 else (5 + d)
                        if d == 0:
                            src = TM[ai, i0:i0 + NS, :]
                        else:
                            src = TM[ai, L - i0 - NS:L - i0, :][::-1, :]
                        P.op("sp", lambda e: e.dma_start(out=cmp_[8 * d:8 * d + 8, oi, :].rearrange("h (s k) -> h s k", k=64),
                                                         in_=src.rearrange("s (h k) -> h s k", k=64)), writes=[(kc_, d, oi)], dma=True)
                    if d == 0:
                        vsrc = TM[2, i0:i0 + NS, :]
                    else:
                        vsrc = TM[2, L - i0 - NS:L - i0, :][::-1, :]
                    P.op("act", lambda e: e.dma_start(out=vt[64 * d:64 * d + 64, :, :], in_=vsrc.rearrange("s (hv vi) -> hv s vi", vi=8)),
                         writes=[(kv, d)], dma=True)
                for oi in range(5):
                    for q4 in range(NS * 64 // 512):
                        pr, kpr = psR.next()
                        P.op("pe", lambda e: e.matmul(pr[:], lhsT=rep[:], rhs=cmp_[:, oi, q4 * 512:(q4 + 1) * 512], start=True, stop=True),
                             reads=["q_rep", (kc_, 0, oi), (kc_, 1, oi)], writes=[kpr])
                        P.op("act", lambda e: e.copy(out=opt[:, oi, q4 * 8:(q4 + 1) * 8, :].rearrange("p s k -> p (s k)"), in_=pr[:]),
                             reads=[kpr], writes=[(ko, oi)])
                opk = [(ko, oi) for oi in range(5)]
                for s_ in range(NS):
                    bc = lambda oi: opt[:, oi, s_, :].unsqueeze(1).broadcast_to([128, 8, 64])
                    vk, kvk = vkR.next()
                    P.op("pool", lambda e: e.tensor_tensor(out=vk[:], in0=vt[:, s_, :].unsqueeze(2).broadcast_to([128, 8, 64]), in1=bc(3), op=ALU.mult),
                         reads=[(kv, 0), (kv, 1)] + opk, writes=[kvk])
                    tA, ktA = tmpR.next()
                    P.op("dve", lambda e: e.tensor_tensor(out=tA[:], in0=S[:], in1=bc(0), op=ALU.mult), reads=["q_S"] + opk, writes=[ktA])
                    if pend is not None:
                        pt_, pkt, pyt, pky, ps_ = pend
                        P.op("dve", lambda e: e.tensor_reduce(out=pyt[:, ps_, :], in_=pt_[:], axis=AX.X, op=ALU.add), reads=[pkt], writes=[pky])
                        pend = None
                    P.op("dve", lambda e: e.tensor_tensor(out=S[:], in0=S[:], in1=bc(1), op=ALU.mult), reads=["q_S"] + opk, writes=["q_S"])
                    P.op("dve", lambda e: e.tensor_reduce(out=U[:], in_=tA[:], axis=AX.X, op=ALU.add), reads=[ktA], writes=["q_U"])
                    P.op("dve", lambda e: e.tensor_tensor(out=S[:], in0=S[:], in1=vk[:], op=ALU.add), reads=["q_S", kvk], writes=["q_S"])
                    P.op("dve", lambda e: e.tensor_tensor(out=tmp2[:], in0=U[:].unsqueeze(2).broadcast_to([128, 8, 64]), in1=bc(2), op=ALU.mult),
                         reads=["q_U"] + opk, writes=["q_tmp2"])
                    P.op("dve", lambda e: e.tensor_tensor(out=S[:], in0=S[:], in1=tmp2[:], op=ALU.add), reads=["q_S", "q_tmp2"], writes=["q_S"])
                    tR, ktR = tmpR.next()
                    P.op("dve", lambda e: e.tensor_tensor(out=tR[:], in0=S[:], in1=bc(4), op=ALU.mult), reads=["q_S"] + opk, writes=[ktR])
                    pend = (tR, ktR, yt_, ky, s_)
                if pend is not None:
                    pt_, pkt, pyt, pky, ps_ = pend
                    P.op("dve", lambda e: e.tensor_reduce(out=pyt[:, ps_, :], in_=pt_[:], axis=AX.X, op=ALU.add), reads=[pkt], writes=[pky])
                    pend = None
                for d in range(2):
                    if d == 0:
                        dst = YT[0, i0:i0 + NS, :]
                    else:
                        dst = YT[1, L - i0 - NS:L - i0, :][::-1, :]
                    P.op("pool", lambda e: e.dma_start(out=dst.rearrange("s (hv vi) -> hv s vi", vi=8), in_=yt_[64 * d:64 * d + 64, :, :]),
                         reads=[ky], dma=True)
            P.barrier()
        with ExitStack() as s2:
            def sb(name, shape, dt=F32):
                return self.sbt(s2, name, shape, dt)
            pc = sb("z_pc", [128, 4, 8]); bones = sb("z_bones", [128, 128])
            ya = Rot([sb("z_ya%d" % i, [128, 512]) for i in range(2)], "z_ya")
            yb_ = Rot([sb("z_yb%d" % i, [128, 512]) for i in range(2)], "z_yb")
            ycm = sb("z_ycm", [128, 4, 512]); gt = sb("z_g", [128, 4, 512]); bt_ = sb("z_b", [128, 4, 512])
            mean = sb("z_mean", [128, 512]); cen = sb("z_cen", [128, 512]); sq = sb("z_sq", [128, 512]); rstd = sb("z_rstd", [128, 512])
            yo = sb("z_yo", [128, 4, 512], BF16)
            psT = Rot(self.psum[0:4], "psT")
            psA = Rot(self.psum[4:8], "psA")
            P.op("sp", lambda e: e.dma_start(out=pc[:], in_=self.rw_pc[j]), writes=["z_pc"], dma=True)
            P.op("dve", lambda e: e.memset(bones[:], 0.0), writes=["z_bones"])
            P.op("dve", lambda e: e.memset(bones[0:64, 0:64], 1.0 / 64.0), writes=["z_bones"])
            P.op("dve", lambda e: e.memset(bones[64:128, 64:128], 1.0 / 64.0), writes=["z_bones"])
            cm = lambda ap: ap.rearrange("(c p) t -> p c t", p=128)
            for t0 in range(0, L, 512):
                P.op("sp", lambda e: e.dma_start(out=gt[:], in_=cm(GC[:, t0:t0 + 512])), writes=["z_g"], dma=True)
                P.op("sp", lambda e: e.dma_start(out=bt_[:], in_=cm(BC[:, t0:t0 + 512])), writes=["z_b"], dma=True)
                for sbk in range(4):
                    a_, ka = ya.next()
                    b_, kb = yb_.next()
                    P.op("sp", lambda e: e.dma_start(out=a_[:], in_=YT[0, t0 + sbk * 128:t0 + (sbk + 1) * 128, :]), writes=[ka], dma=True)
                    P.op("sp", lambda e: e.dma_start(out=b_[:], in_=YT[1, t0 + sbk * 128:t0 + (sbk + 1) * 128, :]), writes=[kb], dma=True)
                    P.op("pool", lambda e: e.tensor_tensor(out=a_[:], in0=a_[:], in1=b_[:], op=ALU.add), reads=[ka, kb], writes=[ka])
                    pt, kpt = psT.next()
                    for ct in range(4):
                        P.op("pe", lambda e: e.matmul(pt[:, ct * 128:(ct + 1) * 128], lhsT=a_[:, ct * 128:(ct + 1) * 128], rhs=self.ident[:], start=True, stop=True),
                             reads=[ka, "ident"], writes=[kpt])
                    P.op("act", lambda e: e.copy(out=ycm[:, :, sbk * 128:(sbk + 1) * 128], in_=pt[:].rearrange("p (c t) -> p c t", c=4)),
                         reads=[kpt], writes=[("z_ycm", sbk)])
                yk = [("z_ycm", q) for q in range(4)]
                for ct in range(4):
                    pa, kpa = psA.next()
                    P.op("pe", lambda e: e.matmul(pa[:], lhsT=bones[:], rhs=ycm[:, ct, :], start=True, stop=True), reads=["z_bones"] + yk, writes=[kpa])
                    P.op("dve", lambda e: e.tensor_tensor(out=cen[:], in0=ycm[:, ct, :], in1=pa[:], op=ALU.subtract), reads=yk + [kpa], writes=["z_cen"])
                    P.op("pool", lambda e: e.tensor_tensor(out=sq[:], in0=cen[:], in1=cen[:], op=ALU.mult), reads=["z_cen"], writes=["z_sq"])
                    pa2, kpa2 = psA.next()
                    P.op("pe", lambda e: e.matmul(pa2[:], lhsT=bones[:], rhs=sq[:], start=True, stop=True), reads=["z_bones", "z_sq"], writes=[kpa2])
                    P.op("dve", lambda e: e.tensor_scalar(out=rstd[:], in0=pa2[:], scalar1=64e-5, scalar2=None, op0=ALU.add), reads=[kpa2], writes=["z_rstd"])
                    P.op("act", lambda e: e.activation(out=rstd[:], in_=rstd[:], func=AF.Sqrt), reads=["z_rstd"], writes=["z_rstd"])
                    P.op("dve", lambda e: e.reciprocal(out=rstd[:], in_=rstd[:]), reads=["z_rstd"], writes=["z_rstd"])
                    P.op("dve", lambda e: e.tensor_tensor(out=cen[:], in0=cen[:], in1=rstd[:], op=ALU.mult), reads=["z_cen", "z_rstd"], writes=["z_cen"])
                    P.op("dve", lambda e: e.tensor_scalar(out=cen[:], in0=cen[:], scalar1=pc[:, ct, 6:7], scalar2=pc[:, ct, 7:8], op0=ALU.mult, op1=ALU.add),
                         reads=["z_cen", "z_pc"], writes=["z_cen"])
                    P.op("pool", lambda e: e.tensor_tensor(out=cen[:], in0=cen[:], in1=bt_[:, ct, :], op=ALU.add), reads=["z_cen", "z_b"], writes=["z_cen"])
                    P.op("dve", lambda e: e.tensor_tensor(out=yo[:, ct, :], in0=cen[:], in1=gt[:, ct, :], op=ALU.mult), reads=["z_cen", "z_g"], writes=["z_yo"])
                P.op("pool", lambda e: e.dma_start(out=cm(self.YM[512:1024, t0:t0 + 512]), in_=yo[:]), reads=["z_yo"], dma=True)

    def _mod_reduce(self, eng, t, q, key):
        P = self.P
        P.op(eng, lambda e: e.tensor_scalar(out=t[1], in0=t[0], scalar1=1.0 / q, scalar2=MAGIC, op0=ALU.mult, op1=ALU.add), reads=[key], writes=[key + "_t"])
        P.op(eng, lambda e: e.tensor_scalar(out=t[1], in0=t[1], scalar1=MAGIC, scalar2=-float(q), op0=ALU.subtract, op1=ALU.mult), reads=[key + "_t"], writes=[key + "_t"])
        P.op(eng, lambda e: e.tensor_tensor(out=t[0], in0=t[0], in1=t[1], op=ALU.add), reads=[key, key + "_t"], writes=[key])

    def gen_dft(self, L):
        nc, P = self.nc, self.P
        nt = L // 128
        q = L // 32
        DF = self.dscr("DF_%d" % L, [2, nt, 128, nt, 128], BF16)
        DI = self.dscr("DI_%d" % L, [2, nt, 128, nt, 128], BF16)
        self.dft[L] = (DF, DI)
        sc = (TWO_PI / (4.0 * L)) * 0.999999
        with ExitStack() as s2:
            def sb(name, shape, dt=F32):
                return self.sbt(s2, name, shape, dt)
            ii = sb("g_ii", [128, 128], I32)
            pcol = sb("g_pcol", [128, 1]); p2col = sb("g_p2col", [128, 1]); jrow = sb("g_jrow", [128, 128]); crow = sb("g_crow", [128, nt])
            m1 = sb("g_m1", [128, nt, 128])
            P.op("pool", lambda e: e.iota(ii[:, 0:1], pattern=[[0, 1]], base=0, channel_multiplier=1), writes=["g_ii"])
            P.op("dve", lambda e: e.tensor_copy(out=pcol[:], in_=ii[:, 0:1]), reads=["g_ii"], writes=["g_pcol"])
            P.op("dve", lambda e: e.tensor_scalar(out=p2col[:], in0=pcol[:], scalar1=2.0, scalar2=1.0, op0=ALU.mult, op1=ALU.add), reads=["g_pcol"], writes=["g_p2col"])
            P.op("pool", lambda e: e.iota(ii[:], pattern=[[1, 128]], base=0, channel_multiplier=0), reads=["g_pcol"], writes=["g_ii"])
            P.op("dve", lambda e: e.tensor_copy(out=jrow[:], in_=ii[:]), reads=["g_ii"], writes=["g_jrow"])
            P.op("dve", lambda e: e.tensor_copy(out=crow[:], in_=jrow[:, 0:nt]), reads=["g_jrow"], writes=["g_crow"])
            P.op("dve", lambda e: e.tensor_tensor(out=m1[:], in0=crow[:].unsqueeze(2).broadcast_to([128, nt, 128]),
                                                  in1=jrow[:].unsqueeze(1).broadcast_to([128, nt, 128]), op=ALU.mult), reads=["g_crow", "g_jrow"], writes=["g_m1"])
            P.op("dve", lambda e: e.tensor_scalar(out=m1[:].rearrange("p a b -> p (a b)"), in0=m1[:].rearrange("p a b -> p (a b)"), scalar1=256.0, scalar2=None, op0=ALU.mult),
                 reads=["g_m1"], writes=["g_m1"])
            sets = {}
            for eng in ("dve",):
                sets[eng] = dict(
                    f2=sb("g_f2" + eng, [128, 128]), f1=sb("g_f1" + eng, [128, 128]), col=sb("g_col" + eng, [128, 2]),
                    G=sb("g_G" + eng, [128, nt, 128]), Gt=sb("g_Gt" + eng, [128, nt, 128]), A=sb("g_A" + eng, [128, nt, 128]),
                    o=[sb("g_o%d%s" % (i, eng), [128, nt, 128], BF16) for i in range(2)])
            fl = lambda ap: ap.rearrange("p a b -> p (a b)")
            for g in range(nt):
                for inv in (0, 1):
                    eng = "dve"
                    S_ = sets[eng]
                    k = "g_" + eng
                    if inv == 0:
                        fc = g
                        P.op(eng, lambda e: e.tensor_scalar(out=S_["f2"][:], in0=jrow[:], scalar1=2.0, scalar2=float(256 * fc + 1), op0=ALU.mult, op1=ALU.add), reads=["g_jrow"], writes=[k + "f2"])
                        P.op(eng, lambda e: e.tensor_scalar(out=S_["f1"][:], in0=S_["f2"][:], scalar1=pcol[:, 0:1], scalar2=None, op0=ALU.mult), reads=[k + "f2", "g_pcol"], writes=[k + "f1"])
                        P.op(eng, lambda e: e.tensor_tensor(out=S_["G"][:], in0=S_["f2"][:].unsqueeze(1).broadcast_to([128, nt, 128]),
                                                            in1=crow[:].unsqueeze(2).broadcast_to([128, nt, 128]), op=ALU.mult), reads=[k + "f2", "g_crow"], writes=[k + "G"])
                        self._mod_reduce(eng, (fl(S_["G"][:]), fl(S_["Gt"][:])), q, k + "G")
                        P.op(eng, lambda e: e.tensor_scalar(out=fl(S_["G"][:]), in0=fl(S_["G"][:]), scalar1=128.0, scalar2=None, op0=ALU.mult), reads=[k + "G"], writes=[k + "G"])
                        P.op(eng, lambda e: e.tensor_tensor(out=S_["G"][:], in0=S_["G"][:], in1=S_["f1"][:].unsqueeze(1).broadcast_to([128, nt, 128]), op=ALU.add),
                             reads=[k + "G", k + "f1"], writes=[k + "G"])
                    else:
                        tc = g
                        P.op(eng, lambda e: e.tensor_scalar(out=S_["col"][:, 0:1], in0=p2col[:], scalar1=float(tc), scalar2=None, op0=ALU.mult), reads=["g_p2col"], writes=[k + "col"])
                        self._mod_reduce(eng, (S_["col"][:, 0:1], S_["col"][:, 1:2]), q, k + "col")
                        P.op(eng, lambda e: e.tensor_scalar(out=S_["col"][:, 0:1], in0=S_["col"][:, 0:1], scalar1=128.0, scalar2=None, op0=ALU.mult), reads=[k + "col"], writes=[k + "col"])
                        P.op(eng, lambda e: e.tensor_scalar(out=S_["f1"][:], in0=jrow[:], scalar1=p2col[:, 0:1], scalar2=S_["col"][:, 0:1], op0=ALU.mult, op1=ALU.add),
                             reads=["g_jrow", "g_p2col", k + "col"], writes=[k + "f1"])
                        P.op(eng, lambda e: e.tensor_tensor(out=S_["G"][:], in0=m1[:], in1=S_["f1"][:].unsqueeze(1).broadcast_to([128, nt, 128]), op=ALU.add),
                             reads=["g_m1", k + "f1"], writes=[k + "G"])
                    for trig in (0, 1):
                        P.op(eng, lambda e: e.tensor_scalar(out=fl(S_["A"][:]), in0=fl(S_["G"][:]), scalar1=float(L if trig == 0 else 0), scalar2=None, op0=ALU.add),
                             reads=[k + "G"], writes=[k + "A"])
                        self._mod_reduce(eng, (fl(S_["A"][:]), fl(S_["Gt"][:])), 4 * L, k + "A")
                        ob = S_["o"][trig]
                        ok = k + "o%d" % trig
                        P.op("act", lambda e: e.activation(out=fl(ob[:]), in_=fl(S_["A"][:]), func=AF.Sin, scale=sc), reads=[k + "A"], writes=[ok])
                        dst = (DF if inv == 0 else DI)[trig, g]
                        P.op("sp", lambda e: e.dma_start(out=dst, in_=ob[:]), reads=[ok], dma=True)
            P.barrier()

    def hyena(self, L, j):
        nc, P = self.nc, self.P
        nt = L // 128
        DF, DI = self.dft[L]
        HC = self.HYC
        HZ = self.HYZ
        KT = self.HYK
        KF = self.HYF
        with ExitStack() as s2:
            def sb(name, shape, dt=F32):
                return self.sbt(s2, name, shape, dt)
            hc = sb("h_hc", [128, 12, 4])
            pp = Rot([sb("h_pp%d" % i, [128, 2050]) for i in range(2)], "h_pp")
            oo = Rot([sb("h_oo%d" % i, [128, 2048]) for i in range(2)], "h_oo")
            TBc = min(L, 2048)
            P.op("sp", lambda e: e.dma_start(out=hc[:], in_=self.hy_cols[j]), writes=["h_hc"], dma=True)
            for rt_ in range(12):
                for t0 in range(0, L, TBc):
                    p_, kp = pp.next()
                    o_, ko = oo.next()
                    lo_t = max(t0 - 1, 0); hi_t = min(t0 + TBc + 1, L)
                    if t0 == 0:
                        P.op("pool", lambda e: e.memset(p_[:, 0:1], 0.0), writes=[kp])
                    if t0 + TBc == L:
                        P.op("pool", lambda e: e.memset(p_[:, TBc + 1:TBc + 2], 0.0), writes=[kp])
                    P.op("sp", lambda e: e.dma_start(out=p_[:, lo_t - (t0 - 1):hi_t - (t0 - 1)], in_=self.PJ[1024 + rt_ * 128:1024 + (rt_ + 1) * 128, lo_t:hi_t]), writes=[kp], dma=True)
                    eng = "dve"
                    P.op(eng, lambda e: e.tensor_scalar(out=o_[:, 0:TBc], in0=p_[:, 0:TBc], scalar1=hc[:, rt_, 0:1], scalar2=hc[:, rt_, 3:4], op0=ALU.mult, op1=ALU.add),
                         reads=[kp, "h_hc"], writes=[ko])
                    for q_ in (1, 2):
                        P.op(eng, lambda e: e.scalar_tensor_tensor(out=o_[:, 0:TBc], in0=p_[:, q_:q_ + TBc], scalar=hc[:, rt_, q_:q_ + 1], op0=ALU.mult, in1=o_[:, 0:TBc], op1=ALU.add),
                             reads=[kp, ko, "h_hc"], writes=[ko])
                    P.op("pool", lambda e: e.dma_start(out=HC[rt_ // 4, (rt_ % 4) * 128:(rt_ % 4 + 1) * 128, t0:t0 + TBc], in_=o_[:, 0:TBc]), reads=[ko], dma=True)
            P.barrier()
        with ExitStack() as s2:
            def sb(name, shape, dt=F32):
                return self.sbt(s2, name, shape, dt)
            w1 = sb("f_w1", [33, 64]); w2 = sb("f_w2", [64, 64]); w3 = sb("f_w3", [64, 2048]); fc_ = sb("f_fc", [64, 6])
            zp = sb("f_zp", [33, 512]); h1 = sb("f_h1", [64, 512]); h2 = sb("f_h2", [64, 512]); tq = sb("f_tq", [64, 512])
            drow = sb("f_drow", [128, 512]); tcol = sb("f_tcol", [128, nt]); dec = sb("f_dec", [128, 512])
            kf = sb("f_kf", [128, 2048]); kb = Rot([sb("f_kb%d" % i, [128, 2048], BF16) for i in range(2)], "f_kb"); ka = sb("f_ka", [128, 2048], BF16)
            rn = sb("f_rn", [128, 1024])
            psK = Rot(self.psum[0:3], "psK")
            psH = Rot(self.psum[3:4], "psH")
            psNm = self.psum[4:8]
            P.op("sp", lambda e: e.dma_start(out=w1[:], in_=self.hy_w1[j]), writes=["f_w"], dma=True)
            P.op("sp", lambda e: e.dma_start(out=w2[:], in_=self.hy_w2[j]), writes=["f_w"], dma=True)
            P.op("sp", lambda e: e.dma_start(out=w3[:], in_=self.hy_w3[j]), writes=["f_w"], dma=True)
            P.op("sp", lambda e: e.dma_start(out=fc_[:], in_=self.hy_fcols[j]), writes=["f_fc"], dma=True)
            P.op("sp", lambda e: e.dma_start(out=drow[:], in_=self.hy_drow[:, :]), writes=["f_drow"], dma=True)
            P.op("sp", lambda e: e.dma_start(out=tcol[:], in_=self.hy_ntcol[L][:, :]), writes=["f_tcol"], dma=True)
            for i_ in range(2):
                P.op("dve", lambda e: e.tensor_tensor(out=fc_[:, 4 + i_:5 + i_], in0=fc_[:, i_:i_ + 1], in1=fc_[:, 2 + i_:3 + i_], op=ALU.mult), reads=["f_fc"], writes=["f_fc"])
            for tb in range(nt):
                if tb % 4 == 0:
                    c0 = tb * 128
                    P.op("sp", lambda e: e.dma_start(out=zp[:], in_=self.hy_zpos[L][:, c0:c0 + 512]), writes=["f_zp"], dma=True)
                    ph, kph = psH.next()
                    P.op("pe", lambda e: e.matmul(ph[0:64, :], lhsT=w1[:], rhs=zp[:], start=True, stop=True), reads=["f_w", "f_zp"], writes=[kph])
                    P.op("dve", lambda e: e.tensor_scalar(out=h1[:], in0=ph[0:64, :], scalar1=fc_[:, 2:3], scalar2=fc_[:, 4:5], op0=ALU.mult, op1=ALU.add), reads=[kph, "f_fc"], writes=["f_h1"])
                    self._range_reduce(h1[:], tq[:], "f_h1", "f_tq", 512)
                    P.op("act", lambda e: e.activation(out=h1[:], in_=h1[:], func=AF.Sin), reads=["f_h1"], writes=["f_h1"])
                    ph, kph = psH.next()
                    P.op("pe", lambda e: e.matmul(ph[0:64, :], lhsT=w2[:], rhs=h1[:], start=True, stop=True), reads=["f_w", "f_h1"], writes=[kph])
                    P.op("dve", lambda e: e.tensor_scalar(out=h2[:], in0=ph[0:64, :], scalar1=fc_[:, 3:4], scalar2=fc_[:, 5:6], op0=ALU.mult, op1=ALU.add), reads=[kph, "f_fc"], writes=["f_h2"])
                    self._range_reduce(h2[:], tq[:], "f_h2", "f_tq", 512)
                    P.op("act", lambda e: e.activation(out=h2[:], in_=h2[:], func=AF.Sin), reads=["f_h2"], writes=["f_h2"])
                cc = (tb % 4) * 128
                P.op("act", lambda e: e.activation(out=dec[:], in_=drow[:], func=AF.Exp, scale=tcol[:, tb:tb + 1]), reads=["f_drow", "f_tcol"], writes=["f_dec"])
                for g4 in range(4):
                    pk, kpk = psK.next()
                    P.op("pe", lambda e: e.matmul(pk[:], lhsT=h2[:, cc:cc + 128], rhs=w3[:, g4 * 512:(g4 + 1) * 512], start=True, stop=True), reads=["f_h2", "f_w"], writes=[kpk])
                    P.op("dve", lambda e: e.tensor_tensor(out=kf[:, g4 * 512:(g4 + 1) * 512], in0=pk[:], in1=dec[:], op=ALU.mult), reads=[kpk, "f_dec"], writes=[("f_kf", g4)])
                kfk = [("f_kf", g4) for g4 in range(4)]
                if tb == 0:
                    P.op("dve", lambda e: e.memset(kf[0:1, 1024:2048], 0.0), reads=kfk, writes=kfk)
                kb_, kkb = kb.next()
                P.op("pool", lambda e: e.tensor_copy(out=kb_[:], in_=kf[:]), reads=kfk, writes=[kkb])
                P.op("pool", lambda e: e.dma_start(out=KT[tb * 128:(tb + 1) * 128, :], in_=kb_[:]), reads=[kkb], dma=True)
                P.op("act", lambda e: e.activation(out=ka[:], in_=kf[:], func=AF.Abs), reads=kfk, writes=["f_ka"])
                for g4 in range(4):
                    P.op("pe", lambda e: e.matmul(psNm[g4][:], lhsT=self.ones_bf[:], rhs=ka[:, g4 * 512:(g4 + 1) * 512], start=(tb == 0), stop=(tb == nt - 1)),
                         reads=["f_ka", "ones_bf"], writes=[("psNm", g4)])
            for o in range(2):
                P.op("dve", lambda e: e.tensor_copy(out=rn[:, o * 512:(o + 1) * 512], in_=psNm[o][:]), reads=[("psNm", o)], writes=["f_rn"])
                P.op("dve", lambda e: e.tensor_tensor(out=rn[:, o * 512:(o + 1) * 512], in0=rn[:, o * 512:(o + 1) * 512], in1=psNm[2 + o][:], op=ALU.add),
                     reads=["f_rn", ("psNm", 2 + o)], writes=["f_rn"])
            P.op("dve", lambda e: e.reciprocal(out=rn[:], in_=rn[:]), reads=["f_rn"], writes=["f_rn"])
            P.op("pool", lambda e: e.dma_start(out=self.HYRN[:, :], in_=rn[:]), reads=["f_rn"], dma=True)
            P.barrier()
        with ExitStack() as s2:
            def sb(name, shape, dt=F32):
                return self.sbt(s2, name, shape, dt)
            rn = sb("k_rn", [128, 1024])
            kt = sb("k_kt", [128, nt, 2, 256], BF16)
            wc = Rot([sb("k_wc%d" % i, [128, nt, 128], BF16) for i in range(2)], "k_wc")
            ws = Rot([sb("k_ws%d" % i, [128, nt, 128], BF16) for i in range(2)], "k_ws")
            ko_ = Rot([sb("k_ko%d" % i, [128, 2, 256]) for i in range(2)], "k_ko")
            ps4 = [Rot(self.psum[2 * i:2 * i + 2], "psF%d" % i) for i in range(4)]
            P.op("sp", lambda e: e.dma_start(out=rn[:], in_=self.HYRN[:, :]), writes=["k_rn"], dma=True)
            for o in range(2):
                for hf in range(2):
                    for d in range(2):
                        c0 = d * 1024 + o * 512 + hf * 256
                        P.op("sp", lambda e: e.dma_start(out=kt[:, :, d, :], in_=KT[0:L, c0:c0 + 256].rearrange("(tc p) c -> p tc c", p=128)), writes=[("k_kt", d)], dma=True)
                    for fc in range(nt):
                        wc_, kwc = wc.next(); ws_, kws = ws.next()
                        P.op("sp", lambda e: e.dma_start(out=wc_[:], in_=DF[0, fc]), writes=[kwc], dma=True)
                        P.op("act", lambda e: e.dma_start(out=ws_[:], in_=DF[1, fc]), writes=[kws], dma=True)
                        acc = [r_.next() for r_ in ps4]
                        for tc in range(nt):
                            for ai, (w_, kw_, d) in enumerate(((wc_, kwc, 0), (ws_, kws, 0), (wc_, kwc, 1), (ws_, kws, 1))):
                                P.op("pe", lambda e: e.matmul(acc[ai][0][:, 0:256], lhsT=w_[:, tc, :], rhs=kt[:, tc, d, :], start=(tc == 0), stop=(tc == nt - 1)),
                                     reads=[kw_, ("k_kt", d)], writes=[acc[ai][1]])
                        o_, kko = ko_.next()
                        rs = rn[:, o * 512 + hf * 256:o * 512 + hf * 256 + 256]
                        P.op("dve", lambda e: e.tensor_copy(out=o_[:, 0, :], in_=acc[0][0][:, 0:256]), reads=[acc[0][1]], writes=[kko])
                        P.op("dve", lambda e: e.tensor_tensor(out=o_[:, 0, :], in0=o_[:, 0, :], in1=acc[2][0][:, 0:256], op=ALU.add), reads=[kko, acc[2][1]], writes=[kko])
                        P.op("dve", lambda e: e.tensor_tensor(out=o_[:, 0, :], in0=o_[:, 0, :], in1=rs, op=ALU.mult), reads=[kko, "k_rn"], writes=[kko])
                        P.op("dve", lambda e: e.tensor_copy(out=o_[:, 1, :], in_=acc[3][0][:, 0:256]), reads=[acc[3][1]], writes=[kko])
                        P.op("dve", lambda e: e.tensor_tensor(out=o_[:, 1, :], in0=o_[:, 1, :], in1=acc[1][0][:, 0:256], op=ALU.subtract), reads=[kko, acc[1][1]], writes=[kko])
                        P.op("dve", lambda e: e.tensor_tensor(out=o_[:, 1, :], in0=o_[:, 1, :], in1=rs, op=ALU.mult), reads=[kko, "k_rn"], writes=[kko])
                        P.op("pool", lambda e: e.dma_start(out=KF[o, hf, fc], in_=o_[:]), reads=[kko], dma=True)
            P.barrier()
        with ExitStack() as s2:
            def sb(name, shape, dt=F32):
                return self.sbt(s2, name, shape, dt)
            hb = sb("c_hb", [128, 4, 2])
            zt = sb("c_zt", [128, nt, 256], BF16)
            Pp = sb("c_P", [128, nt, 2, 256], BF16)
            wc = Rot([sb("c_wc%d" % i, [128, nt, 128], BF16) for i in range(2)], "c_wc")
            ws = Rot([sb("c_ws%d" % i, [128, nt, 128], BF16) for i in range(2)], "c_ws")
            kfR = Rot([sb("c_kf%d" % i, [128, 2, 256]) for i in range(2)], "c_kf")
            zin = Rot([sb("c_zin%d" % i, [128, 512]) for i in range(2)], "c_zin")
            zb = Rot([sb("c_zb%d" % i, [128, 512], BF16) for i in range(2)], "c_zb")
            ta_ = sb("c_ta", [128, 256]); tb_ = sb("c_tb", [128, 256])
            ytk = Rot([sb("c_ytk%d" % i, [128, 256]) for i in range(2)], "c_ytk")
            ycm = sb("c_ycm", [128, 2, 512]); gx = sb("c_gx", [128, 512]); zc = sb("c_zc", [128, 512]); ob = sb("c_ob", [128, 512], BF16)
            psX = [Rot(self.psum[0:2], "psXr"), Rot(self.psum[2:4], "psXs")]
            psY = Rot(self.psum[4:6], "psY")
            psT = Rot(self.psum[6:8], "psT")
            P.op("sp", lambda e: e.dma_start(out=hb[:], in_=self.hy_bias[j]), writes=["c_hb"], dma=True)
            for o in range(2):
                for hf in range(2):
                    zsrc = HC[0] if o == 0 else HZ
                    for ct2 in range(2):
                        r0 = hf * 256 + ct2 * 128
                        for t0 in range(0, L, 512):
                            zi, kzi = zin.next()
                            P.op("sp", lambda e: e.dma_start(out=zi[:], in_=zsrc[r0:r0 + 128, t0:t0 + 512]), writes=[kzi], dma=True)
                            zb_, kzb = zb.next()
                            P.op("pool", lambda e: e.tensor_copy(out=zb_[:], in_=zi[:]), reads=[kzi], writes=[kzb])
                            pt, kpt = psT.next()
                            for sbk in range(4):
                                P.op("pe", lambda e: e.matmul(pt[:, sbk * 128:(sbk + 1) * 128], lhsT=zi[:, sbk * 128:(sbk + 1) * 128], rhs=self.ident[:], start=True, stop=True),
                                     reads=[kzi, "ident"], writes=[kpt])
                            P.op("act", lambda e: e.copy(out=zt[:, t0 // 128:t0 // 128 + 4, ct2 * 128:(ct2 + 1) * 128], in_=pt[:].rearrange("p (a b) -> p a b", a=4)),
                                 reads=[kpt], writes=["c_zt"])
                    for fc in range(nt):
                        wc_, kwc = wc.next(); ws_, kws = ws.next()
                        P.op("sp", lambda e: e.dma_start(out=wc_[:], in_=DF[0, fc]), writes=[kwc], dma=True)
                        P.op("act", lambda e: e.dma_start(out=ws_[:], in_=DF[1, fc]), writes=[kws], dma=True)
                        kf_, kkf = kfR.next()
                        P.op("pool", lambda e: e.dma_start(out=kf_[:], in_=KF[o, hf, fc]), writes=[kkf], dma=True)
                        xr, kxr = psX[0].next(); xs, kxs = psX[1].next()
                        for tc in range(nt):
                            P.op("pe", lambda e: e.matmul(xr[:, 0:256], lhsT=wc_[:, tc, :], rhs=zt[:, tc, :], start=(tc == 0), stop=(tc == nt - 1)), reads=[kwc, "c_zt"], writes=[kxr])
                            P.op("pe", lambda e: e.matmul(xs[:, 0:256], lhsT=ws_[:, tc, :], rhs=zt[:, tc, :], start=(tc == 0), stop=(tc == nt - 1)), reads=[kws, "c_zt"], writes=[kxs])
                        P.op("dve", lambda e: e.tensor_tensor(out=ta_[:], in0=xr[:, 0:256], in1=kf_[:, 0, :], op=ALU.mult), reads=[kxr, kkf], writes=["c_ta"])
                        P.op("dve", lambda e: e.tensor_tensor(out=tb_[:], in0=xs[:, 0:256], in1=kf_[:, 1, :], op=ALU.mult), reads=[kxs, kkf], writes=["c_tb"])
                        P.op("dve", lambda e: e.tensor_tensor(out=Pp[:, fc, 0, :], in0=ta_[:], in1=tb_[:], op=ALU.add), reads=["c_ta", "c_tb"], writes=[("c_P", fc)])
                        P.op("dve", lambda e: e.tensor_tensor(out=ta_[:], in0=xs[:, 0:256], in1=kf_[:, 0, :], op=ALU.mult), reads=[kxs, kkf], writes=["c_ta"])
                        P.op("dve", lambda e: e.tensor_tensor(out=tb_[:], in0=xr[:, 0:256], in1=kf_[:, 1, :], op=ALU.mult), reads=[kxr, kkf], writes=["c_tb"])
                        P.op("dve", lambda e: e.tensor_tensor(out=Pp[:, fc, 1, :], in0=ta_[:], in1=tb_[:], op=ALU.subtract), reads=["c_ta", "c_tb"], writes=[("c_P", fc)])
                    pk = [("c_P", fc) for fc in range(nt)]
                    for tc in range(nt):
                        wc_, kwc = wc.next(); ws_, kws = ws.next()
                        P.op("sp", lambda e: e.dma_start(out=wc_[:], in_=DI[0, tc]), writes=[kwc], dma=True)
                        P.op("act", lambda e: e.dma_start(out=ws_[:], in_=DI[1, tc]), writes=[kws], dma=True)
                        py, kpy = psY.next()
                        for fc in range(nt):
                            P.op("pe", lambda e: e.matmul(py[:, 0:256], lhsT=wc_[:, fc, :], rhs=Pp[:, fc, 0, :], start=(fc == 0), stop=False), reads=[kwc] + pk, writes=[kpy])
                            P.op("pe", lambda e: e.matmul(py[:, 0:256], lhsT=ws_[:, fc, :], rhs=Pp[:, fc, 1, :], start=False, stop=(fc == nt - 1)), reads=[kws] + pk, writes=[kpy])
                        yk_, kyk = ytk.next()
                        P.op("act", lambda e: e.activation(out=yk_[:], in_=py[:, 0:256], func=AF.Copy, scale=1.0 / L), reads=[kpy], writes=[kyk])
                        pt, kpt = psT.next()
                        for ct2 in range(2):
                            P.op("pe", lambda e: e.matmul(pt[:, ct2 * 128:(ct2 + 1) * 128], lhsT=yk_[:, ct2 * 128:(ct2 + 1) * 128], rhs=self.ident[:], start=True, stop=True),
                                 reads=[kyk, "ident"], writes=[kpt])
                        P.op("act", lambda e: e.copy(out=ycm[:, :, (tc % 4) * 128:(tc % 4 + 1) * 128], in_=pt[:, 0:256].rearrange("p (a b) -> p a b", a=2)),
                             reads=[kpt], writes=[("c_ycm", tc % 4)])
                        if tc % 4 == 3:
                            t0 = (tc - 3) * 128
                            for ct2 in range(2):
                                r0 = hf * 256 + ct2 * 128
                                ctg = hf * 2 + ct2
                                P.op("sp", lambda e: e.dma_start(out=zc[:], in_=zsrc[r0:r0 + 128, t0:t0 + 512]), writes=["c_zc"], dma=True)
                                P.op("sp", lambda e: e.dma_start(out=gx[:], in_=HC[1 + o, r0:r0 + 128, t0:t0 + 512]), writes=["c_gx"], dma=True)
                                P.op("dve", lambda e: e.scalar_tensor_tensor(out=zc[:], in0=zc[:], scalar=hb[:, ctg, o:o + 1], op0=ALU.mult, in1=ycm[:, ct2, :], op1=ALU.add),
                                     reads=["c_zc",AllK.y, "c_hb"] if False else ["c_zc", "c_hb"] + [("c_ycm", q_) for q_ in range(4)], writes=["c_zc"])
                                if o == 0:
                                    P.op("pool", lambda e: e.tensor_tensor(out=zc[:], in0=zc[:], in1=gx[:], op=ALU.mult), reads=["c_zc", "c_gx"], writes=["c_zc"])
                                    P.op("pool", lambda e: e.dma_start(out=HZ[r0:r0 + 128, t0:t0 + 512], in_=zc[:]), reads=["c_zc"], dma=True)
                                else:
                                    P.op("pool", lambda e: e.tensor_tensor(out=ob[:], in0=zc[:], in1=gx[:], op=ALU.mult), reads=["c_zc", "c_gx"], writes=["c_ob"])
                                    P.op("pool", lambda e: e.dma_start(out=self.YM[512 + r0:512 + r0 + 128, t0:t0 + 512], in_=ob[:]), reads=["c_ob"], dma=True)
                    P.barrier()

    def dense_pass(self, xin, yout, L, l):
        nc, P = self.nc, self.P
        first = (l == 0)
        last = (l == self.depth)
        NH = TT // 512
        with ExitStack() as s2:
            def sb(name, shape, dt=F32):
                return self.sbt(s2, name, shape, dt)
            xt = sb("d_xt", [128, 8, TT])
            xn = sb("d_xn", [128, 8, TT], BF16)
            rstd = sb("d_rstd", [128, TT])
            hm = sb("d_hm", [128, 22, TT], BF16)
            sg = Rot([sb("d_sg%d" % i, [128, 512]) for i in range(2)], "d_sg")
            wgR = Rot([sb("d_wg%d" % i, [128, 8, 256], BF16) for i in range(3)], "d_wg")
            wuR = Rot([sb("d_wu%d" % i, [128, 8, 256], BF16) for i in range(3)], "d_wu")
            wdR = Rot([sb("d_wd%d" % i, [128, 22, 128], BF16) for i in range(2)], "d_wd")
            wpR = Rot([sb("d_wp%d" % i, [128, 8, 128], BF16) for i in range(3)], "d_wp")
            pjs = Rot([sb("d_pj%d" % i, [128, 512]) for i in range(2)], "d_pj")
            xtok = sb("d_xtok", [128, D])
            psG = Rot(self.psum[0:2], "psG")
            psU = Rot(self.psum[2:4], "psU")
            psA = Rot(self.psum[4:6], "psA")
            psN = Rot(self.psum[6:8], "psN")
            hs = lambda h: slice(h * 512, (h + 1) * 512)
            hmk = [("d_hm", c) for c in range(22)]

            def rmsnorm():
                P.op("act", lambda e: e.activation(out=hm[:, 0:8, :].rearrange("p a b -> p (a b)"),
                                                   in_=xt[:].rearrange("p a b -> p (a b)"), func=AF.Square),
                     reads=["d_xt"], writes=hmk[0:8])
                for h in range(NH):
                    pn, kn = psN.next()
                    for kc in range(8):
                        P.op("pe", lambda e: e.matmul(pn[:], lhsT=self.ones_bf[:], rhs=hm[:, kc, hs(h)], start=(kc == 0), stop=(kc == 7)),
                             reads=[("d_hm", kc), "ones_bf"], writes=[kn])
                    P.op("dve", lambda e: e.tensor_scalar(out=rstd[:, hs(h)], in0=pn[:], scalar1=1.0 / D, scalar2=EPS,
                                                          op0=ALU.mult, op1=ALU.add), reads=[kn], writes=["d_rstd"])
                P.op("act", lambda e: e.activation(out=rstd[:], in_=rstd[:], func=AF.Sqrt), reads=["d_rstd"], writes=["d_rstd"])
                P.op("dve", lambda e: e.reciprocal(out=rstd[:], in_=rstd[:]), reads=["d_rstd"], writes=["d_rstd"])

            def normed(which, out_t, key):
                rmsnorm()
                for kc in range(8):
                    P.op("dve", lambda e: e.scalar_tensor_tensor(out=out_t[:, kc, :], in0=xt[:, kc, :],
                                                                 scalar=self.gcols[:, which, kc:kc + 1], op0=ALU.mult,
                                                                 in1=rstd[:], op1=ALU.mult),
                         reads=["d_xt", "d_rstd", "gcols"], writes=[key])

            def ffn(f, lay):
                normed((0 if f == 1 else 2) * DEPTH + lay, xn, "d_xn")
                for mp in range(11):
                    wg, kg = wgR.next()
                    wu, ku = wuR.next()
                    P.op("sp", lambda e: e.dma_start(out=wg[:], in_=self.wb[("g", f, lay)][mp]), writes=[kg], dma=True)
                    P.op("sp", lambda e: e.dma_start(out=wu[:], in_=self.wb[("u", f, lay)][mp]), writes=[ku], dma=True)
                    for mi in range(2):
                        mc = 2 * mp + mi
                        for h in range(NH):
                            pg, kpg = psG.next()
                            pu, kpu = psU.next()
                            for kc in range(8):
                                P.op("pe", lambda e: e.matmul(pg[:], lhsT=wg[:, kc, mi * 128:(mi + 1) * 128], rhs=xn[:, kc, hs(h)],
                                                              start=(kc == 0), stop=(kc == 7)), reads=[kg, "d_xn"], writes=[kpg])
                            for kc in range(8):
                                P.op("pe", lambda e: e.matmul(pu[:], lhsT=wu[:, kc, mi * 128:(mi + 1) * 128], rhs=xn[:, kc, hs(h)],
                                                              start=(kc == 0), stop=(kc == 7)), reads=[ku, "d_xn"], writes=[kpu])
                            s_, ks = sg.next()
                            P.op("act", lambda e: e.activation(out=s_[:], in_=pg[:], func=AF.Silu), reads=[kpg], writes=[ks])
                            P.op("dve", lambda e: e.tensor_tensor(out=hm[:, mc, hs(h)], in0=s_[:], in1=pu[:], op=ALU.mult),
                                 reads=[ks, kpu], writes=[("d_hm", mc)])
                for dc in range(8):
                    wd, kd = wdR.next()
                    P.op("sp", lambda e: e.dma_start(out=wd[:], in_=self.wb[("d", f, lay)][dc]), writes=[kd], dma=True)
                    for h in range(NH):
                        pa, kpa = psA.next()
                        for fc in range(22):
                            P.op("pe", lambda e: e.matmul(pa[:], lhsT=wd[:, fc, :], rhs=hm[:, fc, hs(h)], start=(fc == 0), stop=(fc == 21)),
                                 reads=[kd, ("d_hm", fc)], writes=[kpa])
                        P.op("dve", lambda e: e.scalar_tensor_tensor(out=xt[:, dc, hs(h)], in0=pa[:], scalar=0.5, op0=ALU.mult,
                                                                     in1=xt[:, dc, hs(h)], op1=ALU.add),
                             reads=[kpa, "d_xt"], writes=["d_xt"])

            for ti in range(L // TT):
                t0 = ti * TT
                if first:
                    for b4 in range(TT // 128):
                        P.op("sp", lambda e: e.dma_start(out=xtok[:], in_=xin[t0 + b4 * 128:t0 + (b4 + 1) * 128, :]),
                             writes=["d_xtok"], dma=True)
                        for half in range(2):
                            pn, kn = psN.next()
                            for q in range(4):
                                dc = half * 4 + q
                                P.op("pe", lambda e: e.matmul(pn[:, q * 128:(q + 1) * 128], lhsT=xtok[:, dc * 128:(dc + 1) * 128],
                                                              rhs=self.ident[:], start=True, stop=True),
                                     reads=["d_xtok", "ident"], writes=[kn])
                            P.op("act", lambda e: e.copy(out=xt[:, half * 4:half * 4 + 4, b4 * 128:(b4 + 1) * 128],
                                                         in_=pn[:].rearrange("p (a b) -> p a b", a=4)),
                                 reads=[kn], writes=["d_xt"])
                else:
                    P.op("sp", lambda e: e.dma_start(out=xt[:], in_=self.XT.rearrange("(dc p) t -> p dc t", p=128)[:, :, t0:t0 + TT]),
                         writes=["d_xt"], dma=True)
                    lay = l - 1
                    P.op("sp", lambda e: e.dma_start(out=xn[:], in_=self.YM.rearrange("(dc p) t -> p dc t", p=128)[:, :, t0:t0 + TT]),
                         writes=["d_xn"], dma=True)
                    for dc in range(8):
                        wp, kp = wpR.next()
                        P.op("sp", lambda e: e.dma_start(out=wp[:], in_=self.wb[("out", lay)][dc]), writes=[kp], dma=True)
                        for h in range(NH):
                            pa, kpa = psA.next()
                            for kc in range(8):
                                P.op("pe", lambda e: e.matmul(pa[:], lhsT=wp[:, kc, :], rhs=xn[:, kc, hs(h)], start=(kc == 0), stop=(kc == 7)),
                                     reads=[kp, "d_xn"], writes=[kpa])
                            P.op("dve", lambda e: e.tensor_tensor(out=xt[:, dc, hs(h)], in0=pa[:], in1=xt[:, dc, hs(h)], op=ALU.add),
                                 reads=[kpa, "d_xt"], writes=["d_xt"])
                    ffn(2, lay)
                if not last:
                    ffn(1, l)
                    normed(1 * DEPTH + l, xn, "d_xn")
                    cin = AB_IN if l % 2 == 0 else CD_IN
                    for ci in range((cin + 127) // 128):
                        w = min(128, cin - ci * 128)
                        wp, kp = wpR.next()
                        P.op("sp", lambda e: e.dma_start(out=wp[:], in_=self.wb[("in", l)][ci]), writes=[kp], dma=True)
                        for h in range(NH):
                            pa, kpa = psA.next()
                            for kc in range(8):
                                P.op("pe", lambda e: e.matmul(pa[0:w, :], lhsT=wp[:, kc, 0:w], rhs=xn[:, kc, hs(h)], start=(kc == 0), stop=(kc == 7)),
                                     reads=[kp, "d_xn"], writes=[kpa])
                            pj, kj = pjs.next()
                            P.op("act", lambda e: e.copy(out=pj[0:w, :], in_=pa[0:w, :]), reads=[kpa], writes=[kj])
                            P.op("pool", lambda e: e.dma_start(out=self.PJ[ci * 128:ci * 128 + w, t0 + h * 512:t0 + (h + 1) * 512], in_=pj[0:w, :]),
                                 reads=[kj], dma=True)
                    P.op("pool", lambda e: e.dma_start(out=self.XT.rearrange("(dc p) t -> p dc t", p=128)[:, :, t0:t0 + TT], in_=xt[:]),
                         reads=["d_xt"], dma=True)
                else:
                    rmsnorm()
                    for kc in range(8):
                        P.op("dve", lambda e: e.scalar_tensor_tensor(out=xt[:, kc, :], in0=xt[:, kc, :],
                                                                     scalar=self.gcols[:, 3 * DEPTH, kc:kc + 1], op0=ALU.mult,
                                                                     in1=rstd[:], op1=ALU.mult),
                             reads=["d_xt", "d_rstd", "gcols"], writes=["d_xt"])
                    for b4 in range(TT // 128):
                        for half in range(2):
                            pn, kn = psN.next()
                            for q in range(4):
                                dc = half * 4 + q
                                P.op("pe", lambda e: e.matmul(pn[:, q * 128:(q + 1) * 128], lhsT=xt[:, dc, b4 * 128:(b4 + 1) * 128],
                                                              rhs=self.ident[:], start=True, stop=True),
                                     reads=["d_xt", "ident"], writes=[kn])
                            P.op("act", lambda e: e.copy(out=xtok[:, half * 512:(half + 1) * 512], in_=pn[:]),
                                 reads=[kn], writes=["d_xtok"])
                        P.op("pool", lambda e: e.dma_start(out=yout[t0 + b4 * 128:t0 + (b4 + 1) * 128, :], in_=xtok[:]),
                             reads=["d_xtok"], dma=True)


_CACHE = {}


def _host_layout(inputs):
    f = lambda a: np.ascontiguousarray(np.asarray(a, dtype=np.float32))
    shared = {}
    for nm in ("ffn1_w_gate", "ffn1_w_up", "ffn1_w_down", "ffn2_w_gate", "ffn2_w_up", "ffn2_w_down",
               "ab_w_in", "ab_w_out", "cd_w_in", "cd_w_out"):
        shared[nm] = f(inputs[nm])
    norms = np.concatenate([f(inputs["ffn1_norm"]), f(inputs["mix_norm"]), f(inputs["ffn2_norm"]),
                            f(inputs["final_norm"])[None, :]], axis=0)
    shared["norms"] = np.ascontiguousarray(norms.reshape(3 * DEPTH + 1, 8, 128).transpose(2, 0, 1))
    cw, cb = f(inputs["lru_conv_w"]), f(inputs["lru_conv_b"])
    lam, ba, bx = f(inputs["lru_lambda"]), f(inputs["lru_ba"]), f(inputs["lru_bx"])
    colv = [cw[:, 0], cw[:, 1], cw[:, 2], cw[:, 3], cb]
    for d in range(2):
        colv += [lam[:, d], ba[:, d], bx[:, d]]
    colv = np.stack(colv, axis=-1)
    shared["lru_cols"] = np.ascontiguousarray(colv.reshape(2, 4, 128, 11).transpose(0, 2, 1, 3))
    wa, wx = f(inputs["lru_wa"]), f(inputs["lru_wx"])
    wbd = np.zeros((2, 4, 128, 4, 128), np.float32)
    for d in range(2):
        for wi, wsrc in enumerate((wa, wx)):
            for ct in range(4):
                for hh in range(2):
                    wbd[:, ct, hh * 64:(hh + 1) * 64, 2 * d + wi, hh * 64:(hh + 1) * 64] = wsrc[:, d, 2 * ct + hh]
    shared["lru_wbd"] = wbd
    lre, lim, lst = f(inputs["s5_lambda_re"]), f(inputs["s5_lambda_im"]), f(inputs["s5_log_step"])
    prm = np.stack([lre, lim, np.broadcast_to(lst[..., None], lre.shape)], axis=-1)
    prm = prm.reshape(2, 2, 16, 2, 64, 3).transpose(0, 3, 4, 2, 1, 5)
    shared["s5_prm"] = np.ascontiguousarray(prm.reshape(2, 128, 32, 3))
    bre, bim = f(inputs["s5_b_re"]), f(inputs["s5_b_im"])
    bt = np.zeros((2, 16, 128, 2, 2, 128), np.float32)
    for ri, bsrc in enumerate((bre, bim)):
        for g in range(32):
            T, gp, gl = g // 2, (g % 8) // 2, g % 2
            r0 = gp * 32 + gl * 16
            bt[:, T, r0:r0 + 16, :, ri, gl * 64:(gl + 1) * 64] = bsrc[:, :, g].transpose(0, 3, 1, 2)
    shared["s5_bt"] = bt
    cre_, cim_ = f(inputs["s5_c_re"]), f(inputs["s5_c_im"])
    ct = np.zeros((2, 16, 2, 128, 2, 32), np.float32)
    for ri, csrc in enumerate((cre_, cim_)):
        for g in range(32):
            T, gl = g // 2, g % 2
            ct[:, T, :, gl * 64:(gl + 1) * 64, ri, gl * 16:(gl + 1) * 16] = csrc[:, :, g].transpose(0, 1, 3, 2)
    shared["s5_ct"] = ct
    pc = np.stack([f(inputs["s5_d"]), f(inputs["s5_glu_b"])], axis=-1)
    shared["s5_pc"] = np.ascontiguousarray(pc.reshape(2, 4, 128, 2).transpose(0, 2, 1, 3))
    shared["s5_glu_w"] = f(inputs["s5_glu_w"])
    mu = f(inputs["rw_mu"])
    mut = np.zeros((2, 14, 128), np.float32)
    mut[:, 0:12] = mu[:, 0:1536].reshape(2, 12, 128)
    mut[:, 12, 0:96] = mu[:, 1536:1632]
    mut[:, 13, 0:64] = mu[:, 1632:1696]
    shared["rw_mu"] = np.ascontiguousarray(mut.transpose(0, 2, 1))
    w0 = f(inputs["rw_w0"])
    pcs = np.stack([w0[:, 0], w0[:, 1], f(inputs["rw_a0"]), f(inputs["rw_k_k"]), f(inputs["rw_k_a"]),
                    f(inputs["rw_r_k"]).reshape(2, 512), f(inputs["rw_ln_w"]), f(inputs["rw_ln_b"])], axis=-1)
    shared["rw_pc"] = np.ascontiguousarray(pcs.reshape(2, 4, 128, 8).transpose(0, 2, 1, 3))
    lwt = np.zeros((2, 128, 2, 512), np.float32)
    wup = f(inputs["rw_w_up"])
    lwt[:, 0:32, 0] = wup[:, 0]
    lwt[:, 32:64, 0] = wup[:, 1]
    lwt[:, 64:96, 0] = f(inputs["rw_a_up"])
    lwt[:, 0:64, 1] = f(inputs["rw_g_up"])
    shared["rw_lw"] = lwt
    hcw, hcb = f(inputs["hy_conv_w"]), f(inputs["hy_conv_b"])
    hcols = np.stack([hcw[:, 0], hcw[:, 1], hcw[:, 2], hcb], axis=-1)
    shared["hy_cols"] = np.ascontiguousarray(hcols.reshape(2, 12, 128, 4).transpose(0, 2, 1, 3))
    shared["hy_w1"] = f(inputs["hy_f_w1"]); shared["hy_w2"] = f(inputs["hy_f_w2"]); shared["hy_w3"] = f(inputs["hy_f_w3"])
    fq = f(inputs["hy_f_freq"])
    z64 = np.zeros((2, 64), np.float32)
    shared["hy_fcols"] = np.ascontiguousarray(np.stack([f(inputs["hy_f_b1"]), f(inputs["hy_f_b2"]), fq[:, 0], fq[:, 1], z64, z64], axis=-1))
    hbias = f(inputs["hy_bias"])
    shared["hy_bias"] = np.ascontiguousarray(hbias.reshape(2, 2, 4, 128).transpose(0, 3, 2, 1))
    return shared


def _hy_consts(L):
    t = np.linspace(0.0, 1.0, L, dtype=np.float32)[:, None]
    bands = 16
    freqs = np.linspace(1e-4, bands - 1, bands, dtype=np.float32)[None, :]
    wpos = (np.float32(2.0 * math.pi / L) * np.arange(L, dtype=np.float32))[:, None]
    z = np.concatenate([t, np.cos(freqs * wpos), -np.sin(freqs * wpos)], axis=-1).astype(np.float32)
    ntcol = np.ascontiguousarray((-t[:, 0]).reshape(L // 128, 128).T)
    return np.ascontiguousarray(z.T), ntcol


def _hy_drow():
    max_decay = math.log(1e-2) / 0.3
    min_decay = math.log(1e-2) / 1.5
    deltas = np.abs(np.linspace(min_decay, max_decay, 512, dtype=np.float32))
    return np.ascontiguousarray(np.broadcast_to(deltas[None, :], (128, 512))).astype(np.float32)


def run(inputs, LP, LS, enable, depth=DEPTH, ncores=8):
    key = (LP, LS, tuple(sorted(enable.items())), depth)
    if key not in _CACHE:
        _CACHE[key] = Builder(LP, LS, enable, depth).build()
    nc = _CACHE[key]
    shared = _host_layout(inputs)
    shared["hy_drow"] = _hy_drow()
    for LL in sorted(set((LP, LS))):
        zT, ntc = _hy_consts(LL)
        shared["hy_zpos_%d" % LL] = zT
        shared["hy_ntcol_%d" % LL] = ntc
    xp = np.asarray(inputs["x_prompt"], dtype=np.float32)
    xs = np.asarray(inputs["x_sample"], dtype=np.float32)
    in_maps = []
    for c in range(ncores):
        m = dict(shared)
        m["xp"] = np.ascontiguousarray(xp[c])
        m["xs"] = np.ascontiguousarray(xs[c % 2])
        in_maps.append(m)
    res = run_bass_kernel_spmd(nc, in_maps, core_ids=list(range(ncores)))
    yp = np.stack([res.results[c]["yp"] for c in range(ncores)], axis=0).astype(np.float32)
    ys = np.stack([res.results[c]["ys"] for c in range(2)], axis=0).astype(np.float32)
    return yp, ys


ENABLE = dict(s5=True, rwkv=True, lru=True, hyena=True)


def kernel(**inputs):
    return run(inputs, 4096, 8192, ENABLE)
```

```python
import math
from contextlib import ExitStack
import numpy as np
import concourse.bass as bass
import concourse.mybir as mybir
from concourse.bass_utils import run_bass_kernel_spmd

F32 = mybir.dt.float32
BF16 = mybir.dt.bfloat16
I32 = mybir.dt.int32
ALU = mybir.AluOpType
AF = mybir.ActivationFunctionType
AX = mybir.AxisListType

D = 1024
DFF = 2816
DEPTH = 4
AB_IN = 2208
CD_IN = 2560
TT = 1024
EPS = 1e-6
MAGIC = 12582912.0
TWO_PI = 2.0 * math.pi
TBMAX = 2048


class Prog:
    ENG = ("pe", "act", "dve", "pool", "sp")
    NRING = 8
    LOOKBACK = 6

    def __init__(self, nc, stack):
        self.nc = nc
        self.cnt = {e: 0 for e in self.ENG}
        self.ringn = {e: 0 for e in self.ENG}
        self.waited = {e: {} for e in self.ENG}
        self.lastw = {}
        self.readers = {}
        self.nins = 0
        self.engobj = {"pe": nc.tensor, "act": nc.scalar, "dve": nc.vector, "pool": nc.gpsimd, "sp": nc.sync}
        self.sem = {e: stack.enter_context(nc.semaphore("s_" + e)) for e in self.ENG}
        self.ring = {e: [stack.enter_context(nc.semaphore("r_%s%d" % (e, i))) for i in range(self.NRING)]
                     for e in ("sp", "act", "pool")}

    def _semh(self, semk):
        if semk[0] == "eng":
            return self.sem[semk[1]]
        return self.ring[semk[1]][semk[2]]

    def _need(self, eng, tok, waits):
        if tok is None:
            return
        semk, val, teng, isdma = tok
        if (not isdma) and teng == eng and val <= self.cnt[eng] - self.LOOKBACK:
            return
        if self.waited[eng].get(semk, 0) >= val:
            return
        if waits.get(semk, 0) < val:
            waits[semk] = val

    def op(self, eng, fn, reads=(), writes=(), dma=False):
        waits = {}
        for k in reads:
            self._need(eng, self.lastw.get(k), waits)
        for k in writes:
            self._need(eng, self.lastw.get(k), waits)
            for t in self.readers.get(k, {}).values():
                self._need(eng, t, waits)
        if dma:
            i = self.ringn[eng]
            self.ringn[eng] += 1
            slot = i % self.NRING
            val = 16 * (i // self.NRING + 1)
            semk = ("ring", eng, slot)
            if i >= self.NRING:
                self._need(eng, (semk, val - 16, eng, True), waits)
            tok = (semk, val, eng, True)
        else:
            self.cnt[eng] += 1
            tok = (("eng", eng), self.cnt[eng], eng, False)
        e = self.engobj[eng]
        for semk, val in waits.items():
            self.waited[eng][semk] = val
            e.wait_ge(self._semh(semk), val)
        ins = fn(e)
        ins.then_inc(self._semh(tok[0]), 16 if dma else 1)
        self.nins += 1 + len(waits)
        for k in reads:
            self.readers.setdefault(k, {})[(eng, tok[0] if dma else 0)] = tok
        for k in writes:
            self.lastw[k] = tok
            self.readers[k] = {}
        return tok

    def _all_tokens(self):
        toks = []
        for q in self.ENG:
            if self.cnt[q] > 0:
                toks.append((("eng", q), self.cnt[q], q, False))
        for q in ("sp", "act", "pool"):
            n = self.ringn[q]
            for slot in range(min(n, self.NRING)):
                c = (n - 1 - slot) // self.NRING + 1
                toks.append((("ring", q, slot), 16 * c, q, True))
        return toks

    def barrier(self, engs=None):
        toks = self._all_tokens()
        for eng in (engs or self.ENG):
            e = self.engobj[eng]
            for semk, val, teng, isdma in toks:
                if self.waited[eng].get(semk, 0) >= val:
                    continue
                if (not isdma) and teng == eng:
                    continue
                self.waited[eng][semk] = val
                e.wait_ge(self._semh(semk), val)
                self.nins += 1
        if engs is None:
            self.lastw = {}
            self.readers = {}


class Rot:
    def __init__(self, bufs, name):
        self.bufs = bufs
        self.name = name
        self.i = 0

    def next(self):
        j = self.i % len(self.bufs)
        self.i += 1
        return self.bufs[j], (self.name, j)


def _wlayout(nco, R, cw):
    return [nco, 128, R // 128, cw]


class Builder:
    def __init__(self, LP, LS, enable, depth=DEPTH):
        self.LP, self.LS, self.enable, self.depth = LP, LS, enable, depth
        self.Lmax = max(LP, LS)
        self.nc = bass.Bass("TRN2", target_bir_lowering=False)
        self.inputs = {}
        self.scr = {}
        self.uid = 0

    def sbt(self, stack, name, shape, dt=F32):
        self.uid += 1
        return stack.enter_context(self.nc.sbuf_tensor("%s_u%d" % (name, self.uid), list(shape), dt))

    def din(self, name, shape, dt=F32):
        t = self.nc.dram_tensor(name, list(shape), dt, kind="ExternalInput").ap()
        self.inputs[name] = t
        return t

    def dscr(self, name, shape, dt=F32):
        t = self.nc.dram_tensor(name, list(shape), dt).ap()
        self.scr[name] = t
        return t

    def build(self):
        nc = self.nc
        LP, LS, Lmax = self.LP, self.LS, self.Lmax
        self.xp = self.din("xp", [LP, D])
        self.xs = self.din("xs", [LS, D])
        self.yp = nc.dram_tensor("yp", [LP, D], F32, kind="ExternalOutput").ap()
        self.ys = nc.dram_tensor("ys", [LS, D], F32, kind="ExternalOutput").ap()
        self.w_raw = {}
        for nm in ("ffn1_w_gate", "ffn1_w_up", "ffn2_w_gate", "ffn2_w_up"):
            self.w_raw[nm] = self.din(nm, [DEPTH, D, DFF])
        for nm in ("ffn1_w_down", "ffn2_w_down"):
            self.w_raw[nm] = self.din(nm, [DEPTH, DFF, D])
        self.w_raw["ab_w_in"] = self.din("ab_w_in", [2, D, AB_IN])
        self.w_raw["cd_w_in"] = self.din("cd_w_in", [2, D, CD_IN])
        self.w_raw["ab_w_out"] = self.din("ab_w_out", [2, D, D])
        self.w_raw["cd_w_out"] = self.din("cd_w_out", [2, D, D])
        self.norms = self.din("norms", [128, 3 * DEPTH + 1, 8])
        self.lru_cols = self.din("lru_cols", [2, 128, 4, 11])
        self.s5_prm = self.din("s5_prm", [2, 128, 32, 3])
        self.rw_mu = self.din("rw_mu", [2, 128, 14])
        self.hy_cols = self.din("hy_cols", [2, 128, 12, 4])
        self.hy_w1 = self.din("hy_w1", [2, 33, 64])
        self.hy_w2 = self.din("hy_w2", [2, 64, 64])
        self.hy_w3 = self.din("hy_w3", [2, 64, 2048])
        self.hy_fcols = self.din("hy_fcols", [2, 64, 6])
        self.hy_bias = self.din("hy_bias", [2, 128, 4, 2])
        self.hy_drow = self.din("hy_drow", [128, 512])
        self.hy_zpos = {}
        self.hy_ntcol = {}
        for LL in sorted(set((LP, LS))):
            self.hy_zpos[LL] = self.din("hy_zpos_%d" % LL, [33, LL])
            self.hy_ntcol[LL] = self.din("hy_ntcol_%d" % LL, [128, LL // 128])
        self.rw_pc = self.din("rw_pc", [2, 128, 4, 8])
        self.rw_lw = self.din("rw_lw", [2, 128, 2, 512])
        self.s5_bt = self.din("s5_bt", [2, 16, 128, 2, 2, 128])
        self.s5_ct = self.din("s5_ct", [2, 16, 2, 128, 2, 32])
        self.s5_pc = self.din("s5_pc", [2, 128, 4, 2])
        self.w_raw["s5_glu_w"] = self.din("s5_glu_w", [2, 512, 512])
        self.lru_wbd = self.din("lru_wbd", [2, 4, 128, 4, 128])
        self.wb = {}
        for l in range(DEPTH):
            for f in (1, 2):
                self.wb[("g", f, l)] = self.dscr("wg%d_%d" % (f, l), _wlayout(11, D, 256), BF16)
                self.wb[("u", f, l)] = self.dscr("wu%d_%d" % (f, l), _wlayout(11, D, 256), BF16)
                self.wb[("d", f, l)] = self.dscr("wd%d_%d" % (f, l), _wlayout(8, DFF, 128), BF16)
            cin = AB_IN if l % 2 == 0 else CD_IN
            self.wb[("in", l)] = self.dscr("win_%d" % l, _wlayout((cin + 127) // 128, D, 128), BF16)
            self.wb[("out", l)] = self.dscr("wout_%d" % l, _wlayout(8, D, 128), BF16)
        for jj in range(2):
            self.wb[("glu", jj)] = self.dscr("wglu_%d" % jj, _wlayout(4, 512, 128), BF16)
        self.YS5 = self.dscr("YS5", [2, 512, Lmax])
        self.RWTM = self.dscr("RWTM", [7, Lmax, 512])
        self.HYC = self.dscr("HYC", [3, 512, Lmax])
        self.HYZ = self.dscr("HYZ", [512, Lmax])
        self.HYK = self.dscr("HYK", [Lmax, 2048], BF16)
        self.HYF = self.dscr("HYF", [2, 2, Lmax // 128, 128, 2, 256])
        self.HYRN = self.dscr("HYRN", [128, 1024])
        self.dft = {}
        self.YTM = self.dscr("YTM", [2, Lmax, 512])
        self.RWG = self.dscr("RWG", [512, Lmax])
        self.RWB = self.dscr("RWB", [512, Lmax])
        self.XT = self.dscr("XT", [D, Lmax])
        self.PJ = self.dscr("PJ", [CD_IN, Lmax])
        self.YM = self.dscr("YM", [D, Lmax], BF16)

        with ExitStack() as st:
            self.P = Prog(nc, st)
            self.st = st
            self.ident = st.enter_context(nc.sbuf_tensor("ident", [128, 128], F32))
            self.ones_bf = st.enter_context(nc.sbuf_tensor("ones_bf", [128, 128], BF16))
            self.gcols = st.enter_context(nc.sbuf_tensor("gcols", [128, 3 * DEPTH + 1, 8], F32))
            self.psum = [st.enter_context(nc.psum_tensor("ps%d" % i, [128, 512], F32)) for i in range(8)]
            self.setup_consts()
            self.cast_weights()
            self.P.barrier()
            if self.enable.get("hyena"):
                for LL in sorted(set((LP, LS))):
                    self.gen_dft(LL)
            for (xin, yout, L, tag) in ((self.xp, self.yp, LP, "P"), (self.xs, self.ys, LS, "S")):
                self.trunk(xin, yout, L, tag)
                self.P.barrier()
            self.P.barrier(engs=["sp"])
        return nc

    def setup_consts(self):
        nc, P = self.nc, self.P
        with ExitStack() as s2:
            io = self.sbt(s2, "c_io", [128, 128], I32)
            iof = self.sbt(s2, "c_iof", [128, 128], F32)
            P.op("pool", lambda e: e.iota(io[:], pattern=[[1, 128]], base=0, channel_multiplier=-1), writes=["c_io"])
            P.op("dve", lambda e: e.tensor_copy(out=iof[:], in_=io[:]), reads=["c_io"], writes=["c_iof"])
            P.op("dve", lambda e: e.tensor_scalar(out=self.ident[:], in0=iof[:], scalar1=0.0, scalar2=None,
                                                  op0=ALU.is_equal), reads=["c_iof"], writes=["ident"])
            P.op("dve", lambda e: e.memset(self.ones_bf[:], 1.0), writes=["ones_bf"])
            P.op("sp", lambda e: e.dma_start(out=self.gcols[:], in_=self.norms[:, :, :]), writes=["gcols"], dma=True)
            P.barrier()

    def cast_weights(self):
        nc, P = self.nc, self.P
        with ExitStack() as s2:
            CW = 2048
            stg = Rot([self.sbt(s2, "cw_s%d" % i, [128, CW], F32) for i in range(3)], "cw_s")
            stb = Rot([self.sbt(s2, "cw_b%d" % i, [128, CW], BF16) for i in range(3)], "cw_b")
            self._cast_n = 0

            def cast(src, dst, R, C, cw):
                nco = dst.shape[0]
                for kc in range(R // 128):
                    c0 = 0
                    while c0 < C:
                        wc = min(CW, C - c0)
                        wc_pad = ((wc + cw - 1) // cw) * cw
                        a, ka = stg.next()
                        b, kb = stb.next()
                        P.op("sp", lambda e: e.dma_start(out=a[:, 0:wc], in_=src[kc * 128:(kc + 1) * 128, c0:c0 + wc]),
                             writes=[ka], dma=True)
                        if wc_pad != wc:
                            P.op("pool", lambda e: e.memset(b[:, wc:wc_pad], 0.0), writes=[kb])
                        eng = ("act", "dve", "pool")[self._cast_n % 3]
                        self._cast_n += 1
                        if eng == "act":
                            P.op("act", lambda e: e.copy(out=b[:, 0:wc], in_=a[:, 0:wc]), reads=[ka], writes=[kb])
                        else:
                            P.op(eng, lambda e: e.tensor_copy(out=b[:, 0:wc], in_=a[:, 0:wc]), reads=[ka], writes=[kb])
                        co0 = c0 // cw
                        nn = wc_pad // cw
                        dview = dst.rearrange("co p kc j -> p co kc j")[:, co0:co0 + nn, kc, :]
                        P.op("pool", lambda e: e.dma_start(out=dview, in_=b[:, 0:wc_pad].rearrange("p (co j) -> p co j", j=cw)),
                             reads=[kb], dma=True)
                        c0 += wc

            for l in range(self.depth):
                for f in (1, 2):
                    cast(self.w_raw["ffn%d_w_gate" % f][l], self.wb[("g", f, l)], D, DFF, 256)
                    cast(self.w_raw["ffn%d_w_up" % f][l], self.wb[("u", f, l)], D, DFF, 256)
                    cast(self.w_raw["ffn%d_w_down" % f][l], self.wb[("d", f, l)], DFF, D, 128)
                j = l // 2
                if l % 2 == 0:
                    cast(self.w_raw["s5_glu_w"][j], self.wb[("glu", j)], 512, 512, 128)
                    cast(self.w_raw["ab_w_in"][j], self.wb[("in", l)], D, AB_IN, 128)
                    cast(self.w_raw["ab_w_out"][j], self.wb[("out", l)], D, D, 128)
                else:
                    cast(self.w_raw["cd_w_in"][j], self.wb[("in", l)], D, CD_IN, 128)
                    cast(self.w_raw["cd_w_out"][j], self.wb[("out", l)], D, D, 128)

    def trunk(self, xin, yout, L, tag):
        for l in range(self.depth + 1):
            self.dense_pass(xin, yout, L, l)
            self.P.barrier()
            if l < self.depth:
                self.mixer_pass(L, l)
                self.P.barrier()

    def zero_rows(self, L, r0, r1):
        nc, P = self.nc, self.P
        with ExitStack() as s2:
            z = self.sbt(s2, "mz", [128, 2048], BF16)
            P.op("dve", lambda e: e.memset(z[:], 0.0), writes=["mz"])
            for rc in range(r0 // 128, r1 // 128):
                for t0 in range(0, L, 2048):
                    w = min(2048, L - t0)
                    P.op("pool", lambda e: e.dma_start(out=self.YM[rc * 128:(rc + 1) * 128, t0:t0 + w], in_=z[:, 0:w]),
                         reads=["mz"], dma=True)
            P.barrier()

    def mixer_pass(self, L, l):
        j = l // 2
        if l % 2 == 0:
            if self.enable.get("s5"):
                self.s5(L, j)
            else:
                self.zero_rows(L, 0, 512)
            self.P.barrier()
            if self.enable.get("rwkv"):
                self.rwkv(L, j)
            else:
                self.zero_rows(L, 512, 1024)
        else:
            if self.enable.get("lru"):
                self.lru(L, j)
            else:
                self.zero_rows(L, 0, 512)
            self.P.barrier()
            if self.enable.get("hyena"):
                self.hyena(L, j)
            else:
                self.zero_rows(L, 512, 1024)

    def lru(self, L, j):
        nc, P = self.nc, self.P
        TB = min(L, TBMAX)
        nb = L // TB
        with ExitStack() as s2:
            def sb(name, shape, dt=F32):
                return self.sbt(s2, name, shape, dt)
            cols = sb("l_cols", [128, 4, 11])
            cs = sb("l_cs", [128, 4, 2])
            wbd = sb("l_wbd", [128, 4, 128])
            HS = sb("l_HS", [128, L])
            xpad = sb("l_xpad", [128, TB + 3])
            xc = sb("l_xc", [128, TB])
            gr = sb("l_gr", [128, TB])
            gi = sb("l_gi", [128, TB])
            a_ = sb("l_a", [128, TB])
            om = sb("l_om", [128, TB])
            hR = Rot([sb("l_h%d" % i, [128, TB]) for i in range(2)], "l_h")
            gb = sb("l_gb", [128, TB])
            yo = sb("l_yo", [128, TB], BF16)
            psA = Rot(self.psum[0:4], "psA")
            P.op("sp", lambda e: e.dma_start(out=cols[:], in_=self.lru_cols[j]), writes=["l_cols"], dma=True)
            for d in range(2):
                P.op("act", lambda e: e.activation(out=cs[:, :, d], in_=cols[:, :, 5 + 3 * d], func=AF.Exp, scale=-1.0),
                     reads=["l_cols"], writes=["l_cs"])
                P.op("act", lambda e: e.activation(out=cs[:, :, d], in_=cs[:, :, d], func=AF.Ln, bias=1.0),
                     reads=["l_cs"], writes=["l_cs"])
            P.op("dve", lambda e: e.tensor_scalar(out=cs[:].rearrange("p a b -> p (a b)"), in0=cs[:].rearrange("p a b -> p (a b)"),
                                                  scalar1=-8.0, scalar2=None, op0=ALU.mult), reads=["l_cs"], writes=["l_cs"])
            for ct in range(4):
                P.op("sp", lambda e: e.dma_start(out=wbd[:], in_=self.lru_wbd[j, ct]), writes=["l_wbd"], dma=True)
                for d in range(2):
                    carry = 0.0
                    ckey = None
                    blocks = list(range(nb)) if d == 0 else list(range(nb - 1, -1, -1))
                    for bi in blocks:
                        t0 = bi * TB
                        lo = max(t0 - 2, 0)
                        hi = min(t0 + TB + 1, L)
                        if t0 == 0:
                            P.op("pool", lambda e: e.memset(xpad[:, 0:2], 0.0), writes=["l_xpad"])
                        if t0 + TB == L:
                            P.op("pool", lambda e: e.memset(xpad[:, TB + 2:TB + 3], 0.0), writes=["l_xpad"])
                        P.op("sp", lambda e: e.dma_start(out=xpad[:, lo - (t0 - 2):hi - (t0 - 2)],
                                                         in_=self.PJ[ct * 128:(ct + 1) * 128, lo:hi]), writes=["l_xpad"], dma=True)
                        P.op("dve", lambda e: e.tensor_scalar(out=xc[:], in0=xpad[:, 0:TB], scalar1=cols[:, ct, 0:1],
                                                              scalar2=cols[:, ct, 4:5], op0=ALU.mult, op1=ALU.add),
                             reads=["l_xpad", "l_cols"], writes=["l_xc"])
                        for q in range(1, 4):
                            P.op("dve", lambda e: e.scalar_tensor_tensor(out=xc[:], in0=xpad[:, q:q + TB], scalar=cols[:, ct, q:q + 1],
                                                                         op0=ALU.mult, in1=xc[:], op1=ALU.add),
                                 reads=["l_xpad", "l_xc"], writes=["l_xc"])
                        for sbk in range(TB // 512):
                            sl = slice(sbk * 512, (sbk + 1) * 512)
                            for which, dst, bcol, key in ((0, gr, 6 + 3 * d, "l_gr"), (1, gi, 7 + 3 * d, "l_gi")):
                                pa, kpa = psA.next()
                                P.op("pe", lambda e: e.matmul(pa[:], lhsT=wbd[:, 2 * d + which, :], rhs=xc[:, sl], start=True, stop=True),
                                     reads=["l_wbd", "l_xc"], writes=[kpa])
                                P.op("act", lambda e: e.activation(out=dst[:, sl], in_=pa[:], func=AF.Sigmoid,
                                                                   bias=cols[:, ct, bcol:bcol + 1]),
                                     reads=[kpa, "l_cols"], writes=[key])
                        P.op("act", lambda e: e.activation(out=a_[:], in_=gr[:], func=AF.Exp, scale=cs[:, ct, d:d + 1]),
                             reads=["l_gr", "l_cs"], writes=["l_a"])
                        P.op("pool", lambda e: e.tensor_tensor(out=om[:], in0=a_[:], in1=a_[:], op=ALU.mult), reads=["l_a"], writes=["l_om"])
                        P.op("pool", lambda e: e.tensor_scalar(out=om[:], in0=om[:], scalar1=-1.0, scalar2=1.0, op0=ALU.mult, op1=ALU.add),
                             reads=["l_om"], writes=["l_om"])
                        P.op("pool", lambda e: e.tensor_scalar(out=om[:], in0=om[:], scalar1=1e-30, scalar2=None, op0=ALU.max),
                             reads=["l_om"], writes=["l_om"])
                        P.op("act", lambda e: e.activation(out=om[:], in_=om[:], func=AF.Sqrt), reads=["l_om"], writes=["l_om"])
                        P.op("dve", lambda e: e.tensor_tensor(out=om[:], in0=om[:], in1=gi[:], op=ALU.mult), reads=["l_om", "l_gi"], writes=["l_om"])
                        P.op("dve", lambda e: e.tensor_tensor(out=om[:], in0=om[:], in1=xc[:], op=ALU.mult), reads=["l_om", "l_xc"], writes=["l_om"])
                        rk = ["l_a", "l_om"] + ([ckey] if ckey else [])
                        if d == 0:
                            P.op("dve", lambda e: e.tensor_tensor_scan(out=HS[:, t0:t0 + TB], data0=a_[:], data1=om[:], initial=carry,
                                                                       op0=ALU.mult, op1=ALU.add), reads=rk, writes=[("l_HS", bi)])
                            carry = HS[:, t0 + TB - 1:t0 + TB]
                            ckey = ("l_HS", bi)
                        else:
                            h, kh = hR.next()
                            P.op("dve", lambda e: e.tensor_tensor_scan(out=h[:, ::-1], data0=a_[:, ::-1], data1=om[:, ::-1], initial=carry,
                                                                       op0=ALU.mult, op1=ALU.add), reads=rk, writes=[kh])
                            carry = h[:, 0:1]
                            ckey = kh
                            P.op("sp", lambda e: e.dma_start(out=gb[:], in_=self.PJ[512 + ct * 128:512 + (ct + 1) * 128, t0:t0 + TB]),
                                 writes=["l_gb"], dma=True)
                            P.op("act", lambda e: e.activation(out=gb[:], in_=gb[:], func=AF.Gelu_apprx_tanh), reads=["l_gb"], writes=["l_gb"])
                            P.op("pool", lambda e: e.tensor_tensor(out=gr[:], in0=h[:], in1=HS[:, t0:t0 + TB], op=ALU.add),
                                 reads=[kh, ("l_HS", bi)], writes=["l_gr"])
                            P.op("pool", lambda e: e.tensor_tensor(out=yo[:], in0=gr[:], in1=gb[:], op=ALU.mult),
                                 reads=["l_gr", "l_gb"], writes=["l_yo"])
                            P.op("pool", lambda e: e.dma_start(out=self.YM[ct * 128:(ct + 1) * 128, t0:t0 + TB], in_=yo[:]),
                                 reads=["l_yo"], dma=True)

    def _range_reduce(self, t, tmp, key, tkey, n):
        P = self.P
        P.op("dve", lambda e: e.tensor_scalar(out=tmp, in0=t, scalar1=1.0 / TWO_PI, scalar2=MAGIC, op0=ALU.mult, op1=ALU.add),
             reads=[key], writes=[tkey])
        P.op("dve", lambda e: e.tensor_scalar(out=tmp, in0=tmp, scalar1=MAGIC, scalar2=-TWO_PI, op0=ALU.subtract, op1=ALU.mult),
             reads=[tkey], writes=[tkey])
        P.op("dve", lambda e: e.tensor_tensor(out=t, in0=t, in1=tmp, op=ALU.add), reads=[key, tkey], writes=[key])
        P.op("dve", lambda e: e.tensor_scalar(out=t, in0=t, scalar1=3.14159, scalar2=-3.14159, op0=ALU.min, op1=ALU.max),
             reads=[key], writes=[key])

    def s5(self, L, j):
        nc, P = self.nc, self.P
        TB = min(L, TBMAX)
        nb = L // TB
        YS = self.YS5
        with ExitStack() as s2:
            def sb(name, shape, dt=F32):
                return self.sbt(s2, name, shape, dt)
            NC = 32
            prm = sb("s_prm", [128, NC, 3])
            lr = sb("s_lr", [128, NC]); li = sb("s_li", [128, NC]); stp = sb("s_stp", [128, NC])
            mag = sb("s_mag", [128, NC]); ang = sb("s_ang", [128, NC]); angc = sb("s_angc", [128, NC]); tmpc = sb("s_tmpc", [128, NC])
            cth = sb("s_cth", [128, NC]); sth = sb("s_sth", [128, NC]); nsth = sb("s_nsth", [128, NC])
            are = sb("s_are", [128, NC]); aim = sb("s_aim", [128, NC]); den = sb("s_den", [128, NC])
            cre = sb("s_cre", [128, NC]); cim = sb("s_cim", [128, NC]); t1 = sb("s_t1", [128, NC]); t2 = sb("s_t2", [128, NC])
            ones = sb("s_ones", [128, TB])
            ucm = sb("s_ucm", [128, L])
            bt = sb("s_bt", [128, 2, 2, 128])
            ctl = sb("s_ct", [128, 2, 32])
            cpr = sb("s_cpr", [128, 2, 32])
            Ere = sb("s_Ere", [128, TB]); Eim = sb("s_Eim", [128, TB]); rt = sb("s_rt", [128, TB])
            xre = sb("s_xre", [128, TB]); xim = sb("s_xim", [128, TB])
            gre = sb("s_gre", [128, TB]); gim = sb("s_gim", [128, TB])
            ta = sb("s_ta", [128, TB]); tb = sb("s_tb", [128, TB])
            cw = sb("s_cw", [128, 2]); cw2 = sb("s_cw2", [128, 2]); ini = sb("s_ini", [128, 2]); tin = sb("s_tin", [128, 2])
            yev = Rot([sb("s_yev%d" % i, [32, 512]) for i in range(2)], "s_yev")
            psW = Rot(self.psum[0:4], "psW")
            psY = Rot(self.psum[4:6], "psY")
            P.op("sp", lambda e: e.dma_start(out=prm[:], in_=self.s5_prm[j]), writes=["s_prm"], dma=True)
            P.op("dve", lambda e: e.memset(ones[:], 1.0), writes=["s_ones"])
            P.op("dve", lambda e: e.tensor_scalar(out=lr[:], in0=prm[:, :, 0], scalar1=-1e-4, scalar2=None, op0=ALU.min), reads=["s_prm"], writes=["s_lr"])
            P.op("dve", lambda e: e.tensor_copy(out=li[:], in_=prm[:, :, 1]), reads=["s_prm"], writes=["s_li"])
            P.op("act", lambda e: e.activation(out=stp[:], in_=prm[:, :, 2], func=AF.Exp), reads=["s_prm"], writes=["s_stp"])
            P.op("dve", lambda e: e.tensor_tensor(out=mag[:], in0=lr[:], in1=stp[:], op=ALU.mult), reads=["s_lr", "s_stp"], writes=["s_mag"])
            P.op("act", lambda e: e.activation(out=mag[:], in_=mag[:], func=AF.Exp), reads=["s_mag"], writes=["s_mag"])
            P.op("dve", lambda e: e.tensor_tensor(out=ang[:], in0=li[:], in1=stp[:], op=ALU.mult), reads=["s_li", "s_stp"], writes=["s_ang"])
            P.op("dve", lambda e: e.tensor_scalar(out=angc[:], in0=ang[:], scalar1=0.5 * math.pi, scalar2=None, op0=ALU.add), reads=["s_ang"], writes=["s_angc"])
            self._range_reduce(ang[:], tmpc[:], "s_ang", "s_tmpc", NC)
            self._range_reduce(angc[:], tmpc[:], "s_angc", "s_tmpc", NC)
            P.op("act", lambda e: e.activation(out=sth[:], in_=ang[:], func=AF.Sin), reads=["s_ang"], writes=["s_sth"])
            P.op("act", lambda e: e.activation(out=cth[:], in_=angc[:], func=AF.Sin), reads=["s_angc"], writes=["s_cth"])
            P.op("dve", lambda e: e.tensor_scalar(out=nsth[:], in0=sth[:], scalar1=-1.0, scalar2=None, op0=ALU.mult), reads=["s_sth"], writes=["s_nsth"])
            P.op("dve", lambda e: e.tensor_tensor(out=are[:], in0=mag[:], in1=cth[:], op=ALU.mult), reads=["s_mag", "s_cth"], writes=["s_are"])
            P.op("dve", lambda e: e.tensor_tensor(out=aim[:], in0=mag[:], in1=sth[:], op=ALU.mult), reads=["s_mag", "s_sth"], writes=["s_aim"])
            P.op("dve", lambda e: e.tensor_tensor(out=den[:], in0=lr[:], in1=lr[:], op=ALU.mult), reads=["s_lr"], writes=["s_den"])
            P.op("dve", lambda e: e.tensor_tensor(out=t1[:], in0=li[:], in1=li[:], op=ALU.mult), reads=["s_li"], writes=["s_t1"])
            P.op("dve", lambda e: e.tensor_tensor(out=den[:], in0=den[:], in1=t1[:], op=ALU.add), reads=["s_den", "s_t1"], writes=["s_den"])
            P.op("dve", lambda e: e.reciprocal(out=den[:], in_=den[:]), reads=["s_den"], writes=["s_den"])
            P.op("dve", lambda e: e.tensor_scalar(out=are[:], in0=are[:], scalar1=-1.0, scalar2=None, op0=ALU.add), reads=["s_are"], writes=["s_are"])
            P.op("dve", lambda e: e.tensor_tensor(out=t1[:], in0=are[:], in1=lr[:], op=ALU.mult), reads=["s_are", "s_lr"], writes=["s_t1"])
            P.op("dve", lambda e: e.tensor_tensor(out=t2[:], in0=aim[:], in1=li[:], op=ALU.mult), reads=["s_aim", "s_li"], writes=["s_t2"])
            P.op("dve", lambda e: e.tensor_tensor(out=t1[:], in0=t1[:], in1=t2[:], op=ALU.add), reads=["s_t1", "s_t2"], writes=["s_t1"])
            P.op("dve", lambda e: e.tensor_tensor(out=cre[:], in0=t1[:], in1=den[:], op=ALU.mult), reads=["s_t1", "s_den"], writes=["s_cre"])
            P.op("dve", lambda e: e.tensor_tensor(out=t1[:], in0=aim[:], in1=lr[:], op=ALU.mult), reads=["s_aim", "s_lr"], writes=["s_t1"])
            P.op("dve", lambda e: e.tensor_tensor(out=t2[:], in0=are[:], in1=li[:], op=ALU.mult), reads=["s_are", "s_li"], writes=["s_t2"])
            P.op("dve", lambda e: e.tensor_tensor(out=t1[:], in0=t1[:], in1=t2[:], op=ALU.subtract), reads=["s_t1", "s_t2"], writes=["s_t1"])
            P.op("dve", lambda e: e.tensor_tensor(out=cim[:], in0=t1[:], in1=den[:], op=ALU.mult), reads=["s_t1", "s_den"], writes=["s_cim"])

            for ut in range(4):
                P.op("sp", lambda e: e.dma_start(out=ucm[:], in_=self.PJ[ut * 128:(ut + 1) * 128, 0:L]), writes=["s_ucm"], dma=True)
                for gp in range(4):
                    T = ut * 4 + gp
                    rows = slice(0, 128)
                    P.op("sp", lambda e: e.dma_start(out=bt[:], in_=self.s5_bt[j, T]), writes=["s_bt"], dma=True)
                    for d in range(2):
                        ci = T * 2 + d
                        cc = slice(ci, ci + 1)
                        P.op("sp", lambda e: e.dma_start(out=ctl[:], in_=self.s5_ct[j, T, d]), writes=["s_ct"], dma=True)
                        P.op("dve", lambda e: e.tensor_scalar(out=cpr[:, 0, :], in0=ctl[:, 0, :], scalar1=cre[:, cc], scalar2=None, op0=ALU.mult),
                             reads=["s_ct", "s_cre"], writes=["s_cpr"])
                        P.op("dve", lambda e: e.scalar_tensor_tensor(out=cpr[:, 0, :], in0=ctl[:, 1, :], scalar=cim[:, cc], op0=ALU.mult,
                                                                     in1=cpr[:, 0, :], op1=ALU.subtract),
                             reads=["s_ct", "s_cim", "s_cpr"], writes=["s_cpr"])
                        P.op("dve", lambda e: e.tensor_scalar(out=cpr[:, 0, :], in0=cpr[:, 0, :], scalar1=-1.0, scalar2=None, op0=ALU.mult),
                             reads=["s_cpr"], writes=["s_cpr"])
                        P.op("dve", lambda e: e.tensor_scalar(out=cpr[:, 1, :], in0=ctl[:, 0, :], scalar1=cim[:, cc], scalar2=None, op0=ALU.mult),
                             reads=["s_ct", "s_cim"], writes=["s_cpr1"])
                        P.op("dve", lambda e: e.scalar_tensor_tensor(out=cpr[:, 1, :], in0=ctl[:, 1, :], scalar=cre[:, cc], op0=ALU.mult,
                                                                     in1=cpr[:, 1, :], op1=ALU.add),
                             reads=["s_ct", "s_cre", "s_cpr1"], writes=["s_cpr1"])
                        P.op("dve", lambda e: e.tensor_scalar(out=cpr[:, 1, :], in0=cpr[:, 1, :], scalar1=-1.0, scalar2=None, op0=ALU.mult),
                             reads=["s_cpr1"], writes=["s_cpr1"])
                        P.op("dve", lambda e: e.tensor_scalar(out=rt[:], in0=ones[:], scalar1=mag[:, cc], scalar2=None, op0=ALU.mult),
                             reads=["s_ones", "s_mag"], writes=["s_rt"])
                        P.op("dve", lambda e: e.memset(Ere[:, 0:1], 1.0), writes=["s_E"])
                        P.op("dve", lambda e: e.memset(Eim[:, 0:1], 0.0), writes=["s_E"])
                        P.op("dve", lambda e: e.tensor_copy(out=cw[:, 0:1], in_=cth[:, cc]), reads=["s_cth"], writes=["s_cw"])
                        P.op("dve", lambda e: e.tensor_copy(out=cw[:, 1:2], in_=nsth[:, cc]), reads=["s_nsth"], writes=["s_cw"])
                        m = 1
                        while m < TB:
                            P.op("dve", lambda e: e.tensor_scalar(out=ta[:, 0:m], in0=Eim[:, 0:m], scalar1=cw[:, 1:2], scalar2=None, op0=ALU.mult),
                                 reads=["s_E", "s_cw"], writes=["s_ta"])
                            P.op("dve", lambda e: e.scalar_tensor_tensor(out=Ere[:, m:2 * m], in0=Ere[:, 0:m], scalar=cw[:, 0:1], op0=ALU.mult,
                                                                         in1=ta[:, 0:m], op1=ALU.subtract),
                                 reads=["s_E", "s_cw", "s_ta"], writes=["s_E"])
                            P.op("dve", lambda e: e.tensor_scalar(out=tb[:, 0:m], in0=Ere[:, 0:m], scalar1=cw[:, 1:2], scalar2=None, op0=ALU.mult),
                                 reads=["s_E", "s_cw"], writes=["s_tb"])
                            P.op("dve", lambda e: e.scalar_tensor_tensor(out=Eim[:, m:2 * m], in0=Eim[:, 0:m], scalar=cw[:, 0:1], op0=ALU.mult,
                                                                         in1=tb[:, 0:m], op1=ALU.add),
                                 reads=["s_E", "s_cw", "s_tb"], writes=["s_E"])
                            m *= 2
                            if m < TB:
                                P.op("dve", lambda e: e.tensor_tensor(out=cw2[:, 0:1], in0=cw[:, 1:2], in1=cw[:, 1:2], op=ALU.mult), reads=["s_cw"], writes=["s_cw2"])
                                P.op("dve", lambda e: e.tensor_tensor(out=cw2[:, 1:2], in0=cw[:, 0:1], in1=cw[:, 1:2], op=ALU.mult), reads=["s_cw"], writes=["s_cw2"])
                                P.op("dve", lambda e: e.scalar_tensor_tensor(out=cw[:, 0:1], in0=cw[:, 0:1], scalar=cw[:, 0:1], op0=ALU.mult,
                                                                             in1=cw2[:, 0:1], op1=ALU.subtract), reads=["s_cw", "s_cw2"], writes=["s_cw"])
                                P.op("dve", lambda e: e.tensor_scalar(out=cw[:, 1:2], in0=cw2[:, 1:2], scalar1=2.0, scalar2=None, op0=ALU.mult),
                                     reads=["s_cw2"], writes=["s_cw"])
                        rv = (lambda ap: ap[:, ::-1]) if d == 1 else (lambda ap: ap[:])
                        blocks = list(range(nb)) if d == 0 else list(range(nb - 1, -1, -1))
                        first_blk = True
                        for bi in blocks:
                            t0 = bi * TB
                            for sbk in range(TB // 512):
                                c0 = sbk * 512
                                sl = slice(c0, c0 + 512)
                                if d == 0:
                                    Er, Ei = Ere[:, sl], Eim[:, sl]
                                else:
                                    Er, Ei = Ere[:, TB - c0 - 512:TB - c0][:, ::-1], Eim[:, TB - c0 - 512:TB - c0][:, ::-1]
                                pr, kpr = psW.next()
                                pi_, kpi = psW.next()
                                P.op("pe", lambda e: e.matmul(pr[:], lhsT=bt[rows, d, 0, :], rhs=ucm[rows, t0 + c0:t0 + c0 + 512], start=True, stop=True),
                                     reads=["s_bt", "s_ucm"], writes=[kpr])
                                P.op("pe", lambda e: e.matmul(pi_[:], lhsT=bt[rows, d, 1, :], rhs=ucm[rows, t0 + c0:t0 + c0 + 512], start=True, stop=True),
                                     reads=["s_bt", "s_ucm"], writes=[kpi])
                                P.op("dve", lambda e: e.tensor_tensor(out=xre[:, sl], in0=pr[:], in1=Er, op=ALU.mult), reads=[kpr, "s_E"], writes=["s_xre"])
                                P.op("dve", lambda e: e.tensor_tensor(out=ta[:, sl], in0=pi_[:], in1=Ei, op=ALU.mult), reads=[kpi, "s_E"], writes=["s_ta"])
                                P.op("dve", lambda e: e.tensor_tensor(out=xre[:, sl], in0=xre[:, sl], in1=ta[:, sl], op=ALU.subtract),
                                     reads=["s_xre", "s_ta"], writes=["s_xre"])
                                P.op("dve", lambda e: e.tensor_tensor(out=xim[:, sl], in0=pr[:], in1=Ei, op=ALU.mult), reads=[kpr, "s_E"], writes=["s_xim"])
                                P.op("dve", lambda e: e.tensor_tensor(out=tb[:, sl], in0=pi_[:], in1=Er, op=ALU.mult), reads=[kpi, "s_E"], writes=["s_tb"])
                                P.op("dve", lambda e: e.tensor_tensor(out=xim[:, sl], in0=xim[:, sl], in1=tb[:, sl], op=ALU.add),
                                     reads=["s_xim", "s_tb"], writes=["s_xim"])
                            if first_blk:
                                i_re, i_im = 0.0, 0.0
                            else:
                                P.op("dve", lambda e: e.tensor_scalar(out=ini[:, 0:1], in0=tin[:, 1:2], scalar1=sth[:, cc], scalar2=None, op0=ALU.mult),
                                     reads=["s_tin", "s_sth"], writes=["s_ini"])
                                P.op("dve", lambda e: e.scalar_tensor_tensor(out=ini[:, 0:1], in0=tin[:, 0:1], scalar=cth[:, cc], op0=ALU.mult,
                                                                             in1=ini[:, 0:1], op1=ALU.subtract),
                                     reads=["s_tin", "s_cth", "s_ini"], writes=["s_ini"])
                                P.op("dve", lambda e: e.tensor_scalar(out=ini[:, 1:2], in0=tin[:, 0:1], scalar1=sth[:, cc], scalar2=None, op0=ALU.mult),
                                     reads=["s_tin", "s_sth"], writes=["s_ini"])
                                P.op("dve", lambda e: e.scalar_tensor_tensor(out=ini[:, 1:2], in0=tin[:, 1:2], scalar=cth[:, cc], op0=ALU.mult,
                                                                             in1=ini[:, 1:2], op1=ALU.add),
                                     reads=["s_tin", "s_cth", "s_ini"], writes=["s_ini"])
                                i_re, i_im = ini[:, 0:1], ini[:, 1:2]
                            first_blk = False
                            P.op("dve", lambda e: e.tensor_tensor_scan(out=rv(gre), data0=rv(rt), data1=rv(xre), initial=i_re, op0=ALU.mult, op1=ALU.add),
                                 reads=["s_rt", "s_xre", "s_ini"], writes=["s_gre"])
                            P.op("dve", lambda e: e.tensor_tensor_scan(out=rv(gim), data0=rv(rt), data1=rv(xim), initial=i_im, op0=ALU.mult, op1=ALU.add),
                                 reads=["s_rt", "s_xim", "s_ini"], writes=["s_gim"])
                            Erf = Ere[:, ::-1] if d == 1 else Ere[:]
                            Eif = Eim[:, ::-1] if d == 1 else Eim[:]
                            P.op("dve", lambda e: e.tensor_tensor(out=xre[:], in0=gre[:], in1=Erf, op=ALU.mult), reads=["s_gre", "s_E"], writes=["s_xre"])
                            P.op("dve", lambda e: e.tensor_tensor(out=ta[:], in0=gim[:], in1=Eif, op=ALU.mult), reads=["s_gim", "s_E"], writes=["s_ta"])
                            P.op("dve", lambda e: e.tensor_tensor(out=xre[:], in0=xre[:], in1=ta[:], op=ALU.add), reads=["s_xre", "s_ta"], writes=["s_xre"])
                            P.op("dve", lambda e: e.tensor_tensor(out=xim[:], in0=gim[:], in1=Erf, op=ALU.mult), reads=["s_gim", "s_E"], writes=["s_xim"])
                            P.op("dve", lambda e: e.tensor_tensor(out=tb[:], in0=gre[:], in1=Eif, op=ALU.mult), reads=["s_gre", "s_E"], writes=["s_tb"])
                            P.op("dve", lambda e: e.tensor_tensor(out=xim[:], in0=xim[:], in1=tb[:], op=ALU.subtract), reads=["s_xim", "s_tb"], writes=["s_xim"])
                            edge = 0 if d == 1 else TB - 1
                            P.op("dve", lambda e: e.tensor_copy(out=tin[:, 0:1], in_=xre[:, edge:edge + 1]), reads=["s_xre"], writes=["s_tin"])
                            P.op("dve", lambda e: e.tensor_copy(out=tin[:, 1:2], in_=xim[:, edge:edge + 1]), reads=["s_xim"], writes=["s_tin"])
                            for sbk in range(TB // 512):
                                sl = slice(sbk * 512, (sbk + 1) * 512)
                                py, kpy = psY.next()
                                P.op("pe", lambda e: e.matmul(py[0:32, :], lhsT=cpr[:, 0, :], rhs=xre[:, sl], start=True, stop=False),
                                     reads=["s_cpr", "s_cpr1", "s_xre"], writes=[kpy])
                                P.op("pe", lambda e: e.matmul(py[0:32, :], lhsT=cpr[:, 1, :], rhs=xim[:, sl], start=False, stop=True),
                                     reads=["s_cpr", "s_cpr1", "s_xim"], writes=[kpy])
                                yv, kyv = yev.next()
                                P.op("act", lambda e: e.copy(out=yv[:], in_=py[0:32, :]), reads=[kpy], writes=[kyv])
                                P.op("pool", lambda e: e.dma_start(out=YS[d, 32 * T:32 * T + 32, t0 + sbk * 512:t0 + (sbk + 1) * 512], in_=yv[:]),
                                     reads=[kyv], dma=True)
            P.barrier()
        with ExitStack() as s2:
            def sb(name, shape, dt=F32):
                return self.sbt(s2, name, shape, dt)
            pc = sb("s_pc", [128, 4, 2])
            gw = sb("s_gw", [128, 4, 4, 128], BF16)
            yf = sb("s_yf", [128, 4, 512]); yb = sb("s_yb", [128, 4, 512]); uu = sb("s_uu", [128, 4, 512])
            yg = sb("s_yg", [128, 4, 512], BF16)
            sgm = Rot([sb("s_sg%d" % i, [128, 512]) for i in range(2)], "s_sg")
            yo = sb("s_yo", [128, 4, 512], BF16)
            psA = Rot(self.psum[0:4], "psA")
            P.op("sp", lambda e: e.dma_start(out=pc[:], in_=self.s5_pc[j]), writes=["s_pc"], dma=True)
            P.op("sp", lambda e: e.dma_start(out=gw[:], in_=self.wb[("glu", j)].rearrange("jc p ic w -> p jc ic w")), writes=["s_gw"], dma=True)
            for t0 in range(0, L, 512):
                rowv = lambda ap: ap.rearrange("(c p) t -> p c t", p=128)
                P.op("sp", lambda e: e.dma_start(out=yf[:], in_=rowv(YS[0, :, t0:t0 + 512])), writes=["s_yf"], dma=True)
                P.op("sp", lambda e: e.dma_start(out=yb[:], in_=rowv(YS[1, :, t0:t0 + 512])), writes=["s_yb"], dma=True)
                P.op("sp", lambda e: e.dma_start(out=uu[:], in_=rowv(self.PJ[0:512, t0:t0 + 512])), writes=["s_uu"], dma=True)
                P.op("pool", lambda e: e.tensor_tensor(out=yf[:], in0=yf[:], in1=yb[:], op=ALU.add), reads=["s_yf", "s_yb"], writes=["s_yf"])
                for c in range(4):
                    P.op("dve", lambda e: e.scalar_tensor_tensor(out=yf[:, c, :], in0=uu[:, c, :], scalar=pc[:, c, 0:1], op0=ALU.mult,
                                                                 in1=yf[:, c, :], op1=ALU.add), reads=["s_uu", "s_yf", "s_pc"], writes=["s_yf"])
                P.op("act", lambda e: e.activation(out=yg[:].rearrange("p a b -> p (a b)"), in_=yf[:].rearrange("p a b -> p (a b)"),
                                                   func=AF.Gelu_apprx_tanh), reads=["s_yf"], writes=["s_yg"])
                for jc in range(4):
                    pa, kpa = psA.next()
                    for ic in range(4):
                        P.op("pe", lambda e: e.matmul(pa[:], lhsT=gw[:, jc, ic, :], rhs=yg[:, ic, :], start=(ic == 0), stop=(ic == 3)),
                             reads=["s_gw", "s_yg"], writes=[kpa])
                    sg_, ksg = sgm.next()
                    P.op("act", lambda e: e.activation(out=sg_[:], in_=pa[:], func=AF.Sigmoid, bias=pc[:, jc, 1:2]), reads=[kpa, "s_pc"], writes=[ksg])
                    P.op("dve", lambda e: e.tensor_tensor(out=yo[:, jc, :], in0=sg_[:], in1=yg[:, jc, :], op=ALU.mult), reads=[ksg, "s_yg"], writes=["s_yo"])
                P.op("pool", lambda e: e.dma_start(out=self.YM[0:512, t0:t0 + 512].rearrange("(c p) t -> p c t", p=128), in_=yo[:]),
                     reads=["s_yo"], dma=True)

    def rwkv(self, L, j):
        nc, P = self.nc, self.P
        TM = self.RWTM
        YT = self.YTM
        GC, BC = self.RWG, self.RWB
        with ExitStack() as s2:
            def sb(name, shape, dt=F32):
                return self.sbt(s2, name, shape, dt)
            mu = sb("r_mu", [128, 14]); omu = sb("r_omu", [128, 14]); hmu = sb("r_hmu", [128, 14])
            pc = sb("r_pc", [128, 4, 8]); omka = sb("r_omka", [128, 4]); nw0 = sb("r_nw0", [128, 4, 2])
            lw = sb("r_lw", [128, 2, 512])
            bones = sb("r_bones", [128, 128])
            ppad = Rot([sb("r_ppad%d" % i, [128, 514]) for i in range(3)], "r_ppad")
            nsum = Rot([sb("r_nsum%d" % i, [128, 512]) for i in range(2)], "r_nsum")
            rr = sb("r_r", [128, 4, 512]); kk_ = sb("r_k", [128, 4, 512]); vv = sb("r_v", [128, 4, 512])
            lo = sb("r_lo", [128, 512]); xg = sb("r_xg", [128, 512]); tw = sb("r_tw", [128, 512]); sgx = sb("r_sgx", [128, 512])
            aa = sb("r_a", [128, 4, 512]); gg = sb("r_g", [128, 4, 512]); wf = sb("r_wf", [128, 4, 512]); wbk = sb("r_wb", [128, 4, 512])
            An = sb("r_An", [128, 4, 512]); Bn = sb("r_Bn", [128, 4, 512]); k2 = sb("r_k2", [128, 4, 512]); bon = sb("r_bon", [128, 4, 512])
            t1 = sb("r_t1", [128, 512]); t2 = sb("r_t2", [128, 512]); t3 = sb("r_t3", [128, 512])
            tok = Rot([sb("r_tok%d" % i, [128, 512]) for i in range(3)], "r_tok")
            psA = Rot(self.psum[0:6], "psA")
            psT = Rot(self.psum[6:8], "psT")
            P.op("sp", lambda e: e.dma_start(out=mu[:], in_=self.rw_mu[j]), writes=["r_mu"], dma=True)
            P.op("sp", lambda e: e.dma_start(out=pc[:], in_=self.rw_pc[j]), writes=["r_pc"], dma=True)
            P.op("sp", lambda e: e.dma_start(out=lw[:], in_=self.rw_lw[j]), writes=["r_lw"], dma=True)
            P.op("dve", lambda e: e.tensor_scalar(out=omu[:], in0=mu[:], scalar1=-1.0, scalar2=1.0, op0=ALU.mult, op1=ALU.add), reads=["r_mu"], writes=["r_omu"])
            P.op("dve", lambda e: e.tensor_scalar(out=hmu[:], in0=mu[:], scalar1=0.5, scalar2=None, op0=ALU.mult), reads=["r_mu"], writes=["r_hmu"])
            P.op("dve", lambda e: e.tensor_scalar(out=omka[:], in0=pc[:, :, 4], scalar1=-1.0, scalar2=1.0, op0=ALU.mult, op1=ALU.add), reads=["r_pc"], writes=["r_omka"])
            P.op("dve", lambda e: e.tensor_scalar(out=nw0[:], in0=pc[:, :, 0:2], scalar1=-1.0, scalar2=None, op0=ALU.mult), reads=["r_pc"], writes=["r_nw0"])
            P.op("dve", lambda e: e.memset(bones[:], 0.0), writes=["r_bones"])
            P.op("dve", lambda e: e.memset(bones[0:64, 0:64], 1.0), writes=["r_bones"])
            P.op("dve", lambda e: e.memset(bones[64:128, 64:128], 1.0), writes=["r_bones"])
            row_tiles = [(512 + 128 * i, 128) for i in range(12)] + [(512 + 1536, 96), (512 + 1632, 64)]
            dests = [(rr, i) for i in range(4)] + [(kk_, i) for i in range(4)] + [(vv, i) for i in range(4)] + [(lo, None), (xg, None)]
            for t0 in range(0, L, 512):
                lo_t = max(t0 - 1, 0)
                hi_t = min(t0 + 513, L)
                for ti, ((r0, nr), (dst, di)) in enumerate(zip(row_tiles, dests)):
                    pp, kp = ppad.next()
                    if t0 == 0:
                        P.op("pool", lambda e: e.memset(pp[:, 0:1], 0.0), writes=[kp])
                    if t0 + 512 == L:
                        P.op("pool", lambda e: e.memset(pp[:, 513:514], 0.0), writes=[kp])
                    P.op("sp", lambda e: e.dma_start(out=pp[0:nr, lo_t - (t0 - 1):hi_t - (t0 - 1)], in_=self.PJ[r0:r0 + nr, lo_t:hi_t]),
                         writes=[kp], dma=True)
                    ns, kn = nsum.next()
                    P.op("pool", lambda e: e.tensor_tensor(out=ns[0:nr, :], in0=pp[0:nr, 0:512], in1=pp[0:nr, 2:514], op=ALU.add), reads=[kp], writes=[kn])
                    P.op("pool", lambda e: e.tensor_scalar(out=ns[0:nr, :], in0=ns[0:nr, :], scalar1=hmu[0:nr, ti:ti + 1], scalar2=None, op0=ALU.mult),
                         reads=[kn, "r_hmu"], writes=[kn])
                    dap = dst[0:nr, :] if di is None else dst[0:nr, di, :]
                    dkey = "r_dst%d" % ti
                    P.op("dve", lambda e: e.scalar_tensor_tensor(out=dap, in0=pp[0:nr, 1:513], scalar=omu[0:nr, ti:ti + 1], op0=ALU.mult,
                                                                 in1=ns[0:nr, :], op1=ALU.add), reads=[kp, kn, "r_omu"], writes=[dkey])
                allk = ["r_dst%d" % i for i in range(14)]
                P.op("act", lambda e: e.activation(out=tw[0:64, :], in_=lo[0:64, :], func=AF.Tanh), reads=allk, writes=["r_tw"])
                P.op("act", lambda e: e.activation(out=sgx[0:64, :], in_=xg[0:64, :], func=AF.Sigmoid), reads=allk, writes=["r_sgx"])
                for ct in range(4):
                    cs_ = slice(ct * 128, (ct + 1) * 128)
                    pa, kpa = psA.next()
                    P.op("pe", lambda e: e.matmul(pa[:], lhsT=lw[64:96, 0, cs_], rhs=lo[64:96, :], start=True, stop=True), reads=["r_lw"] + allk, writes=[kpa])
                    P.op("act", lambda e: e.activation(out=aa[:, ct, :], in_=pa[:], func=AF.Sigmoid, bias=pc[:, ct, 2:3]), reads=[kpa, "r_pc"], writes=[("r_a", ct)])
                    pa, kpa = psA.next()
                    P.op("pe", lambda e: e.matmul(pa[:], lhsT=lw[0:64, 1, cs_], rhs=sgx[0:64, :], start=True, stop=True), reads=["r_lw", "r_sgx"], writes=[kpa])
                    P.op("act", lambda e: e.copy(out=gg[:, ct, :], in_=pa[:]), reads=[kpa], writes=[("r_g", ct)])
                    for d, wdst in ((0, wf), (1, wbk)):
                        pa, kpa = psA.next()
                        P.op("pe", lambda e: e.matmul(pa[:], lhsT=lw[32 * d:32 * d + 32, 0, cs_], rhs=tw[32 * d:32 * d + 32, :], start=True, stop=True),
                             reads=["r_lw", "r_tw"], writes=[kpa])
                        wk = ("r_w", d, ct)
                        P.op("act", lambda e: e.activation(out=wdst[:, ct, :], in_=pa[:], func=AF.Exp, scale=-1.0, bias=nw0[:, ct, d:d + 1]), reads=[kpa, "r_nw0"], writes=[wk])
                        P.op("act", lambda e: e.activation(out=wdst[:, ct, :], in_=wdst[:, ct, :], func=AF.Ln, bias=1.0), reads=[wk], writes=[wk])
                        P.op("act", lambda e: e.activation(out=wdst[:, ct, :], in_=wdst[:, ct, :], func=AF.Exp, scale=-1.0, bias=-0.5), reads=[wk], writes=[wk])
                        P.op("act", lambda e: e.activation(out=wdst[:, ct, :], in_=wdst[:, ct, :], func=AF.Exp, scale=-1.0), reads=[wk], writes=[wk])
                    P.op("dve", lambda e: e.tensor_scalar(out=t1[:], in0=kk_[:, ct, :], scalar1=pc[:, ct, 3:4], scalar2=None, op0=ALU.mult), reads=allk + ["r_pc"], writes=["r_t1"])
                    P.op("pool", lambda e: e.tensor_tensor(out=t2[:], in0=t1[:], in1=t1[:], op=ALU.mult), reads=["r_t1"], writes=["r_t2"])
                    pa, kpa = psA.next()
                    P.op("pe", lambda e: e.matmul(pa[:], lhsT=bones[:], rhs=t2[:], start=True, stop=True), reads=["r_bones", "r_t2"], writes=[kpa])
                    P.op("act", lambda e: e.activation(out=t3[:], in_=pa[:], func=AF.Sqrt), reads=[kpa], writes=["r_t3"])
                    P.op("dve", lambda e: e.tensor_scalar(out=t3[:], in0=t3[:], scalar1=1e-12, scalar2=None, op0=ALU.max), reads=["r_t3"], writes=["r_t3"])
                    P.op("dve", lambda e: e.reciprocal(out=t3[:], in_=t3[:]), reads=["r_t3"], writes=["r_t3"])
                    P.op("dve", lambda e: e.tensor_tensor(out=t1[:], in0=t1[:], in1=t3[:], op=ALU.mult), reads=["r_t1", "r_t3"], writes=["r_t1"])
                    P.op("pool", lambda e: e.tensor_scalar(out=An[:, ct, :], in0=t1[:], scalar1=-1.0, scalar2=None, op0=ALU.mult), reads=["r_t1"], writes=[("r_An", ct)])
                    P.op("dve", lambda e: e.tensor_tensor(out=Bn[:, ct, :], in0=t1[:], in1=aa[:, ct, :], op=ALU.mult), reads=["r_t1", ("r_a", ct)], writes=[("r_Bn", ct)])
                    P.op("dve", lambda e: e.tensor_scalar(out=t2[:], in0=aa[:, ct, :], scalar1=pc[:, ct, 4:5], scalar2=omka[:, ct:ct + 1], op0=ALU.mult, op1=ALU.add),
                         reads=[("r_a", ct), "r_pc", "r_omka"], writes=["r_t2"])
                    P.op("dve", lambda e: e.tensor_tensor(out=k2[:, ct, :], in0=kk_[:, ct, :], in1=t2[:], op=ALU.mult), reads=allk + ["r_t2"], writes=[("r_k2", ct)])
                    P.op("pool", lambda e: e.tensor_tensor(out=t3[:], in0=rr[:, ct, :], in1=k2[:, ct, :], op=ALU.mult), reads=allk + [("r_k2", ct)], writes=["r_t3"])
                    P.op("dve", lambda e: e.tensor_scalar(out=t3[:], in0=t3[:], scalar1=pc[:, ct, 5:6], scalar2=None, op0=ALU.mult), reads=["r_t3", "r_pc"], writes=["r_t3"])
                    pa, kpa = psA.next()
                    P.op("pe", lambda e: e.matmul(pa[:], lhsT=bones[:], rhs=t3[:], start=True, stop=True), reads=["r_bones", "r_t3"], writes=[kpa])
                    P.op("dve", lambda e: e.tensor_tensor(out=bon[:, ct, :], in0=pa[:], in1=vv[:, ct, :], op=ALU.mult), reads=[kpa] + allk, writes=[("r_bon", ct)])
                cm = lambda ap: ap.rearrange("(c p) t -> p c t", p=128)
                P.op("pool", lambda e: e.dma_start(out=cm(GC[:, t0:t0 + 512]), in_=gg[:]), reads=[("r_g", c) for c in range(4)], dma=True)
                P.op("pool", lambda e: e.dma_start(out=cm(BC[:, t0:t0 + 512]), in_=bon[:]), reads=[("r_bon", c) for c in range(4)], dma=True)
                srcs = [(rr, allk), (k2, [("r_k2", c) for c in range(4)]), (vv, allk), (An, [("r_An", c) for c in range(4)]),
                        (Bn, [("r_Bn", c) for c in range(4)]), (wf, [("r_w", 0, c) for c in range(4)]), (wbk, [("r_w", 1, c) for c in range(4)])]
                for ai, (src, skeys) in enumerate(srcs):
                    for sbk in range(4):
                        pt, kpt = psT.next()
                        for ct in range(4):
                            P.op("pe", lambda e: e.matmul(pt[:, ct * 128:(ct + 1) * 128], lhsT=src[:, ct, sbk * 128:(sbk + 1) * 128], rhs=self.ident[:],
                                                          start=True, stop=True), reads=skeys + ["ident"], writes=[kpt])
                        tk, ktk = tok.next()
                        P.op("act", lambda e: e.copy(out=tk[:], in_=pt[:]), reads=[kpt], writes=[ktk])
                        P.op("pool", lambda e: e.dma_start(out=TM[ai, t0 + sbk * 128:t0 + (sbk + 1) * 128, :], in_=tk[:]), reads=[ktk], dma=True)
            P.barrier()
        NS = 16
        with ExitStack() as s2:
            def sb(name, shape, dt=F32):
                return self.sbt(s2, name, shape, dt)
            rep = sb("q_rep", [16, 128]); iot = sb("q_iot", [16, 128], I32); iof = sb("q_iof", [16, 128]); iog = sb("q_iog", [16, 128])
            S = sb("q_S", [128, 8, 64]); tmp2 = sb("q_tmp2", [128, 8, 64]); U = sb("q_U", [128, 8])
            tmpR = Rot([sb("q_tmp%d" % i, [128, 8, 64]) for i in range(4)], "q_tmp")
            pend = None
            cmpR = Rot([sb("q_cmp%d" % i, [16, 5, NS * 64]) for i in range(2)], "q_cmp")
            opR = Rot([sb("q_op%d" % i, [128, 5, NS, 64]) for i in range(2)], "q_op")
            vR = Rot([sb("q_v%d" % i, [128, NS, 8]) for i in range(2)], "q_v")
            yR = Rot([sb("q_y%d" % i, [128, NS, 8]) for i in range(2)], "q_y")
            vkR = Rot([sb("q_vk%d" % i, [128, 8, 64]) for i in range(4)], "q_vk")
            psR = Rot(self.psum[0:8], "psR")
            P.op("pool", lambda e: e.iota(iot[:], pattern=[[1, 128]], base=0, channel_multiplier=-8), writes=["q_iot"])
            P.op("dve", lambda e: e.tensor_copy(out=iof[:], in_=iot[:]), reads=["q_iot"], writes=["q_iof"])
            P.op("dve", lambda e: e.tensor_scalar(out=iog[:], in0=iof[:], scalar1=0.0, scalar2=None, op0=ALU.is_ge), reads=["q_iof"], writes=["q_iog"])
            P.op("dve", lambda e: e.tensor_scalar(out=iof[:], in0=iof[:], scalar1=8.0, scalar2=None, op0=ALU.is_lt), reads=["q_iof"], writes=["q_iof"])
            P.op("dve", lambda e: e.tensor_tensor(out=rep[:], in0=iof[:], in1=iog[:], op=ALU.mult), reads=["q_iof", "q_iog"], writes=["q_rep"])
            P.op("dve", lambda e: e.memset(S[:].rearrange("p a b -> p (a b)"), 0.0), writes=["q_S"])
            arr_of = [3, None, 4, 1, 0]
            for i0 in range(0, L, NS):
                cmp_, kc_ = cmpR.next()
                opt, ko = opR.next()
                vt, kv = vR.next()
                yt_, ky = yR.next()
                for d in range(2):
                    for oi in range(5):
                        ai = arr_of[oi] if oi != 1 else (5 + d)
                        if d == 0:
                            src = TM[ai, i0:i0 + NS, :]
                        else:
                            src = TM[ai, L - i0 - NS:L - i0, :][::-1, :]
                        P.op("sp", lambda e: e.dma_start(out=cmp_[8 * d:8 * d + 8, oi, :].rearrange("h (s k) -> h s k", k=64),
                                                         in_=src.rearrange("s (h k) -> h s k", k=64)), writes=[(kc_, d, oi)], dma=True)
                    if d == 0:
                        vsrc = TM[2, i0:i0 + NS, :]
                    else:
                        vsrc = TM[2, L - i0 - NS:L - i0, :][::-1, :]
                    P.op("act", lambda e: e.dma_start(out=vt[64 * d:64 * d + 64, :, :], in_=vsrc.rearrange("s (hv vi) -> hv s vi", vi=8)),
                         writes=[(kv, d)], dma=True)
                for oi in range(5):
                    for q4 in range(NS * 64 // 512):
                        pr, kpr = psR.next()
                        P.op("pe", lambda e: e.matmul(pr[:], lhsT=rep[:], rhs=cmp_[:, oi, q4 * 512:(q4 + 1) * 512], start=True, stop=True),
                             reads=["q_rep", (kc_, 0, oi), (kc_, 1, oi)], writes=[kpr])
                        P.op("act", lambda e: e.copy(out=opt[:, oi, q4 * 8:(q4 + 1) * 8, :].rearrange("p s k -> p (s k)"), in_=pr[:]),
                             reads=[kpr], writes=[(ko, oi)])
                opk = [(ko, oi) for oi in range(5)]
                for s_ in range(NS):
                    bc = lambda oi: opt[:, oi, s_, :].unsqueeze(1).broadcast_to([128, 8, 64])
                    vk, kvk = vkR.next()
                    for vi_ in range(8):
                        P.op("act", lambda e: e.activation(out=vk[:, vi_, :], in_=opt[:, 3, s_, :], func=AF.Copy, scale=vt[:, s_, vi_:vi_ + 1]),
                             reads=[(kv, 0), (kv, 1)] + opk, writes=[(kvk, vi_)])
                    kvk = [(kvk, vi_) for vi_ in range(8)]
                    tA, ktA = tmpR.next()
                    P.op("dve", lambda e: e.tensor_tensor(out=tA[:], in0=S[:], in1=bc(0), op=ALU.mult), reads=["q_S"] + opk, writes=[ktA])
                    if pend is not None:
                        pt_, pkt, pyt, pky, ps_ = pend
                        P.op("dve", lambda e: e.tensor_reduce(out=pyt[:, ps_, :], in_=pt_[:], axis=AX.X, op=ALU.add), reads=[pkt], writes=[pky])
                        pend = None
                    P.op("dve", lambda e: e.tensor_tensor(out=S[:], in0=S[:], in1=bc(1), op=ALU.mult), reads=["q_S"] + opk, writes=["q_S"])
                    P.op("dve", lambda e: e.tensor_reduce(out=U[:], in_=tA[:], axis=AX.X, op=ALU.add), reads=[ktA], writes=["q_U"])
                    P.op("dve", lambda e: e.tensor_tensor(out=S[:], in0=S[:], in1=vk[:], op=ALU.add), reads=["q_S"] + kvk, writes=["q_S"])
                    P.op("dve", lambda e: e.tensor_tensor(out=tmp2[:], in0=U[:].unsqueeze(2).broadcast_to([128, 8, 64]), in1=bc(2), op=ALU.mult),
                         reads=["q_U"] + opk, writes=["q_tmp2"])
                    P.op("dve", lambda e: e.tensor_tensor(out=S[:], in0=S[:], in1=tmp2[:], op=ALU.add), reads=["q_S", "q_tmp2"], writes=["q_S"])
                    tR, ktR = tmpR.next()
                    P.op("dve", lambda e: e.tensor_tensor(out=tR[:], in0=S[:], in1=bc(4), op=ALU.mult), reads=["q_S"] + opk, writes=[ktR])
                    pend = (tR, ktR, yt_, ky, s_)
                if pend is not None:
                    pt_, pkt, pyt, pky, ps_ = pend
                    P.op("dve", lambda e: e.tensor_reduce(out=pyt[:, ps_, :], in_=pt_[:], axis=AX.X, op=ALU.add), reads=[pkt], writes=[pky])
                    pend = None
                for d in range(2):
                    if d == 0:
                        dst = YT[0, i0:i0 + NS, :]
                    else:
                        dst = YT[1, L - i0 - NS:L - i0, :][::-1, :]
                    P.op("pool", lambda e: e.dma_start(out=dst.rearrange("s (hv vi) -> hv s vi", vi=8), in_=yt_[64 * d:64 * d + 64, :, :]),
                         reads=[ky], dma=True)
            P.barrier()
        with ExitStack() as s2:
            def sb(name, shape, dt=F32):
                return self.sbt(s2, name, shape, dt)
            pc = sb("z_pc", [128, 4, 8]); bones = sb("z_bones", [128, 128])
            ya = Rot([sb("z_ya%d" % i, [128, 512]) for i in range(2)], "z_ya")
            yb_ = Rot([sb("z_yb%d" % i, [128, 512]) for i in range(2)], "z_yb")
            ycm = sb("z_ycm", [128, 4, 512]); gt = sb("z_g", [128, 4, 512]); bt_ = sb("z_b", [128, 4, 512])
            mean = sb("z_mean", [128, 512]); cen = sb("z_cen", [128, 512]); sq = sb("z_sq", [128, 512]); rstd = sb("z_rstd", [128, 512])
            yo = sb("z_yo", [128, 4, 512], BF16)
            psT = Rot(self.psum[0:4], "psT")
            psA = Rot(self.psum[4:8], "psA")
            P.op("sp", lambda e: e.dma_start(out=pc[:], in_=self.rw_pc[j]), writes=["z_pc"], dma=True)
            P.op("dve", lambda e: e.memset(bones[:], 0.0), writes=["z_bones"])
            P.op("dve", lambda e: e.memset(bones[0:64, 0:64], 1.0 / 64.0), writes=["z_bones"])
            P.op("dve", lambda e: e.memset(bones[64:128, 64:128], 1.0 / 64.0), writes=["z_bones"])
            cm = lambda ap: ap.rearrange("(c p) t -> p c t", p=128)
            for t0 in range(0, L, 512):
                P.op("sp", lambda e: e.dma_start(out=gt[:], in_=cm(GC[:, t0:t0 + 512])), writes=["z_g"], dma=True)
                P.op("sp", lambda e: e.dma_start(out=bt_[:], in_=cm(BC[:, t0:t0 + 512])), writes=["z_b"], dma=True)
                for sbk in range(4):
                    a_, ka = ya.next()
                    b_, kb = yb_.next()
                    P.op("sp", lambda e: e.dma_start(out=a_[:], in_=YT[0, t0 + sbk * 128:t0 + (sbk + 1) * 128, :]), writes=[ka], dma=True)
                    P.op("sp", lambda e: e.dma_start(out=b_[:], in_=YT[1, t0 + sbk * 128:t0 + (sbk + 1) * 128, :]), writes=[kb], dma=True)
                    P.op("pool", lambda e: e.tensor_tensor(out=a_[:], in0=a_[:], in1=b_[:], op=ALU.add), reads=[ka, kb], writes=[ka])
                    pt, kpt = psT.next()
                    for ct in range(4):
                        P.op("pe", lambda e: e.matmul(pt[:, ct * 128:(ct + 1) * 128], lhsT=a_[:, ct * 128:(ct + 1) * 128], rhs=self.ident[:], start=True, stop=True),
                             reads=[ka, "ident"], writes=[kpt])
                    P.op("act", lambda e: e.copy(out=ycm[:, :, sbk * 128:(sbk + 1) * 128], in_=pt[:].rearrange("p (c t) -> p c t", c=4)),
                         reads=[kpt], writes=[("z_ycm", sbk)])
                yk = [("z_ycm", q) for q in range(4)]
                for ct in range(4):
                    pa, kpa = psA.next()
                    P.op("pe", lambda e: e.matmul(pa[:], lhsT=bones[:], rhs=ycm[:, ct, :], start=True, stop=True), reads=["z_bones"] + yk, writes=[kpa])
                    P.op("dve", lambda e: e.tensor_tensor(out=cen[:], in0=ycm[:, ct, :], in1=pa[:], op=ALU.subtract), reads=yk + [kpa], writes=["z_cen"])
                    P.op("pool", lambda e: e.tensor_tensor(out=sq[:], in0=cen[:], in1=cen[:], op=ALU.mult), reads=["z_cen"], writes=["z_sq"])
                    pa2, kpa2 = psA.next()
                    P.op("pe", lambda e: e.matmul(pa2[:], lhsT=bones[:], rhs=sq[:], start=True, stop=True), reads=["z_bones", "z_sq"], writes=[kpa2])
                    P.op("dve", lambda e: e.tensor_scalar(out=rstd[:], in0=pa2[:], scalar1=64e-5, scalar2=None, op0=ALU.add), reads=[kpa2], writes=["z_rstd"])
                    P.op("act", lambda e: e.activation(out=rstd[:], in_=rstd[:], func=AF.Sqrt), reads=["z_rstd"], writes=["z_rstd"])
                    P.op("dve", lambda e: e.reciprocal(out=rstd[:], in_=rstd[:]), reads=["z_rstd"], writes=["z_rstd"])
                    P.op("dve", lambda e: e.tensor_tensor(out=cen[:], in0=cen[:], in1=rstd[:], op=ALU.mult), reads=["z_cen", "z_rstd"], writes=["z_cen"])
                    P.op("dve", lambda e: e.tensor_scalar(out=cen[:], in0=cen[:], scalar1=pc[:, ct, 6:7], scalar2=pc[:, ct, 7:8], op0=ALU.mult, op1=ALU.add),
                         reads=["z_cen", "z_pc"], writes=["z_cen"])
                    P.op("pool", lambda e: e.tensor_tensor(out=cen[:], in0=cen[:], in1=bt_[:, ct, :], op=ALU.add), reads=["z_cen", "z_b"], writes=["z_cen"])
                    P.op("dve", lambda e: e.tensor_tensor(out=yo[:, ct, :], in0=cen[:], in1=gt[:, ct, :], op=ALU.mult), reads=["z_cen", "z_g"], writes=["z_yo"])
                P.op("pool", lambda e: e.dma_start(out=cm(self.YM[512:1024, t0:t0 + 512]), in_=yo[:]), reads=["z_yo"], dma=True)

    def _mod_reduce(self, eng, t, q, key):
        P = self.P
        P.op(eng, lambda e: e.tensor_scalar(out=t[1], in0=t[0], scalar1=1.0 / q, scalar2=MAGIC, op0=ALU.mult, op1=ALU.add), reads=[key], writes=[key + "_t"])
        P.op(eng, lambda e: e.tensor_scalar(out=t[1], in0=t[1], scalar1=MAGIC, scalar2=-float(q), op0=ALU.subtract, op1=ALU.mult), reads=[key + "_t"], writes=[key + "_t"])
        P.op(eng, lambda e: e.tensor_tensor(out=t[0], in0=t[0], in1=t[1], op=ALU.add), reads=[key, key + "_t"], writes=[key])

    def gen_dft(self, L):
        nc, P = self.nc, self.P
        nt = L // 128
        q = L // 32
        DF = self.dscr("DF_%d" % L, [2, nt, 128, nt, 128], BF16)
        DI = self.dscr("DI_%d" % L, [2, nt, 128, nt, 128], BF16)
        self.dft[L] = (DF, DI)
        sc = (TWO_PI / (4.0 * L)) * 0.999999
        with ExitStack() as s2:
            def sb(name, shape, dt=F32):
                return self.sbt(s2, name, shape, dt)
            ii = sb("g_ii", [128, 128], I32)
            pcol = sb("g_pcol", [128, 1]); p2col = sb("g_p2col", [128, 1]); jrow = sb("g_jrow", [128, 128]); crow = sb("g_crow", [128, nt])
            m1 = sb("g_m1", [128, nt, 128])
            P.op("pool", lambda e: e.iota(ii[:, 0:1], pattern=[[0, 1]], base=0, channel_multiplier=1), writes=["g_ii"])
            P.op("dve", lambda e: e.tensor_copy(out=pcol[:], in_=ii[:, 0:1]), reads=["g_ii"], writes=["g_pcol"])
            P.op("dve", lambda e: e.tensor_scalar(out=p2col[:], in0=pcol[:], scalar1=2.0, scalar2=1.0, op0=ALU.mult, op1=ALU.add), reads=["g_pcol"], writes=["g_p2col"])
            P.op("pool", lambda e: e.iota(ii[:], pattern=[[1, 128]], base=0, channel_multiplier=0), reads=["g_pcol"], writes=["g_ii"])
            P.op("dve", lambda e: e.tensor_copy(out=jrow[:], in_=ii[:]), reads=["g_ii"], writes=["g_jrow"])
            P.op("dve", lambda e: e.tensor_copy(out=crow[:], in_=jrow[:, 0:nt]), reads=["g_jrow"], writes=["g_crow"])
            P.op("dve", lambda e: e.tensor_tensor(out=m1[:], in0=crow[:].unsqueeze(2).broadcast_to([128, nt, 128]),
                                                  in1=jrow[:].unsqueeze(1).broadcast_to([128, nt, 128]), op=ALU.mult), reads=["g_crow", "g_jrow"], writes=["g_m1"])
            P.op("dve", lambda e: e.tensor_scalar(out=m1[:].rearrange("p a b -> p (a b)"), in0=m1[:].rearrange("p a b -> p (a b)"), scalar1=256.0, scalar2=None, op0=ALU.mult),
                 reads=["g_m1"], writes=["g_m1"])
            sets = {}
            for eng in ("dve",):
                sets[eng] = dict(
                    f2=sb("g_f2" + eng, [128, 128]), f1=sb("g_f1" + eng, [128, 128]), col=sb("g_col" + eng, [128, 2]),
                    G=sb("g_G" + eng, [128, nt, 128]), Gt=sb("g_Gt" + eng, [128, nt, 128]), A=sb("g_A" + eng, [128, nt, 128]),
                    o=[sb("g_o%d%s" % (i, eng), [128, nt, 128], BF16) for i in range(2)])
            fl = lambda ap: ap.rearrange("p a b -> p (a b)")
            for g in range(nt):
                for inv in (0, 1):
                    eng = "dve"
                    S_ = sets[eng]
                    k = "g_" + eng
                    if inv == 0:
                        fc = g
                        P.op(eng, lambda e: e.tensor_scalar(out=S_["f2"][:], in0=jrow[:], scalar1=2.0, scalar2=float(256 * fc + 1), op0=ALU.mult, op1=ALU.add), reads=["g_jrow"], writes=[k + "f2"])
                        P.op(eng, lambda e: e.tensor_scalar(out=S_["f1"][:], in0=S_["f2"][:], scalar1=pcol[:, 0:1], scalar2=None, op0=ALU.mult), reads=[k + "f2", "g_pcol"], writes=[k + "f1"])
                        P.op(eng, lambda e: e.tensor_tensor(out=S_["G"][:], in0=S_["f2"][:].unsqueeze(1).broadcast_to([128, nt, 128]),
                                                            in1=crow[:].unsqueeze(2).broadcast_to([128, nt, 128]), op=ALU.mult), reads=[k + "f2", "g_crow"], writes=[k + "G"])
                        self._mod_reduce(eng, (fl(S_["G"][:]), fl(S_["Gt"][:])), q, k + "G")
                        P.op(eng, lambda e: e.tensor_scalar(out=fl(S_["G"][:]), in0=fl(S_["G"][:]), scalar1=128.0, scalar2=None, op0=ALU.mult), reads=[k + "G"], writes=[k + "G"])
                        P.op(eng, lambda e: e.tensor_tensor(out=S_["G"][:], in0=S_["G"][:], in1=S_["f1"][:].unsqueeze(1).broadcast_to([128, nt, 128]), op=ALU.add),
                             reads=[k + "G", k + "f1"], writes=[k + "G"])
                    else:
                        tc = g
                        P.op(eng, lambda e: e.tensor_scalar(out=S_["col"][:, 0:1], in0=p2col[:], scalar1=float(tc), scalar2=None, op0=ALU.mult), reads=["g_p2col"], writes=[k + "col"])
                        self._mod_reduce(eng, (S_["col"][:, 0:1], S_["col"][:, 1:2]), q, k + "col")
                        P.op(eng, lambda e: e.tensor_scalar(out=S_["col"][:, 0:1], in0=S_["col"][:, 0:1], scalar1=128.0, scalar2=None, op0=ALU.mult), reads=[k + "col"], writes=[k + "col"])
                        P.op(eng, lambda e: e.tensor_scalar(out=S_["f1"][:], in0=jrow[:], scalar1=p2col[:, 0:1], scalar2=S_["col"][:, 0:1], op0=ALU.mult, op1=ALU.add),
                             reads=["g_jrow", "g_p2col", k + "col"], writes=[k + "f1"])
                        P.op(eng, lambda e: e.tensor_tensor(out=S_["G"][:], in0=m1[:], in1=S_["f1"][:].unsqueeze(1).broadcast_to([128, nt, 128]), op=ALU.add),
                             reads=["g_m1", k + "f1"], writes=[k + "G"])
                    for trig in (0, 1):
                        P.op(eng, lambda e: e.tensor_scalar(out=fl(S_["A"][:]), in0=fl(S_["G"][:]), scalar1=float(L if trig == 0 else 0), scalar2=None, op0=ALU.add),
                             reads=[k + "G"], writes=[k + "A"])
                        self._mod_reduce(eng, (fl(S_["A"][:]), fl(S_["Gt"][:])), 4 * L, k + "A")
                        ob = S_["o"][trig]
                        ok = k + "o%d" % trig
                        P.op("act", lambda e: e.activation(out=fl(ob[:]), in_=fl(S_["A"][:]), func=AF.Sin, scale=sc), reads=[k + "A"], writes=[ok])
                        dst = (DF if inv == 0 else DI)[trig, g]
                        P.op("sp", lambda e: e.dma_start(out=dst, in_=ob[:]), reads=[ok], dma=True)
            P.barrier()

    def hyena(self, L, j):
        nc, P = self.nc, self.P
        nt = L // 128
        DF, DI = self.dft[L]
        HC = self.HYC
        HZ = self.HYZ
        KT = self.HYK
        KF = self.HYF
        with ExitStack() as s2:
            def sb(name, shape, dt=F32):
                return self.sbt(s2, name, shape, dt)
            hc = sb("h_hc", [128, 12, 4])
            pp = Rot([sb("h_pp%d" % i, [128, 2050]) for i in range(2)], "h_pp")
            oo = Rot([sb("h_oo%d" % i, [128, 2048]) for i in range(2)], "h_oo")
            TBc = min(L, 2048)
            P.op("sp", lambda e: e.dma_start(out=hc[:], in_=self.hy_cols[j]), writes=["h_hc"], dma=True)
            for rt_ in range(12):
                for t0 in range(0, L, TBc):
                    p_, kp = pp.next()
                    o_, ko = oo.next()
                    lo_t = max(t0 - 1, 0); hi_t = min(t0 + TBc + 1, L)
                    if t0 == 0:
                        P.op("pool", lambda e: e.memset(p_[:, 0:1], 0.0), writes=[kp])
                    if t0 + TBc == L:
                        P.op("pool", lambda e: e.memset(p_[:, TBc + 1:TBc + 2], 0.0), writes=[kp])
                    P.op("sp", lambda e: e.dma_start(out=p_[:, lo_t - (t0 - 1):hi_t - (t0 - 1)], in_=self.PJ[1024 + rt_ * 128:1024 + (rt_ + 1) * 128, lo_t:hi_t]), writes=[kp], dma=True)
                    eng = "dve"
                    P.op(eng, lambda e: e.tensor_scalar(out=o_[:, 0:TBc], in0=p_[:, 0:TBc], scalar1=hc[:, rt_, 0:1], scalar2=hc[:, rt_, 3:4], op0=ALU.mult, op1=ALU.add),
                         reads=[kp, "h_hc"], writes=[ko])
                    for q_ in (1, 2):
                        P.op(eng, lambda e: e.scalar_tensor_tensor(out=o_[:, 0:TBc], in0=p_[:, q_:q_ + TBc], scalar=hc[:, rt_, q_:q_ + 1], op0=ALU.mult, in1=o_[:, 0:TBc], op1=ALU.add),
                             reads=[kp, ko, "h_hc"], writes=[ko])
                    P.op("pool", lambda e: e.dma_start(out=HC[rt_ // 4, (rt_ % 4) * 128:(rt_ % 4 + 1) * 128, t0:t0 + TBc], in_=o_[:, 0:TBc]), reads=[ko], dma=True)
            P.barrier()
        with ExitStack() as s2:
            def sb(name, shape, dt=F32):
                return self.sbt(s2, name, shape, dt)
            w1 = sb("f_w1", [33, 64]); w2 = sb("f_w2", [64, 64]); w3 = sb("f_w3", [64, 2048]); fc_ = sb("f_fc", [64, 6])
            zp = sb("f_zp", [33, 512]); h1 = sb("f_h1", [64, 512]); h2 = sb("f_h2", [64, 512]); tq = sb("f_tq", [64, 512])
            drow = sb("f_drow", [128, 512]); tcol = sb("f_tcol", [128, nt]); dec = sb("f_dec", [128, 512])
            kf = sb("f_kf", [128, 2048]); kb = Rot([sb("f_kb%d" % i, [128, 2048], BF16) for i in range(2)], "f_kb"); ka = sb("f_ka", [128, 2048], BF16)
            rn = sb("f_rn", [128, 1024])
            psK = Rot(self.psum[0:3], "psK")
            psH = Rot(self.psum[3:4], "psH")
            psNm = self.psum[4:8]
            P.op("sp", lambda e: e.dma_start(out=w1[:], in_=self.hy_w1[j]), writes=["f_w"], dma=True)
            P.op("sp", lambda e: e.dma_start(out=w2[:], in_=self.hy_w2[j]), writes=["f_w"], dma=True)
            P.op("sp", lambda e: e.dma_start(out=w3[:], in_=self.hy_w3[j]), writes=["f_w"], dma=True)
            P.op("sp", lambda e: e.dma_start(out=fc_[:], in_=self.hy_fcols[j]), writes=["f_fc"], dma=True)
            P.op("sp", lambda e: e.dma_start(out=drow[:], in_=self.hy_drow[:, :]), writes=["f_drow"], dma=True)
            P.op("sp", lambda e: e.dma_start(out=tcol[:], in_=self.hy_ntcol[L][:, :]), writes=["f_tcol"], dma=True)
            for i_ in range(2):
                P.op("dve", lambda e: e.tensor_tensor(out=fc_[:, 4 + i_:5 + i_], in0=fc_[:, i_:i_ + 1], in1=fc_[:, 2 + i_:3 + i_], op=ALU.mult), reads=["f_fc"], writes=["f_fc"])
            for tb in range(nt):
                if tb % 4 == 0:
                    c0 = tb * 128
                    P.op("sp", lambda e: e.dma_start(out=zp[:], in_=self.hy_zpos[L][:, c0:c0 + 512]), writes=["f_zp"], dma=True)
                    ph, kph = psH.next()
                    P.op("pe", lambda e: e.matmul(ph[0:64, :], lhsT=w1[:], rhs=zp[:], start=True, stop=True), reads=["f_w", "f_zp"], writes=[kph])
                    P.op("dve", lambda e: e.tensor_scalar(out=h1[:], in0=ph[0:64, :], scalar1=fc_[:, 2:3], scalar2=fc_[:, 4:5], op0=ALU.mult, op1=ALU.add), reads=[kph, "f_fc"], writes=["f_h1"])
                    self._range_reduce(h1[:], tq[:], "f_h1", "f_tq", 512)
                    P.op("act", lambda e: e.activation(out=h1[:], in_=h1[:], func=AF.Sin), reads=["f_h1"], writes=["f_h1"])
                    ph, kph = psH.next()
                    P.op("pe", lambda e: e.matmul(ph[0:64, :], lhsT=w2[:], rhs=h1[:], start=True, stop=True), reads=["f_w", "f_h1"], writes=[kph])
                    P.op("dve", lambda e: e.tensor_scalar(out=h2[:], in0=ph[0:64, :], scalar1=fc_[:, 3:4], scalar2=fc_[:, 5:6], op0=ALU.mult, op1=ALU.add), reads=[kph, "f_fc"], writes=["f_h2"])
                    self._range_reduce(h2[:], tq[:], "f_h2", "f_tq", 512)
                    P.op("act", lambda e: e.activation(out=h2[:], in_=h2[:], func=AF.Sin), reads=["f_h2"], writes=["f_h2"])
                cc = (tb % 4) * 128
                P.op("act", lambda e: e.activation(out=dec[:], in_=drow[:], func=AF.Exp, scale=tcol[:, tb:tb + 1]), reads=["f_drow", "f_tcol"], writes=["f_dec"])
                for g4 in range(4):
                    pk, kpk = psK.next()
                    P.op("pe", lambda e: e.matmul(pk[:], lhsT=h2[:, cc:cc + 128], rhs=w3[:, g4 * 512:(g4 + 1) * 512], start=True, stop=True), reads=["f_h2", "f_w"], writes=[kpk])
                    P.op("dve", lambda e: e.tensor_tensor(out=kf[:, g4 * 512:(g4 + 1) * 512], in0=pk[:], in1=dec[:], op=ALU.mult), reads=[kpk, "f_dec"], writes=[("f_kf", g4)])
                kfk = [("f_kf", g4) for g4 in range(4)]
                if tb == 0:
                    P.op("dve", lambda e: e.memset(kf[0:1, 1024:2048], 0.0), reads=kfk, writes=kfk)
                kb_, kkb = kb.next()
                P.op("pool", lambda e: e.tensor_copy(out=kb_[:], in_=kf[:]), reads=kfk, writes=[kkb])
                P.op("pool", lambda e: e.dma_start(out=KT[tb * 128:(tb + 1) * 128, :], in_=kb_[:]), reads=[kkb], dma=True)
                P.op("act", lambda e: e.activation(out=ka[:], in_=kf[:], func=AF.Abs), reads=kfk, writes=["f_ka"])
                for g4 in range(4):
                    P.op("pe", lambda e: e.matmul(psNm[g4][:], lhsT=self.ones_bf[:], rhs=ka[:, g4 * 512:(g4 + 1) * 512], start=(tb == 0), stop=(tb == nt - 1)),
                         reads=["f_ka", "ones_bf"], writes=[("psNm", g4)])
            for o in range(2):
                P.op("dve", lambda e: e.tensor_copy(out=rn[:, o * 512:(o + 1) * 512], in_=psNm[o][:]), reads=[("psNm", o)], writes=["f_rn"])
                P.op("dve", lambda e: e.tensor_tensor(out=rn[:, o * 512:(o + 1) * 512], in0=rn[:, o * 512:(o + 1) * 512], in1=psNm[2 + o][:], op=ALU.add),
                     reads=["f_rn", ("psNm", 2 + o)], writes=["f_rn"])
            P.op("dve", lambda e: e.reciprocal(out=rn[:], in_=rn[:]), reads=["f_rn"], writes=["f_rn"])
            P.op("pool", lambda e: e.dma_start(out=self.HYRN[:, :], in_=rn[:]), reads=["f_rn"], dma=True)
            P.barrier()
        with ExitStack() as s2:
            def sb(name, shape, dt=F32):
                return self.sbt(s2, name, shape, dt)
            rn = sb("k_rn", [128, 1024])
            kt = sb("k_kt", [128, nt, 2, 256], BF16)
            wc = Rot([sb("k_wc%d" % i, [128, nt, 128], BF16) for i in range(2)], "k_wc")
            ws = Rot([sb("k_ws%d" % i, [128, nt, 128], BF16) for i in range(2)], "k_ws")
            ko_ = Rot([sb("k_ko%d" % i, [128, 2, 256]) for i in range(2)], "k_ko")
            ps4 = [Rot(self.psum[2 * i:2 * i + 2], "psF%d" % i) for i in range(4)]
            P.op("sp", lambda e: e.dma_start(out=rn[:], in_=self.HYRN[:, :]), writes=["k_rn"], dma=True)
            for o in range(2):
                for hf in range(2):
                    for d in range(2):
                        c0 = d * 1024 + o * 512 + hf * 256
                        P.op("sp", lambda e: e.dma_start(out=kt[:, :, d, :], in_=KT[0:L, c0:c0 + 256].rearrange("(tc p) c -> p tc c", p=128)), writes=[("k_kt", d)], dma=True)
                    for fc in range(nt):
                        wc_, kwc = wc.next(); ws_, kws = ws.next()
                        P.op("sp", lambda e: e.dma_start(out=wc_[:], in_=DF[0, fc]), writes=[kwc], dma=True)
                        P.op("act", lambda e: e.dma_start(out=ws_[:], in_=DF[1, fc]), writes=[kws], dma=True)
                        accC, kaC = ps4[0].next()
                        accS, kaS = ps4[1].next()
                        for tc in range(nt):
                            rhs_ = kt[:, tc, :, :].rearrange("p d c -> p (d c)")
                            P.op("pe", lambda e: e.matmul(accC[:], lhsT=wc_[:, tc, :], rhs=rhs_, start=(tc == 0), stop=(tc == nt - 1)),
                                 reads=[kwc, ("k_kt", 0), ("k_kt", 1)], writes=[kaC])
                            P.op("pe", lambda e: e.matmul(accS[:], lhsT=ws_[:, tc, :], rhs=rhs_, start=(tc == 0), stop=(tc == nt - 1)),
                                 reads=[kws, ("k_kt", 0), ("k_kt", 1)], writes=[kaS])
                        o_, kko = ko_.next()
                        rs = rn[:, o * 512 + hf * 256:o * 512 + hf * 256 + 256]
                        P.op("dve", lambda e: e.tensor_copy(out=o_[:, 0, :], in_=accC[:, 0:256]), reads=[kaC], writes=[kko])
                        P.op("dve", lambda e: e.tensor_tensor(out=o_[:, 0, :], in0=o_[:, 0, :], in1=accC[:, 256:512], op=ALU.add), reads=[kko, kaC], writes=[kko])
                        P.op("dve", lambda e: e.tensor_tensor(out=o_[:, 0, :], in0=o_[:, 0, :], in1=rs, op=ALU.mult), reads=[kko, "k_rn"], writes=[kko])
                        P.op("dve", lambda e: e.tensor_copy(out=o_[:, 1, :], in_=accS[:, 256:512]), reads=[kaS], writes=[kko])
                        P.op("dve", lambda e: e.tensor_tensor(out=o_[:, 1, :], in0=o_[:, 1, :], in1=accS[:, 0:256], op=ALU.subtract), reads=[kko, kaS], writes=[kko])
                        P.op("dve", lambda e: e.tensor_tensor(out=o_[:, 1, :], in0=o_[:, 1, :], in1=rs, op=ALU.mult), reads=[kko, "k_rn"], writes=[kko])
                        P.op("pool", lambda e: e.dma_start(out=KF[o, hf, fc], in_=o_[:]), reads=[kko], dma=True)
            P.barrier()
        with ExitStack() as s2:
            def sb(name, shape, dt=F32):
                return self.sbt(s2, name, shape, dt)
            hb = sb("c_hb", [128, 4, 2])
            zt = sb("c_zt", [128, nt, 256], BF16)
            Pp = sb("c_P", [128, nt, 2, 256], BF16)
            wc = Rot([sb("c_wc%d" % i, [128, nt, 128], BF16) for i in range(2)], "c_wc")
            ws = Rot([sb("c_ws%d" % i, [128, nt, 128], BF16) for i in range(2)], "c_ws")
            kfR = Rot([sb("c_kf%d" % i, [128, 2, 256]) for i in range(2)], "c_kf")
            zin = Rot([sb("c_zin%d" % i, [128, 512]) for i in range(2)], "c_zin")
            zb = Rot([sb("c_zb%d" % i, [128, 512], BF16) for i in range(2)], "c_zb")
            ta_ = sb("c_ta", [128, 256]); tb_ = sb("c_tb", [128, 256])
            ytk = Rot([sb("c_ytk%d" % i, [128, 256]) for i in range(2)], "c_ytk")
            ycm = sb("c_ycm", [128, 2, 512]); gx = sb("c_gx", [128, 512]); zc = sb("c_zc", [128, 512]); ob = sb("c_ob", [128, 512], BF16)
            psX = [Rot(self.psum[0:2], "psXr"), Rot(self.psum[2:4], "psXs")]
            psY = Rot(self.psum[4:6], "psY")
            psT = Rot(self.psum[6:8], "psT")
            P.op("sp", lambda e: e.dma_start(out=hb[:], in_=self.hy_bias[j]), writes=["c_hb"], dma=True)
            for o in range(2):
                for hf in range(2):
                    zsrc = HC[0] if o == 0 else HZ
                    for ct2 in range(2):
                        r0 = hf * 256 + ct2 * 128
                        for t0 in range(0, L, 512):
                            zi, kzi = zin.next()
                            P.op("sp", lambda e: e.dma_start(out=zi[:], in_=zsrc[r0:r0 + 128, t0:t0 + 512]), writes=[kzi], dma=True)
                            zb_, kzb = zb.next()
                            P.op("pool", lambda e: e.tensor_copy(out=zb_[:], in_=zi[:]), reads=[kzi], writes=[kzb])
                            pt, kpt = psT.next()
                            for sbk in range(4):
                                P.op("pe", lambda e: e.matmul(pt[:, sbk * 128:(sbk + 1) * 128], lhsT=zi[:, sbk * 128:(sbk + 1) * 128], rhs=self.ident[:], start=True, stop=True),
                                     reads=[kzi, "ident"], writes=[kpt])
                            P.op("act", lambda e: e.copy(out=zt[:, t0 // 128:t0 // 128 + 4, ct2 * 128:(ct2 + 1) * 128], in_=pt[:].rearrange("p (a b) -> p a b", a=4)),
                                 reads=[kpt], writes=["c_zt"])
                    for fc in range(nt):
                        wc_, kwc = wc.next(); ws_, kws = ws.next()
                        P.op("sp", lambda e: e.dma_start(out=wc_[:], in_=DF[0, fc]), writes=[kwc], dma=True)
                        P.op("act", lambda e: e.dma_start(out=ws_[:], in_=DF[1, fc]), writes=[kws], dma=True)
                        kf_, kkf = kfR.next()
                        P.op("pool", lambda e: e.dma_start(out=kf_[:], in_=KF[o, hf, fc]), writes=[kkf], dma=True)
                        xr, kxr = psX[0].next(); xs, kxs = psX[1].next()
                        for tc in range(nt):
                            P.op("pe", lambda e: e.matmul(xr[:, 0:256], lhsT=wc_[:, tc, :], rhs=zt[:, tc, :], start=(tc == 0), stop=(tc == nt - 1)), reads=[kwc, "c_zt"], writes=[kxr])
                            P.op("pe", lambda e: e.matmul(xs[:, 0:256], lhsT=ws_[:, tc, :], rhs=zt[:, tc, :], start=(tc == 0), stop=(tc == nt - 1)), reads=[kws, "c_zt"], writes=[kxs])
                        P.op("dve", lambda e: e.tensor_tensor(out=ta_[:], in0=xr[:, 0:256], in1=kf_[:, 0, :], op=ALU.mult), reads=[kxr, kkf], writes=["c_ta"])
                        P.op("dve", lambda e: e.tensor_tensor(out=tb_[:], in0=xs[:, 0:256], in1=kf_[:, 1, :], op=ALU.mult), reads=[kxs, kkf], writes=["c_tb"])
                        P.op("dve", lambda e: e.tensor_tensor(out=Pp[:, fc, 0, :], in0=ta_[:], in1=tb_[:], op=ALU.add), reads=["c_ta", "c_tb"], writes=[("c_P", fc)])
                        P.op("dve", lambda e: e.tensor_tensor(out=ta_[:], in0=xs[:, 0:256], in1=kf_[:, 0, :], op=ALU.mult), reads=[kxs, kkf], writes=["c_ta"])
                        P.op("dve", lambda e: e.tensor_tensor(out=tb_[:], in0=xr[:, 0:256], in1=kf_[:, 1, :], op=ALU.mult), reads=[kxr, kkf], writes=["c_tb"])
                        P.op("dve", lambda e: e.tensor_tensor(out=Pp[:, fc, 1, :], in0=ta_[:], in1=tb_[:], op=ALU.subtract), reads=["c_ta", "c_tb"], writes=[("c_P", fc)])
                    pk = [("c_P", fc) for fc in range(nt)]
                    for tc in range(nt):
                        wc_, kwc = wc.next(); ws_, kws = ws.next()
                        P.op("sp", lambda e: e.dma_start(out=wc_[:], in_=DI[0, tc]), writes=[kwc], dma=True)
                        P.op("act", lambda e: e.dma_start(out=ws_[:], in_=DI[1, tc]), writes=[kws], dma=True)
                        py, kpy = psY.next()
                        for fc in range(nt):
                            P.op("pe", lambda e: e.matmul(py[:, 0:256], lhsT=wc_[:, fc, :], rhs=Pp[:, fc, 0, :], start=(fc == 0), stop=False), reads=[kwc] + pk, writes=[kpy])
                            P.op("pe", lambda e: e.matmul(py[:, 0:256], lhsT=ws_[:, fc, :], rhs=Pp[:, fc, 1, :], start=False, stop=(fc == nt - 1)), reads=[kws] + pk, writes=[kpy])
                        yk_, kyk = ytk.next()
                        P.op("act", lambda e: e.activation(out=yk_[:], in_=py[:, 0:256], func=AF.Copy, scale=1.0 / L), reads=[kpy], writes=[kyk])
                        pt, kpt = psT.next()
                        for ct2 in range(2):
                            P.op("pe", lambda e: e.matmul(pt[:, ct2 * 128:(ct2 + 1) * 128], lhsT=yk_[:, ct2 * 128:(ct2 + 1) * 128], rhs=self.ident[:], start=True, stop=True),
                                 reads=[kyk, "ident"], writes=[kpt])
                        P.op("act", lambda e: e.copy(out=ycm[:, :, (tc % 4) * 128:(tc % 4 + 1) * 128], in_=pt[:, 0:256].rearrange("p (a b) -> p a b", a=2)),
                             reads=[kpt], writes=[("c_ycm", tc % 4)])
                        if tc % 4 == 3:
                            t0 = (tc - 3) * 128
                            for ct2 in range(2):
                                r0 = hf * 256 + ct2 * 128
                                ctg = hf * 2 + ct2
                                P.op("sp", lambda e: e.dma_start(out=zc[:], in_=zsrc[r0:r0 + 128, t0:t0 + 512]), writes=["c_zc"], dma=True)
                                P.op("sp", lambda e: e.dma_start(out=gx[:], in_=HC[1 + o, r0:r0 + 128, t0:t0 + 512]), writes=["c_gx"], dma=True)
                                P.op("dve", lambda e: e.scalar_tensor_tensor(out=zc[:], in0=zc[:], scalar=hb[:, ctg, o:o + 1], op0=ALU.mult, in1=ycm[:, ct2, :], op1=ALU.add),
                                     reads=["c_zc",AllK.y, "c_hb"] if False else ["c_zc", "c_hb"] + [("c_ycm", q_) for q_ in range(4)], writes=["c_zc"])
                                if o == 0:
                                    P.op("pool", lambda e: e.tensor_tensor(out=zc[:], in0=zc[:], in1=gx[:], op=ALU.mult), reads=["c_zc", "c_gx"], writes=["c_zc"])
                                    P.op("pool", lambda e: e.dma_start(out=HZ[r0:r0 + 128, t0:t0 + 512], in_=zc[:]), reads=["c_zc"], dma=True)
                                else:
                                    P.op("pool", lambda e: e.tensor_tensor(out=ob[:], in0=zc[:], in1=gx[:], op=ALU.mult), reads=["c_zc", "c_gx"], writes=["c_ob"])
                                    P.op("pool", lambda e: e.dma_start(out=self.YM[512 + r0:512 + r0 + 128, t0:t0 + 512], in_=ob[:]), reads=["c_ob"], dma=True)
                    P.barrier()

    def dense_pass(self, xin, yout, L, l):
        nc, P = self.nc, self.P
        first = (l == 0)
        last = (l == self.depth)
        NH = TT // 512
        with ExitStack() as s2:
            def sb(name, shape, dt=F32):
                return self.sbt(s2, name, shape, dt)
            xt = sb("d_xt", [128, 8, TT])
            xn = sb("d_xn", [128, 8, TT], BF16)
            rstd = sb("d_rstd", [128, TT])
            hm = sb("d_hm", [128, 22, TT], BF16)
            sg = Rot([sb("d_sg%d" % i, [128, 512]) for i in range(2)], "d_sg")
            wgR = Rot([sb("d_wg%d" % i, [128, 8, 256], BF16) for i in range(3)], "d_wg")
            wuR = Rot([sb("d_wu%d" % i, [128, 8, 256], BF16) for i in range(3)], "d_wu")
            wdR = Rot([sb("d_wd%d" % i, [128, 22, 128], BF16) for i in range(2)], "d_wd")
            wpR = Rot([sb("d_wp%d" % i, [128, 8, 128], BF16) for i in range(3)], "d_wp")
            pjs = Rot([sb("d_pj%d" % i, [128, 512]) for i in range(2)], "d_pj")
            xtok = sb("d_xtok", [128, D])
            psG = Rot(self.psum[0:2], "psG")
            psU = Rot(self.psum[2:4], "psU")
            psA = Rot(self.psum[4:6], "psA")
            psN = Rot(self.psum[6:8], "psN")
            hs = lambda h: slice(h * 512, (h + 1) * 512)
            hmk = [("d_hm", c) for c in range(22)]

            def rmsnorm():
                P.op("act", lambda e: e.activation(out=hm[:, 0:8, :].rearrange("p a b -> p (a b)"),
                                                   in_=xt[:].rearrange("p a b -> p (a b)"), func=AF.Square),
                     reads=["d_xt"], writes=hmk[0:8])
                for h in range(NH):
                    pn, kn = psN.next()
                    for kc in range(8):
                        P.op("pe", lambda e: e.matmul(pn[:], lhsT=self.ones_bf[:], rhs=hm[:, kc, hs(h)], start=(kc == 0), stop=(kc == 7)),
                             reads=[("d_hm", kc), "ones_bf"], writes=[kn])
                    P.op("dve", lambda e: e.tensor_scalar(out=rstd[:, hs(h)], in0=pn[:], scalar1=1.0 / D, scalar2=EPS,
                                                          op0=ALU.mult, op1=ALU.add), reads=[kn], writes=["d_rstd"])
                P.op("act", lambda e: e.activation(out=rstd[:], in_=rstd[:], func=AF.Sqrt), reads=["d_rstd"], writes=["d_rstd"])
                P.op("dve", lambda e: e.reciprocal(out=rstd[:], in_=rstd[:]), reads=["d_rstd"], writes=["d_rstd"])

            def normed(which, out_t, key):
                rmsnorm()
                for kc in range(8):
                    P.op("dve", lambda e: e.scalar_tensor_tensor(out=out_t[:, kc, :], in0=xt[:, kc, :],
                                                                 scalar=self.gcols[:, which, kc:kc + 1], op0=ALU.mult,
                                                                 in1=rstd[:], op1=ALU.mult),
                         reads=["d_xt", "d_rstd", "gcols"], writes=[key])

            def ffn(f, lay):
                normed((0 if f == 1 else 2) * DEPTH + lay, xn, "d_xn")
                for mp in range(11):
                    wg, kg = wgR.next()
                    wu, ku = wuR.next()
                    P.op("sp", lambda e: e.dma_start(out=wg[:], in_=self.wb[("g", f, lay)][mp]), writes=[kg], dma=True)
                    P.op("sp", lambda e: e.dma_start(out=wu[:], in_=self.wb[("u", f, lay)][mp]), writes=[ku], dma=True)
                    for mi in range(2):
                        mc = 2 * mp + mi
                        for h in range(NH):
                            pg, kpg = psG.next()
                            pu, kpu = psU.next()
                            for kc in range(8):
                                P.op("pe", lambda e: e.matmul(pg[:], lhsT=wg[:, kc, mi * 128:(mi + 1) * 128], rhs=xn[:, kc, hs(h)],
                                                              start=(kc == 0), stop=(kc == 7)), reads=[kg, "d_xn"], writes=[kpg])
                            for kc in range(8):
                                P.op("pe", lambda e: e.matmul(pu[:], lhsT=wu[:, kc, mi * 128:(mi + 1) * 128], rhs=xn[:, kc, hs(h)],
                                                              start=(kc == 0), stop=(kc == 7)), reads=[ku, "d_xn"], writes=[kpu])
                            s_, ks = sg.next()
                            P.op("act", lambda e: e.activation(out=s_[:], in_=pg[:], func=AF.Silu), reads=[kpg], writes=[ks])
                            P.op("dve", lambda e: e.tensor_tensor(out=hm[:, mc, hs(h)], in0=s_[:], in1=pu[:], op=ALU.mult),
                                 reads=[ks, kpu], writes=[("d_hm", mc)])
                for dc in range(8):
                    wd, kd = wdR.next()
                    P.op("sp", lambda e: e.dma_start(out=wd[:], in_=self.wb[("d", f, lay)][dc]), writes=[kd], dma=True)
                    for h in range(NH):
                        pa, kpa = psA.next()
                        for fc in range(22):
                            P.op("pe", lambda e: e.matmul(pa[:], lhsT=wd[:, fc, :], rhs=hm[:, fc, hs(h)], start=(fc == 0), stop=(fc == 21)),
                                 reads=[kd, ("d_hm", fc)], writes=[kpa])
                        P.op("dve", lambda e: e.scalar_tensor_tensor(out=xt[:, dc, hs(h)], in0=pa[:], scalar=0.5, op0=ALU.mult,
                                                                     in1=xt[:, dc, hs(h)], op1=ALU.add),
                             reads=[kpa, "d_xt"], writes=["d_xt"])

            for ti in range(L // TT):
                t0 = ti * TT
                if first:
                    for b4 in range(TT // 128):
                        P.op("sp", lambda e: e.dma_start(out=xtok[:], in_=xin[t0 + b4 * 128:t0 + (b4 + 1) * 128, :]),
                             writes=["d_xtok"], dma=True)
                        for half in range(2):
                            pn, kn = psN.next()
                            for q in range(4):
                                dc = half * 4 + q
                                P.op("pe", lambda e: e.matmul(pn[:, q * 128:(q + 1) * 128], lhsT=xtok[:, dc * 128:(dc + 1) * 128],
                                                              rhs=self.ident[:], start=True, stop=True),
                                     reads=["d_xtok", "ident"], writes=[kn])
                            P.op("act", lambda e: e.copy(out=xt[:, half * 4:half * 4 + 4, b4 * 128:(b4 + 1) * 128],
                                                         in_=pn[:].rearrange("p (a b) -> p a b", a=4)),
                                 reads=[kn], writes=["d_xt"])
                else:
                    P.op("sp", lambda e: e.dma_start(out=xt[:], in_=self.XT.rearrange("(dc p) t -> p dc t", p=128)[:, :, t0:t0 + TT]),
                         writes=["d_xt"], dma=True)
                    lay = l - 1
                    P.op("sp", lambda e: e.dma_start(out=xn[:], in_=self.YM.rearrange("(dc p) t -> p dc t", p=128)[:, :, t0:t0 + TT]),
                         writes=["d_xn"], dma=True)
                    for dc in range(8):
                        wp, kp = wpR.next()
                        P.op("sp", lambda e: e.dma_start(out=wp[:], in_=self.wb[("out", lay)][dc]), writes=[kp], dma=True)
                        for h in range(NH):
                            pa, kpa = psA.next()
                            for kc in range(8):
                                P.op("pe", lambda e: e.matmul(pa[:], lhsT=wp[:, kc, :], rhs=xn[:, kc, hs(h)], start=(kc == 0), stop=(kc == 7)),
                                     reads=[kp, "d_xn"], writes=[kpa])
                            P.op("dve", lambda e: e.tensor_tensor(out=xt[:, dc, hs(h)], in0=pa[:], in1=xt[:, dc, hs(h)], op=ALU.add),
                                 reads=[kpa, "d_xt"], writes=["d_xt"])
                    ffn(2, lay)
                if not last:
                    ffn(1, l)
                    normed(1 * DEPTH + l, xn, "d_xn")
                    cin = AB_IN if l % 2 == 0 else CD_IN
                    for ci in range((cin + 127) // 128):
                        w = min(128, cin - ci * 128)
                        wp, kp = wpR.next()
                        P.op("sp", lambda e: e.dma_start(out=wp[:], in_=self.wb[("in", l)][ci]), writes=[kp], dma=True)
                        for h in range(NH):
                            pa, kpa = psA.next()
                            for kc in range(8):
                                P.op("pe", lambda e: e.matmul(pa[0:w, :], lhsT=wp[:, kc, 0:w], rhs=xn[:, kc, hs(h)], start=(kc == 0), stop=(kc == 7)),
                                     reads=[kp, "d_xn"], writes=[kpa])
                            pj, kj = pjs.next()
                            P.op("act", lambda e: e.copy(out=pj[0:w, :], in_=pa[0:w, :]), reads=[kpa], writes=[kj])
                            P.op("pool", lambda e: e.dma_start(out=self.PJ[ci * 128:ci * 128 + w, t0 + h * 512:t0 + (h + 1) * 512], in_=pj[0:w, :]),
                                 reads=[kj], dma=True)
                    P.op("pool", lambda e: e.dma_start(out=self.XT.rearrange("(dc p) t -> p dc t", p=128)[:, :, t0:t0 + TT], in_=xt[:]),
                         reads=["d_xt"], dma=True)
                else:
                    rmsnorm()
                    for kc in range(8):
                        P.op("dve", lambda e: e.scalar_tensor_tensor(out=xt[:, kc, :], in0=xt[:, kc, :],
                                                                     scalar=self.gcols[:, 3 * DEPTH, kc:kc + 1], op0=ALU.mult,
                                                                     in1=rstd[:], op1=ALU.mult),
                             reads=["d_xt", "d_rstd", "gcols"], writes=["d_xt"])
                    for b4 in range(TT // 128):
                        for half in range(2):
                            pn, kn = psN.next()
                            for q in range(4):
                                dc = half * 4 + q
                                P.op("pe", lambda e: e.matmul(pn[:, q * 128:(q + 1) * 128], lhsT=xt[:, dc, b4 * 128:(b4 + 1) * 128],
                                                              rhs=self.ident[:], start=True, stop=True),
                                     reads=["d_xt", "ident"], writes=[kn])
                            P.op("act", lambda e: e.copy(out=xtok[:, half * 512:(half + 1) * 512], in_=pn[:]),
                                 reads=[kn], writes=["d_xtok"])
                        P.op("pool", lambda e: e.dma_start(out=yout[t0 + b4 * 128:t0 + (b4 + 1) * 128, :], in_=xtok[:]),
                             reads=["d_xtok"], dma=True)


_CACHE = {}


def _host_layout(inputs):
    f = lambda a: np.ascontiguousarray(np.asarray(a, dtype=np.float32))
    shared = {}
    for nm in ("ffn1_w_gate", "ffn1_w_up", "ffn1_w_down", "ffn2_w_gate", "ffn2_w_up", "ffn2_w_down",
               "ab_w_in", "ab_w_out", "cd_w_in", "cd_w_out"):
        shared[nm] = f(inputs[nm])
    norms = np.concatenate([f(inputs["ffn1_norm"]), f(inputs["mix_norm"]), f(inputs["ffn2_norm"]),
                            f(inputs["final_norm"])[None, :]], axis=0)
    shared["norms"] = np.ascontiguousarray(norms.reshape(3 * DEPTH + 1, 8, 128).transpose(2, 0, 1))
    cw, cb = f(inputs["lru_conv_w"]), f(inputs["lru_conv_b"])
    lam, ba, bx = f(inputs["lru_lambda"]), f(inputs["lru_ba"]), f(inputs["lru_bx"])
    colv = [cw[:, 0], cw[:, 1], cw[:, 2], cw[:, 3], cb]
    for d in range(2):
        colv += [lam[:, d], ba[:, d], bx[:, d]]
    colv = np.stack(colv, axis=-1)
    shared["lru_cols"] = np.ascontiguousarray(colv.reshape(2, 4, 128, 11).transpose(0, 2, 1, 3))
    wa, wx = f(inputs["lru_wa"]), f(inputs["lru_wx"])
    wbd = np.zeros((2, 4, 128, 4, 128), np.float32)
    for d in range(2):
        for wi, wsrc in enumerate((wa, wx)):
            for ct in range(4):
                for hh in range(2):
                    wbd[:, ct, hh * 64:(hh + 1) * 64, 2 * d + wi, hh * 64:(hh + 1) * 64] = wsrc[:, d, 2 * ct + hh]
    shared["lru_wbd"] = wbd
    lre, lim, lst = f(inputs["s5_lambda_re"]), f(inputs["s5_lambda_im"]), f(inputs["s5_log_step"])
    prm = np.stack([lre, lim, np.broadcast_to(lst[..., None], lre.shape)], axis=-1)
    prm = prm.reshape(2, 2, 16, 2, 64, 3).transpose(0, 3, 4, 2, 1, 5)
    shared["s5_prm"] = np.ascontiguousarray(prm.reshape(2, 128, 32, 3))
    bre, bim = f(inputs["s5_b_re"]), f(inputs["s5_b_im"])
    bt = np.zeros((2, 16, 128, 2, 2, 128), np.float32)
    for ri, bsrc in enumerate((bre, bim)):
        for g in range(32):
            T, gp, gl = g // 2, (g % 8) // 2, g % 2
            r0 = gp * 32 + gl * 16
            bt[:, T, r0:r0 + 16, :, ri, gl * 64:(gl + 1) * 64] = bsrc[:, :, g].transpose(0, 3, 1, 2)
    shared["s5_bt"] = bt
    cre_, cim_ = f(inputs["s5_c_re"]), f(inputs["s5_c_im"])
    ct = np.zeros((2, 16, 2, 128, 2, 32), np.float32)
    for ri, csrc in enumerate((cre_, cim_)):
        for g in range(32):
            T, gl = g // 2, g % 2
            ct[:, T, :, gl * 64:(gl + 1) * 64, ri, gl * 16:(gl + 1) * 16] = csrc[:, :, g].transpose(0, 1, 3, 2)
    shared["s5_ct"] = ct
    pc = np.stack([f(inputs["s5_d"]), f(inputs["s5_glu_b"])], axis=-1)
    shared["s5_pc"] = np.ascontiguousarray(pc.reshape(2, 4, 128, 2).transpose(0, 2, 1, 3))
    shared["s5_glu_w"] = f(inputs["s5_glu_w"])
    mu = f(inputs["rw_mu"])
    mut = np.zeros((2, 14, 128), np.float32)
    mut[:, 0:12] = mu[:, 0:1536].reshape(2, 12, 128)
    mut[:, 12, 0:96] = mu[:, 1536:1632]
    mut[:, 13, 0:64] = mu[:, 1632:1696]
    shared["rw_mu"] = np.ascontiguousarray(mut.transpose(0, 2, 1))
    w0 = f(inputs["rw_w0"])
    pcs = np.stack([w0[:, 0], w0[:, 1], f(inputs["rw_a0"]), f(inputs["rw_k_k"]), f(inputs["rw_k_a"]),
                    f(inputs["rw_r_k"]).reshape(2, 512), f(inputs["rw_ln_w"]), f(inputs["rw_ln_b"])], axis=-1)
    shared["rw_pc"] = np.ascontiguousarray(pcs.reshape(2, 4, 128, 8).transpose(0, 2, 1, 3))
    lwt = np.zeros((2, 128, 2, 512), np.float32)
    wup = f(inputs["rw_w_up"])
    lwt[:, 0:32, 0] = wup[:, 0]
    lwt[:, 32:64, 0] = wup[:, 1]
    lwt[:, 64:96, 0] = f(inputs["rw_a_up"])
    lwt[:, 0:64, 1] = f(inputs["rw_g_up"])
    shared["rw_lw"] = lwt
    hcw, hcb = f(inputs["hy_conv_w"]), f(inputs["hy_conv_b"])
    hcols = np.stack([hcw[:, 0], hcw[:, 1], hcw[:, 2], hcb], axis=-1)
    shared["hy_cols"] = np.ascontiguousarray(hcols.reshape(2, 12, 128, 4).transpose(0, 2, 1, 3))
    shared["hy_w1"] = f(inputs["hy_f_w1"]); shared["hy_w2"] = f(inputs["hy_f_w2"]); shared["hy_w3"] = f(inputs["hy_f_w3"])
    fq = f(inputs["hy_f_freq"])
    z64 = np.zeros((2, 64), np.float32)
    shared["hy_fcols"] = np.ascontiguousarray(np.stack([f(inputs["hy_f_b1"]), f(inputs["hy_f_b2"]), fq[:, 0], fq[:, 1], z64, z64], axis=-1))
    hbias = f(inputs["hy_bias"])
    shared["hy_bias"] = np.ascontiguousarray(hbias.reshape(2, 2, 4, 128).transpose(0, 3, 2, 1))
    return shared


def _hy_consts(L):
    t = np.linspace(0.0, 1.0, L, dtype=np.float32)[:, None]
    bands = 16
    freqs = np.linspace(1e-4, bands - 1, bands, dtype=np.float32)[None, :]
    wpos = (np.float32(2.0 * math.pi / L) * np.arange(L, dtype=np.float32))[:, None]
    z = np.concatenate([t, np.cos(freqs * wpos), -np.sin(freqs * wpos)], axis=-1).astype(np.float32)
    ntcol = np.ascontiguousarray((-t[:, 0]).reshape(L // 128, 128).T)
    return np.ascontiguousarray(z.T), ntcol


def _hy_drow():
    max_decay = math.log(1e-2) / 0.3
    min_decay = math.log(1e-2) / 1.5
    deltas = np.abs(np.linspace(min_decay, max_decay, 512, dtype=np.float32))
    return np.ascontiguousarray(np.broadcast_to(deltas[None, :], (128, 512))).astype(np.float32)


def run(inputs, LP, LS, enable, depth=DEPTH, ncores=8):
    key = (LP, LS, tuple(sorted(enable.items())), depth)
    if key not in _CACHE:
        _CACHE[key] = Builder(LP, LS, enable, depth).build()
    nc = _CACHE[key]
    shared = _host_layout(inputs)
    shared["hy_drow"] = _hy_drow()
    for LL in sorted(set((LP, LS))):
        zT, ntc = _hy_consts(LL)
        shared["hy_zpos_%d" % LL] = zT
        shared["hy_ntcol_%d" % LL] = ntc
    xp = np.asarray(inputs["x_prompt"], dtype=np.float32)
    xs = np.asarray(inputs["x_sample"], dtype=np.float32)
    in_maps = []
    for c in range(ncores):
        m = dict(shared)
        m["xp"] = np.ascontiguousarray(xp[c])
        m["xs"] = np.ascontiguousarray(xs[c % 2])
        in_maps.append(m)
    res = run_bass_kernel_spmd(nc, in_maps, core_ids=list(range(ncores)))
    yp = np.stack([res.results[c]["yp"] for c in range(ncores)], axis=0).astype(np.float32)
    ys = np.stack([res.results[c]["ys"] for c in range(2)], axis=0).astype(np.float32)
    return yp, ys


ENABLE = dict(s5=True, rwkv=True, lru=True, hyena=True)


def kernel(**inputs):
    return run(inputs, 4096, 8192, ENABLE)
```

```python
import math
from contextlib import ExitStack
import numpy as np
import concourse.bass as bass
import concourse.mybir as mybir
from concourse.bass_utils import run_bass_kernel_spmd

F32 = mybir.dt.float32
BF16 = mybir.dt.bfloat16
I32 = mybir.dt.int32
ALU = mybir.AluOpType
AF = mybir.ActivationFunctionType
AX = mybir.AxisListType

D = 1024
DFF = 2816
DEPTH = 4
AB_IN = 2208
CD_IN = 2560
TT = 1024
EPS = 1e-6
MAGIC = 12582912.0
TWO_PI = 2.0 * math.pi
TBMAX = 2048


class Prog:
    ENG = ("pe", "act", "dve", "pool", "sp")
    NRING = 8
    LOOKBACK = 6

    def __init__(self, nc, stack):
        self.nc = nc
        self.cnt = {e: 0 for e in self.ENG}
        self.ringn = {e: 0 for e in self.ENG}
        self.waited = {e: {} for e in self.ENG}
        self.lastw = {}
        self.readers = {}
        self.nins = 0
        self.cumc = {e: [0] for e in self.ENG}
        self.engobj = {"pe": nc.tensor, "act": nc.scalar, "dve": nc.vector, "pool": nc.gpsimd, "sp": nc.sync}
        self.sem = {e: stack.enter_context(nc.semaphore("s_" + e)) for e in self.ENG}
        self.ring = {e: [stack.enter_context(nc.semaphore("r_%s%d" % (e, i))) for i in range(self.NRING)]
                     for e in ("sp", "act", "pool")}

    def _semh(self, semk):
        if semk[0] == "eng":
            return self.sem[semk[1]]
        return self.ring[semk[1]][semk[2]]

    def _need(self, eng, tok, waits, war=False):
        if tok is None:
            return
        semk, val, teng, isdma = tok
        if (not isdma) and teng == eng:
            if val <= self.cnt[eng] - self.LOOKBACK:
                return
            if war and eng != "pool":
                return
            cc = self.cumc[eng]
            if cc[self.cnt[eng]] - cc[val] >= 550:
                return
        if self.waited[eng].get(semk, 0) >= val:
            return
        if waits.get(semk, 0) < val:
            waits[semk] = val

    def op(self, eng, fn, reads=(), writes=(), dma=False, cost=0):
        waits = {}
        for k in reads:
            self._need(eng, self.lastw.get(k), waits)
        for k in writes:
            self._need(eng, self.lastw.get(k), waits, war=True)
            for t in self.readers.get(k, {}).values():
                self._need(eng, t, waits, war=True)
        if dma:
            i = self.ringn[eng]
            self.ringn[eng] += 1
            slot = i % self.NRING
            val = 16 * (i // self.NRING + 1)
            semk = ("ring", eng, slot)
            if i >= self.NRING:
                self._need(eng, (semk, val - 16, eng, True), waits)
            tok = (semk, val, eng, True)
        else:
            self.cnt[eng] += 1
            self.cumc[eng].append(self.cumc[eng][-1] + cost)
            tok = (("eng", eng), self.cnt[eng], eng, False)
        e = self.engobj[eng]
        for semk, val in waits.items():
            self.waited[eng][semk] = val
            e.wait_ge(self._semh(semk), val)
        ins = fn(e)
        ins.then_inc(self._semh(tok[0]), 16 if dma else 1)
        self.nins += 1 + len(waits)
        for k in reads:
            self.readers.setdefault(k, {})[(eng, tok[0] if dma else 0)] = tok
        for k in writes:
            self.lastw[k] = tok
            self.readers[k] = {}
        return tok

    def _all_tokens(self):
        toks = []
        for q in self.ENG:
            if self.cnt[q] > 0:
                toks.append((("eng", q), self.cnt[q], q, False))
        for q in ("sp", "act", "pool"):
            n = self.ringn[q]
            for slot in range(min(n, self.NRING)):
                c = (n - 1 - slot) // self.NRING + 1
                toks.append((("ring", q, slot), 16 * c, q, True))
        return toks

    def barrier(self, engs=None):
        toks = self._all_tokens()
        for eng in (engs or self.ENG):
            e = self.engobj[eng]
            for semk, val, teng, isdma in toks:
                if self.waited[eng].get(semk, 0) >= val:
                    continue
                if (not isdma) and teng == eng:
                    continue
                self.waited[eng][semk] = val
                e.wait_ge(self._semh(semk), val)
                self.nins += 1
        if engs is None:
            self.lastw = {}
            self.readers = {}


class Rot:
    def __init__(self, bufs, name):
        self.bufs = bufs
        self.name = name
        self.i = 0

    def next(self):
        j = self.i % len(self.bufs)
        self.i += 1
        return self.bufs[j], (self.name, j)


def _wlayout(nco, R, cw):
    return [nco, 128, R // 128, cw]


class Builder:
    def __init__(self, LP, LS, enable, depth=DEPTH):
        self.LP, self.LS, self.enable, self.depth = LP, LS, enable, depth
        self.Lmax = max(LP, LS)
        self.nc = bass.Bass("TRN2", target_bir_lowering=False)
        self.inputs = {}
        self.scr = {}
        self.uid = 0

    def sbt(self, stack, name, shape, dt=F32):
        self.uid += 1
        return stack.enter_context(self.nc.sbuf_tensor("%s_u%d" % (name, self.uid), list(shape), dt))

    def din(self, name, shape, dt=F32):
        t = self.nc.dram_tensor(name, list(shape), dt, kind="ExternalInput").ap()
        self.inputs[name] = t
        return t

    def dscr(self, name, shape, dt=F32):
        t = self.nc.dram_tensor(name, list(shape), dt).ap()
        self.scr[name] = t
        return t

    def build(self):
        nc = self.nc
        LP, LS, Lmax = self.LP, self.LS, self.Lmax
        self.xp = self.din("xp", [LP, D])
        self.xs = self.din("xs", [LS, D])
        self.yp = nc.dram_tensor("yp", [LP, D], F32, kind="ExternalOutput").ap()
        self.ys = nc.dram_tensor("ys", [LS, D], F32, kind="ExternalOutput").ap()
        self.w_raw = {}
        for nm in ("ffn1_w_gate", "ffn1_w_up", "ffn2_w_gate", "ffn2_w_up"):
            self.w_raw[nm] = self.din(nm, [DEPTH, D, DFF])
        for nm in ("ffn1_w_down", "ffn2_w_down"):
            self.w_raw[nm] = self.din(nm, [DEPTH, DFF, D])
        self.w_raw["ab_w_in"] = self.din("ab_w_in", [2, D, AB_IN])
        self.w_raw["cd_w_in"] = self.din("cd_w_in", [2, D, CD_IN])
        self.w_raw["ab_w_out"] = self.din("ab_w_out", [2, D, D])
        self.w_raw["cd_w_out"] = self.din("cd_w_out", [2, D, D])
        self.norms = self.din("norms", [128, 3 * DEPTH + 1, 8])
        self.lru_cols = self.din("lru_cols", [2, 128, 4, 11])
        self.s5_prm = self.din("s5_prm", [2, 128, 32, 3])
        self.rw_mu = self.din("rw_mu", [2, 128, 14])
        self.hy_cols = self.din("hy_cols", [2, 128, 12, 4])
        self.hy_w1 = self.din("hy_w1", [2, 33, 64])
        self.hy_w2 = self.din("hy_w2", [2, 64, 64])
        self.hy_w3 = self.din("hy_w3", [2, 64, 2048])
        self.hy_fcols = self.din("hy_fcols", [2, 64, 6])
        self.hy_bias = self.din("hy_bias", [2, 128, 4, 2])
        self.hy_drow = self.din("hy_drow", [128, 512])
        self.hy_zpos = {}
        self.hy_ntcol = {}
        for LL in sorted(set((LP, LS))):
            self.hy_zpos[LL] = self.din("hy_zpos_%d" % LL, [33, LL])
            self.hy_ntcol[LL] = self.din("hy_ntcol_%d" % LL, [128, LL // 128])
        self.rw_pc = self.din("rw_pc", [2, 128, 4, 8])
        self.rw_lw = self.din("rw_lw", [2, 128, 2, 512])
        self.s5_bt = self.din("s5_bt", [2, 16, 128, 2, 2, 128])
        self.s5_ct = self.din("s5_ct", [2, 16, 2, 128, 2, 32])
        self.s5_pc = self.din("s5_pc", [2, 128, 4, 2])
        self.w_raw["s5_glu_w"] = self.din("s5_glu_w", [2, 512, 512])
        self.lru_wbd = self.din("lru_wbd", [2, 4, 128, 4, 128])
        self.wb = {}
        for l in range(DEPTH):
            for f in (1, 2):
                self.wb[("g", f, l)] = self.dscr("wg%d_%d" % (f, l), _wlayout(11, D, 256), BF16)
                self.wb[("u", f, l)] = self.dscr("wu%d_%d" % (f, l), _wlayout(11, D, 256), BF16)
                self.wb[("d", f, l)] = self.dscr("wd%d_%d" % (f, l), _wlayout(8, DFF, 128), BF16)
            cin = AB_IN if l % 2 == 0 else CD_IN
            self.wb[("in", l)] = self.dscr("win_%d" % l, _wlayout((cin + 127) // 128, D, 128), BF16)
            self.wb[("out", l)] = self.dscr("wout_%d" % l, _wlayout(8, D, 128), BF16)
        for jj in range(2):
            self.wb[("glu", jj)] = self.dscr("wglu_%d" % jj, _wlayout(4, 512, 128), BF16)
        self.YS5 = self.dscr("YS5", [2, 512, Lmax])
        self.RWTM = self.dscr("RWTM", [7, Lmax, 512])
        self.HYC = self.dscr("HYC", [3, 512, Lmax])
        self.HYZ = self.dscr("HYZ", [512, Lmax])
        self.HYK = self.dscr("HYK", [Lmax, 2048], BF16)
        self.HYF = self.dscr("HYF", [2, 2, Lmax // 128, 128, 2, 256])
        self.HYRN = self.dscr("HYRN", [128, 1024])
        self.dft = {}
        self.YTM = self.dscr("YTM", [2, Lmax, 512])
        self.RWG = self.dscr("RWG", [512, Lmax])
        self.RWB = self.dscr("RWB", [512, Lmax])
        self.XT = self.dscr("XT", [D, Lmax])
        self.PJ = self.dscr("PJ", [CD_IN, Lmax])
        self.YM = self.dscr("YM", [D, Lmax], BF16)

        with ExitStack() as st:
            self.P = Prog(nc, st)
            self.st = st
            self.ident = st.enter_context(nc.sbuf_tensor("ident", [128, 128], F32))
            self.ones_bf = st.enter_context(nc.sbuf_tensor("ones_bf", [128, 128], BF16))
            self.gcols = st.enter_context(nc.sbuf_tensor("gcols", [128, 3 * DEPTH + 1, 8], F32))
            self.psum = [st.enter_context(nc.psum_tensor("ps%d" % i, [128, 512], F32)) for i in range(8)]
            self.setup_consts()
            self.cast_weights()
            self.P.barrier()
            if self.enable.get("hyena"):
                for LL in sorted(set((LP, LS))):
                    self.gen_dft(LL)
            for (xin, yout, L, tag) in ((self.xp, self.yp, LP, "P"), (self.xs, self.ys, LS, "S")):
                self.trunk(xin, yout, L, tag)
                self.P.barrier()
            self.P.barrier(engs=["sp"])
        return nc

    def setup_consts(self):
        nc, P = self.nc, self.P
        with ExitStack() as s2:
            io = self.sbt(s2, "c_io", [128, 128], I32)
            iof = self.sbt(s2, "c_iof", [128, 128], F32)
            P.op("pool", lambda e: e.iota(io[:], pattern=[[1, 128]], base=0, channel_multiplier=-1), writes=["c_io"])
            P.op("dve", lambda e: e.tensor_copy(out=iof[:], in_=io[:]), reads=["c_io"], writes=["c_iof"])
            P.op("dve", lambda e: e.tensor_scalar(out=self.ident[:], in0=iof[:], scalar1=0.0, scalar2=None,
                                                  op0=ALU.is_equal), reads=["c_iof"], writes=["ident"])
            P.op("dve", lambda e: e.memset(self.ones_bf[:], 1.0), writes=["ones_bf"])
            P.op("sp", lambda e: e.dma_start(out=self.gcols[:], in_=self.norms[:, :, :]), writes=["gcols"], dma=True)
            P.barrier()

    def cast_weights(self):
        nc, P = self.nc, self.P
        with ExitStack() as s2:
            CW = 2048
            stg = Rot([self.sbt(s2, "cw_s%d" % i, [128, CW], F32) for i in range(3)], "cw_s")
            stb = Rot([self.sbt(s2, "cw_b%d" % i, [128, CW], BF16) for i in range(3)], "cw_b")
            self._cast_n = 0

            def cast(src, dst, R, C, cw):
                nco = dst.shape[0]
                for kc in range(R // 128):
                    c0 = 0
                    while c0 < C:
                        wc = min(CW, C - c0)
                        wc_pad = ((wc + cw - 1) // cw) * cw
                        a, ka = stg.next()
                        b, kb = stb.next()
                        P.op("sp", lambda e: e.dma_start(out=a[:, 0:wc], in_=src[kc * 128:(kc + 1) * 128, c0:c0 + wc]),
                             writes=[ka], dma=True)
                        if wc_pad != wc:
                            P.op("pool", lambda e: e.memset(b[:, wc:wc_pad], 0.0), writes=[kb])
                        eng = ("act", "dve", "pool")[self._cast_n % 3]
                        self._cast_n += 1
                        if eng == "act":
                            P.op("act", lambda e: e.copy(out=b[:, 0:wc], in_=a[:, 0:wc]), reads=[ka], writes=[kb])
                        else:
                            P.op(eng, lambda e: e.tensor_copy(out=b[:, 0:wc], in_=a[:, 0:wc]), reads=[ka], writes=[kb])
                        co0 = c0 // cw
                        nn = wc_pad // cw
                        dview = dst.rearrange("co p kc j -> p co kc j")[:, co0:co0 + nn, kc, :]
                        P.op("pool", lambda e: e.dma_start(out=dview, in_=b[:, 0:wc_pad].rearrange("p (co j) -> p co j", j=cw)),
                             reads=[kb], dma=True)
                        c0 += wc

            for l in range(self.depth):
                for f in (1, 2):
                    cast(self.w_raw["ffn%d_w_gate" % f][l], self.wb[("g", f, l)], D, DFF, 256)
                    cast(self.w_raw["ffn%d_w_up" % f][l], self.wb[("u", f, l)], D, DFF, 256)
                    cast(self.w_raw["ffn%d_w_down" % f][l], self.wb[("d", f, l)], DFF, D, 128)
                j = l // 2
                if l % 2 == 0:
                    cast(self.w_raw["s5_glu_w"][j], self.wb[("glu", j)], 512, 512, 128)
                    cast(self.w_raw["ab_w_in"][j], self.wb[("in", l)], D, AB_IN, 128)
                    cast(self.w_raw["ab_w_out"][j], self.wb[("out", l)], D, D, 128)
                else:
                    cast(self.w_raw["cd_w_in"][j], self.wb[("in", l)], D, CD_IN, 128)
                    cast(self.w_raw["cd_w_out"][j], self.wb[("out", l)], D, D, 128)

    def trunk(self, xin, yout, L, tag):
        for l in range(self.depth + 1):
            self.dense_pass(xin, yout, L, l)
            self.P.barrier()
            if l < self.depth:
                self.mixer_pass(L, l)
                self.P.barrier()

    def zero_rows(self, L, r0, r1):
        nc, P = self.nc, self.P
        with ExitStack() as s2:
            z = self.sbt(s2, "mz", [128, 2048], BF16)
            P.op("dve", lambda e: e.memset(z[:], 0.0), writes=["mz"])
            for rc in range(r0 // 128, r1 // 128):
                for t0 in range(0, L, 2048):
                    w = min(2048, L - t0)
                    P.op("pool", lambda e: e.dma_start(out=self.YM[rc * 128:(rc + 1) * 128, t0:t0 + w], in_=z[:, 0:w]),
                         reads=["mz"], dma=True)
            P.barrier()

    def mixer_pass(self, L, l):
        j = l // 2
        if l % 2 == 0:
            if self.enable.get("s5"):
                self.s5(L, j)
            else:
                self.zero_rows(L, 0, 512)
            self.P.barrier()
            if self.enable.get("rwkv"):
                self.rwkv(L, j)
            else:
                self.zero_rows(L, 512, 1024)
        else:
            if self.enable.get("lru"):
                self.lru(L, j)
            else:
                self.zero_rows(L, 0, 512)
            self.P.barrier()
            if self.enable.get("hyena"):
                self.hyena(L, j)
            else:
                self.zero_rows(L, 512, 1024)

    def lru(self, L, j):
        nc, P = self.nc, self.P
        TB = min(L, TBMAX)
        nb = L // TB
        with ExitStack() as s2:
            def sb(name, shape, dt=F32):
                return self.sbt(s2, name, shape, dt)
            cols = sb("l_cols", [128, 4, 11])
            cs = sb("l_cs", [128, 4, 2])
            wbd = sb("l_wbd", [128, 4, 128])
            HS = sb("l_HS", [128, L])
            xpad = sb("l_xpad", [128, TB + 3])
            xc = sb("l_xc", [128, TB])
            gr = sb("l_gr", [128, TB])
            gi = sb("l_gi", [128, TB])
            a_ = sb("l_a", [128, TB])
            om = sb("l_om", [128, TB])
            hR = Rot([sb("l_h%d" % i, [128, TB]) for i in range(2)], "l_h")
            gb = sb("l_gb", [128, TB])
            yo = sb("l_yo", [128, TB], BF16)
            psA = Rot(self.psum[0:4], "psA")
            P.op("sp", lambda e: e.dma_start(out=cols[:], in_=self.lru_cols[j]), writes=["l_cols"], dma=True)
            for d in range(2):
                P.op("act", lambda e: e.activation(out=cs[:, :, d], in_=cols[:, :, 5 + 3 * d], func=AF.Exp, scale=-1.0),
                     reads=["l_cols"], writes=["l_cs"])
                P.op("act", lambda e: e.activation(out=cs[:, :, d], in_=cs[:, :, d], func=AF.Ln, bias=1.0),
                     reads=["l_cs"], writes=["l_cs"])
            P.op("dve", lambda e: e.tensor_scalar(out=cs[:].rearrange("p a b -> p (a b)"), in0=cs[:].rearrange("p a b -> p (a b)"),
                                                  scalar1=-8.0, scalar2=None, op0=ALU.mult), reads=["l_cs"], writes=["l_cs"])
            for ct in range(4):
                P.op("sp", lambda e: e.dma_start(out=wbd[:], in_=self.lru_wbd[j, ct]), writes=["l_wbd"], dma=True)
                for d in range(2):
                    carry = 0.0
                    ckey = None
                    blocks = list(range(nb)) if d == 0 else list(range(nb - 1, -1, -1))
                    for bi in blocks:
                        t0 = bi * TB
                        lo = max(t0 - 2, 0)
                        hi = min(t0 + TB + 1, L)
                        if t0 == 0:
                            P.op("pool", lambda e: e.memset(xpad[:, 0:2], 0.0), writes=["l_xpad"])
                        if t0 + TB == L:
                            P.op("pool", lambda e: e.memset(xpad[:, TB + 2:TB + 3], 0.0), writes=["l_xpad"])
                        P.op("sp", lambda e: e.dma_start(out=xpad[:, lo - (t0 - 2):hi - (t0 - 2)],
                                                         in_=self.PJ[ct * 128:(ct + 1) * 128, lo:hi]), writes=["l_xpad"], dma=True)
                        P.op("dve", lambda e: e.tensor_scalar(out=xc[:], in0=xpad[:, 0:TB], scalar1=cols[:, ct, 0:1],
                                                              scalar2=cols[:, ct, 4:5], op0=ALU.mult, op1=ALU.add),
                             reads=["l_xpad", "l_cols"], writes=["l_xc"])
                        for q in range(1, 4):
                            P.op("dve", lambda e: e.scalar_tensor_tensor(out=xc[:], in0=xpad[:, q:q + TB], scalar=cols[:, ct, q:q + 1],
                                                                         op0=ALU.mult, in1=xc[:], op1=ALU.add),
                                 reads=["l_xpad", "l_xc"], writes=["l_xc"])
                        for sbk in range(TB // 512):
                            sl = slice(sbk * 512, (sbk + 1) * 512)
                            for which, dst, bcol, key in ((0, gr, 6 + 3 * d, "l_gr"), (1, gi, 7 + 3 * d, "l_gi")):
                                pa, kpa = psA.next()
                                P.op("pe", lambda e: e.matmul(pa[:], lhsT=wbd[:, 2 * d + which, :], rhs=xc[:, sl], start=True, stop=True),
                                     reads=["l_wbd", "l_xc"], writes=[kpa])
                                P.op("act", lambda e: e.activation(out=dst[:, sl], in_=pa[:], func=AF.Sigmoid,
                                                                   bias=cols[:, ct, bcol:bcol + 1]),
                                     reads=[kpa, "l_cols"], writes=[key])
                        P.op("act", lambda e: e.activation(out=a_[:], in_=gr[:], func=AF.Exp, scale=cs[:, ct, d:d + 1]),
                             reads=["l_gr", "l_cs"], writes=["l_a"])
                        P.op("pool", lambda e: e.tensor_tensor(out=om[:], in0=a_[:], in1=a_[:], op=ALU.mult), reads=["l_a"], writes=["l_om"])
                        P.op("pool", lambda e: e.tensor_scalar(out=om[:], in0=om[:], scalar1=-1.0, scalar2=1.0, op0=ALU.mult, op1=ALU.add),
                             reads=["l_om"], writes=["l_om"])
                        P.op("pool", lambda e: e.tensor_scalar(out=om[:], in0=om[:], scalar1=1e-30, scalar2=None, op0=ALU.max),
                             reads=["l_om"], writes=["l_om"])
                        P.op("act", lambda e: e.activation(out=om[:], in_=om[:], func=AF.Sqrt), reads=["l_om"], writes=["l_om"])
                        P.op("dve", lambda e: e.tensor_tensor(out=om[:], in0=om[:], in1=gi[:], op=ALU.mult), reads=["l_om", "l_gi"], writes=["l_om"])
                        P.op("dve", lambda e: e.tensor_tensor(out=om[:], in0=om[:], in1=xc[:], op=ALU.mult), reads=["l_om", "l_xc"], writes=["l_om"])
                        rk = ["l_a", "l_om"] + ([ckey] if ckey else [])
                        if d == 0:
                            P.op("dve", lambda e: e.tensor_tensor_scan(out=HS[:, t0:t0 + TB], data0=a_[:], data1=om[:], initial=carry,
                                                                       op0=ALU.mult, op1=ALU.add), reads=rk, writes=[("l_HS", bi)])
                            carry = HS[:, t0 + TB - 1:t0 + TB]
                            ckey = ("l_HS", bi)
                        else:
                            h, kh = hR.next()
                            P.op("dve", lambda e: e.tensor_tensor_scan(out=h[:, ::-1], data0=a_[:, ::-1], data1=om[:, ::-1], initial=carry,
                                                                       op0=ALU.mult, op1=ALU.add), reads=rk, writes=[kh])
                            carry = h[:, 0:1]
                            ckey = kh
                            P.op("sp", lambda e: e.dma_start(out=gb[:], in_=self.PJ[512 + ct * 128:512 + (ct + 1) * 128, t0:t0 + TB]),
                                 writes=["l_gb"], dma=True)
                            P.op("act", lambda e: e.activation(out=gb[:], in_=gb[:], func=AF.Gelu_apprx_tanh), reads=["l_gb"], writes=["l_gb"])
                            P.op("pool", lambda e: e.tensor_tensor(out=gr[:], in0=h[:], in1=HS[:, t0:t0 + TB], op=ALU.add),
                                 reads=[kh, ("l_HS", bi)], writes=["l_gr"])
                            P.op("pool", lambda e: e.tensor_tensor(out=yo[:], in0=gr[:], in1=gb[:], op=ALU.mult),
                                 reads=["l_gr", "l_gb"], writes=["l_yo"])
                            P.op("pool", lambda e: e.dma_start(out=self.YM[ct * 128:(ct + 1) * 128, t0:t0 + TB], in_=yo[:]),
                                 reads=["l_yo"], dma=True)

    def _range_reduce(self, t, tmp, key, tkey, n):
        P = self.P
        P.op("dve", lambda e: e.tensor_scalar(out=tmp, in0=t, scalar1=1.0 / TWO_PI, scalar2=MAGIC, op0=ALU.mult, op1=ALU.add),
             reads=[key], writes=[tkey])
        P.op("dve", lambda e: e.tensor_scalar(out=tmp, in0=tmp, scalar1=MAGIC, scalar2=-TWO_PI, op0=ALU.subtract, op1=ALU.mult),
             reads=[tkey], writes=[tkey])
        P.op("dve", lambda e: e.tensor_tensor(out=t, in0=t, in1=tmp, op=ALU.add), reads=[key, tkey], writes=[key])
        P.op("dve", lambda e: e.tensor_scalar(out=t, in0=t, scalar1=3.14159, scalar2=-3.14159, op0=ALU.min, op1=ALU.max),
             reads=[key], writes=[key])

    def s5(self, L, j):
        nc, P = self.nc, self.P
        TB = min(L, TBMAX)
        nb = L // TB
        YS = self.YS5
        with ExitStack() as s2:
            def sb(name, shape, dt=F32):
                return self.sbt(s2, name, shape, dt)
            NC = 32
            prm = sb("s_prm", [128, NC, 3])
            lr = sb("s_lr", [128, NC]); li = sb("s_li", [128, NC]); stp = sb("s_stp", [128, NC])
            mag = sb("s_mag", [128, NC]); ang = sb("s_ang", [128, NC]); angc = sb("s_angc", [128, NC]); tmpc = sb("s_tmpc", [128, NC])
            cth = sb("s_cth", [128, NC]); sth = sb("s_sth", [128, NC]); nsth = sb("s_nsth", [128, NC])
            are = sb("s_are", [128, NC]); aim = sb("s_aim", [128, NC]); den = sb("s_den", [128, NC])
            cre = sb("s_cre", [128, NC]); cim = sb("s_cim", [128, NC]); t1 = sb("s_t1", [128, NC]); t2 = sb("s_t2", [128, NC])
            ones = sb("s_ones", [128, TB])
            ucm = sb("s_ucm", [128, L])
            bt = sb("s_bt", [128, 2, 2, 128])
            ctl = sb("s_ct", [128, 2, 32])
            cpr = sb("s_cpr", [128, 2, 32])
            Ere = sb("s_Ere", [128, TB]); Eim = sb("s_Eim", [128, TB]); rt = sb("s_rt", [128, TB])
            xre = sb("s_xre", [128, TB]); xim = sb("s_xim", [128, TB])
            gre = sb("s_gre", [128, TB]); gim = sb("s_gim", [128, TB])
            ta = sb("s_ta", [128, TB]); tb = sb("s_tb", [128, TB])
            cw = sb("s_cw", [128, 2]); cw2 = sb("s_cw2", [128, 2]); ini = sb("s_ini", [128, 2]); tin = sb("s_tin", [128, 2])
            yev = Rot([sb("s_yev%d" % i, [32, 512]) for i in range(2)], "s_yev")
            psW = Rot(self.psum[0:4], "psW")
            psY = Rot(self.psum[4:6], "psY")
            P.op("sp", lambda e: e.dma_start(out=prm[:], in_=self.s5_prm[j]), writes=["s_prm"], dma=True)
            P.op("dve", lambda e: e.memset(ones[:], 1.0), writes=["s_ones"])
            P.op("dve", lambda e: e.tensor_scalar(out=lr[:], in0=prm[:, :, 0], scalar1=-1e-4, scalar2=None, op0=ALU.min), reads=["s_prm"], writes=["s_lr"])
            P.op("dve", lambda e: e.tensor_copy(out=li[:], in_=prm[:, :, 1]), reads=["s_prm"], writes=["s_li"])
            P.op("act", lambda e: e.activation(out=stp[:], in_=prm[:, :, 2], func=AF.Exp), reads=["s_prm"], writes=["s_stp"])
            P.op("dve", lambda e: e.tensor_tensor(out=mag[:], in0=lr[:], in1=stp[:], op=ALU.mult), reads=["s_lr", "s_stp"], writes=["s_mag"])
            P.op("act", lambda e: e.activation(out=mag[:], in_=mag[:], func=AF.Exp), reads=["s_mag"], writes=["s_mag"])
            P.op("dve", lambda e: e.tensor_tensor(out=ang[:], in0=li[:], in1=stp[:], op=ALU.mult), reads=["s_li", "s_stp"], writes=["s_ang"])
            P.op("dve", lambda e: e.tensor_scalar(out=angc[:], in0=ang[:], scalar1=0.5 * math.pi, scalar2=None, op0=ALU.add), reads=["s_ang"], writes=["s_angc"])
            self._range_reduce(ang[:], tmpc[:], "s_ang", "s_tmpc", NC)
            self._range_reduce(angc[:], tmpc[:], "s_angc", "s_tmpc", NC)
            P.op("act", lambda e: e.activation(out=sth[:], in_=ang[:], func=AF.Sin), reads=["s_ang"], writes=["s_sth"])
            P.op("act", lambda e: e.activation(out=cth[:], in_=angc[:], func=AF.Sin), reads=["s_angc"], writes=["s_cth"])
            P.op("dve", lambda e: e.tensor_scalar(out=nsth[:], in0=sth[:], scalar1=-1.0, scalar2=None, op0=ALU.mult), reads=["s_sth"], writes=["s_nsth"])
            P.op("dve", lambda e: e.tensor_tensor(out=are[:], in0=mag[:], in1=cth[:], op=ALU.mult), reads=["s_mag", "s_cth"], writes=["s_are"])
            P.op("dve", lambda e: e.tensor_tensor(out=aim[:], in0=mag[:], in1=sth[:], op=ALU.mult), reads=["s_mag", "s_sth"], writes=["s_aim"])
            P.op("dve", lambda e: e.tensor_tensor(out=den[:], in0=lr[:], in1=lr[:], op=ALU.mult), reads=["s_lr"], writes=["s_den"])
            P.op("dve", lambda e: e.tensor_tensor(out=t1[:], in0=li[:], in1=li[:], op=ALU.mult), reads=["s_li"], writes=["s_t1"])
            P.op("dve", lambda e: e.tensor_tensor(out=den[:], in0=den[:], in1=t1[:], op=ALU.add), reads=["s_den", "s_t1"], writes=["s_den"])
            P.op("dve", lambda e: e.reciprocal(out=den[:], in_=den[:]), reads=["s_den"], writes=["s_den"])
            P.op("dve", lambda e: e.tensor_scalar(out=are[:], in0=are[:], scalar1=-1.0, scalar2=None, op0=ALU.add), reads=["s_are"], writes=["s_are"])
            P.op("dve", lambda e: e.tensor_tensor(out=t1[:], in0=are[:], in1=lr[:], op=ALU.mult), reads=["s_are", "s_lr"], writes=["s_t1"])
            P.op("dve", lambda e: e.tensor_tensor(out=t2[:], in0=aim[:], in1=li[:], op=ALU.mult), reads=["s_aim", "s_li"], writes=["s_t2"])
            P.op("dve", lambda e: e.tensor_tensor(out=t1[:], in0=t1[:], in1=t2[:], op=ALU.add), reads=["s_t1", "s_t2"], writes=["s_t1"])
            P.op("dve", lambda e: e.tensor_tensor(out=cre[:], in0=t1[:], in1=den[:], op=ALU.mult), reads=["s_t1", "s_den"], writes=["s_cre"])
            P.op("dve", lambda e: e.tensor_tensor(out=t1[:], in0=aim[:], in1=lr[:], op=ALU.mult), reads=["s_aim", "s_lr"], writes=["s_t1"])
            P.op("dve", lambda e: e.tensor_tensor(out=t2[:], in0=are[:], in1=li[:], op=ALU.mult), reads=["s_are", "s_li"], writes=["s_t2"])
            P.op("dve", lambda e: e.tensor_tensor(out=t1[:], in0=t1[:], in1=t2[:], op=ALU.subtract), reads=["s_t1", "s_t2"], writes=["s_t1"])
            P.op("dve", lambda e: e.tensor_tensor(out=cim[:], in0=t1[:], in1=den[:], op=ALU.mult), reads=["s_t1", "s_den"], writes=["s_cim"])

            for ut in range(4):
                P.op("sp", lambda e: e.dma_start(out=ucm[:], in_=self.PJ[ut * 128:(ut + 1) * 128, 0:L]), writes=["s_ucm"], dma=True)
                for gp in range(4):
                    T = ut * 4 + gp
                    rows = slice(0, 128)
                    P.op("sp", lambda e: e.dma_start(out=bt[:], in_=self.s5_bt[j, T]), writes=["s_bt"], dma=True)
                    for d in range(2):
                        ci = T * 2 + d
                        cc = slice(ci, ci + 1)
                        P.op("sp", lambda e: e.dma_start(out=ctl[:], in_=self.s5_ct[j, T, d]), writes=["s_ct"], dma=True)
                        P.op("dve", lambda e: e.tensor_scalar(out=cpr[:, 0, :], in0=ctl[:, 0, :], scalar1=cre[:, cc], scalar2=None, op0=ALU.mult),
                             reads=["s_ct", "s_cre"], writes=["s_cpr"])
                        P.op("dve", lambda e: e.scalar_tensor_tensor(out=cpr[:, 0, :], in0=ctl[:, 1, :], scalar=cim[:, cc], op0=ALU.mult,
                                                                     in1=cpr[:, 0, :], op1=ALU.subtract),
                             reads=["s_ct", "s_cim", "s_cpr"], writes=["s_cpr"])
                        P.op("dve", lambda e: e.tensor_scalar(out=cpr[:, 0, :], in0=cpr[:, 0, :], scalar1=-1.0, scalar2=None, op0=ALU.mult),
                             reads=["s_cpr"], writes=["s_cpr"])
                        P.op("dve", lambda e: e.tensor_scalar(out=cpr[:, 1, :], in0=ctl[:, 0, :], scalar1=cim[:, cc], scalar2=None, op0=ALU.mult),
                             reads=["s_ct", "s_cim"], writes=["s_cpr1"])
                        P.op("dve", lambda e: e.scalar_tensor_tensor(out=cpr[:, 1, :], in0=ctl[:, 1, :], scalar=cre[:, cc], op0=ALU.mult,
                                                                     in1=cpr[:, 1, :], op1=ALU.add),
                             reads=["s_ct", "s_cre", "s_cpr1"], writes=["s_cpr1"])
                        P.op("dve", lambda e: e.tensor_scalar(out=cpr[:, 1, :], in0=cpr[:, 1, :], scalar1=-1.0, scalar2=None, op0=ALU.mult),
                             reads=["s_cpr1"], writes=["s_cpr1"])
                        P.op("dve", lambda e: e.tensor_scalar(out=rt[:], in0=ones[:], scalar1=mag[:, cc], scalar2=None, op0=ALU.mult),
                             reads=["s_ones", "s_mag"], writes=["s_rt"])
                        P.op("dve", lambda e: e.memset(Ere[:, 0:1], 1.0), writes=["s_E"])
                        P.op("dve", lambda e: e.memset(Eim[:, 0:1], 0.0), writes=["s_E"])
                        P.op("dve", lambda e: e.tensor_copy(out=cw[:, 0:1], in_=cth[:, cc]), reads=["s_cth"], writes=["s_cw"])
                        P.op("dve", lambda e: e.tensor_copy(out=cw[:, 1:2], in_=nsth[:, cc]), reads=["s_nsth"], writes=["s_cw"])
                        m = 1
                        while m < TB:
                            P.op("dve", lambda e: e.tensor_scalar(out=ta[:, 0:m], in0=Eim[:, 0:m], scalar1=cw[:, 1:2], scalar2=None, op0=ALU.mult),
                                 reads=["s_E", "s_cw"], writes=["s_ta"])
                            P.op("dve", lambda e: e.scalar_tensor_tensor(out=Ere[:, m:2 * m], in0=Ere[:, 0:m], scalar=cw[:, 0:1], op0=ALU.mult,
                                                                         in1=ta[:, 0:m], op1=ALU.subtract),
                                 reads=["s_E", "s_cw", "s_ta"], writes=["s_E"])
                            P.op("dve", lambda e: e.tensor_scalar(out=tb[:, 0:m], in0=Ere[:, 0:m], scalar1=cw[:, 1:2], scalar2=None, op0=ALU.mult),
                                 reads=["s_E", "s_cw"], writes=["s_tb"])
                            P.op("dve", lambda e: e.scalar_tensor_tensor(out=Eim[:, m:2 * m], in0=Eim[:, 0:m], scalar=cw[:, 0:1], op0=ALU.mult,
                                                                         in1=tb[:, 0:m], op1=ALU.add),
                                 reads=["s_E", "s_cw", "s_tb"], writes=["s_E"])
                            m *= 2
                            if m < TB:
                                P.op("dve", lambda e: e.tensor_tensor(out=cw2[:, 0:1], in0=cw[:, 1:2], in1=cw[:, 1:2], op=ALU.mult), reads=["s_cw"], writes=["s_cw2"])
                                P.op("dve", lambda e: e.tensor_tensor(out=cw2[:, 1:2], in0=cw[:, 0:1], in1=cw[:, 1:2], op=ALU.mult), reads=["s_cw"], writes=["s_cw2"])
                                P.op("dve", lambda e: e.scalar_tensor_tensor(out=cw[:, 0:1], in0=cw[:, 0:1], scalar=cw[:, 0:1], op0=ALU.mult,
                                                                             in1=cw2[:, 0:1], op1=ALU.subtract), reads=["s_cw", "s_cw2"], writes=["s_cw"])
                                P.op("dve", lambda e: e.tensor_scalar(out=cw[:, 1:2], in0=cw2[:, 1:2], scalar1=2.0, scalar2=None, op0=ALU.mult),
                                     reads=["s_cw2"], writes=["s_cw"])
                        rv = (lambda ap: ap[:, ::-1]) if d == 1 else (lambda ap: ap[:])
                        blocks = list(range(nb)) if d == 0 else list(range(nb - 1, -1, -1))
                        first_blk = True
                        for bi in blocks:
                            t0 = bi * TB
                            for sbk in range(TB // 512):
                                c0 = sbk * 512
                                sl = slice(c0, c0 + 512)
                                if d == 0:
                                    Er, Ei = Ere[:, sl], Eim[:, sl]
                                else:
                                    Er, Ei = Ere[:, TB - c0 - 512:TB - c0][:, ::-1], Eim[:, TB - c0 - 512:TB - c0][:, ::-1]
                                pr, kpr = psW.next()
                                pi_, kpi = psW.next()
                                P.op("pe", lambda e: e.matmul(pr[:], lhsT=bt[rows, d, 0, :], rhs=ucm[rows, t0 + c0:t0 + c0 + 512], start=True, stop=True),
                                     reads=["s_bt", "s_ucm"], writes=[kpr])
                                P.op("pe", lambda e: e.matmul(pi_[:], lhsT=bt[rows, d, 1, :], rhs=ucm[rows, t0 + c0:t0 + c0 + 512], start=True, stop=True),
                                     reads=["s_bt", "s_ucm"], writes=[kpi])
                                P.op("dve", lambda e: e.tensor_tensor(out=xre[:, sl], in0=pr[:], in1=Er, op=ALU.mult), reads=[kpr, "s_E"], writes=["s_xre"])
                                P.op("dve", lambda e: e.tensor_tensor(out=ta[:, sl], in0=pi_[:], in1=Ei, op=ALU.mult), reads=[kpi, "s_E"], writes=["s_ta"])
                                P.op("dve", lambda e: e.tensor_tensor(out=xre[:, sl], in0=xre[:, sl], in1=ta[:, sl], op=ALU.subtract),
                                     reads=["s_xre", "s_ta"], writes=["s_xre"])
                                P.op("dve", lambda e: e.tensor_tensor(out=xim[:, sl], in0=pr[:], in1=Ei, op=ALU.mult), reads=[kpr, "s_E"], writes=["s_xim"])
                                P.op("dve", lambda e: e.tensor_tensor(out=tb[:, sl], in0=pi_[:], in1=Er, op=ALU.mult), reads=[kpi, "s_E"], writes=["s_tb"])
                                P.op("dve", lambda e: e.tensor_tensor(out=xim[:, sl], in0=xim[:, sl], in1=tb[:, sl], op=ALU.add),
                                     reads=["s_xim", "s_tb"], writes=["s_xim"])
                            if first_blk:
                                i_re, i_im = 0.0, 0.0
                            else:
                                P.op("dve", lambda e: e.tensor_scalar(out=ini[:, 0:1], in0=tin[:, 1:2], scalar1=sth[:, cc], scalar2=None, op0=ALU.mult),
                                     reads=["s_tin", "s_sth"], writes=["s_ini"])
                                P.op("dve", lambda e: e.scalar_tensor_tensor(out=ini[:, 0:1], in0=tin[:, 0:1], scalar=cth[:, cc], op0=ALU.mult,
                                                                             in1=ini[:, 0:1], op1=ALU.subtract),
                                     reads=["s_tin", "s_cth", "s_ini"], writes=["s_ini"])
                                P.op("dve", lambda e: e.tensor_scalar(out=ini[:, 1:2], in0=tin[:, 0:1], scalar1=sth[:, cc], scalar2=None, op0=ALU.mult),
                                     reads=["s_tin", "s_sth"], writes=["s_ini"])
                                P.op("dve", lambda e: e.scalar_tensor_tensor(out=ini[:, 1:2], in0=tin[:, 1:2], scalar=cth[:, cc], op0=ALU.mult,
                                                                             in1=ini[:, 1:2], op1=ALU.add),
                                     reads=["s_tin", "s_cth", "s_ini"], writes=["s_ini"])
                                i_re, i_im = ini[:, 0:1], ini[:, 1:2]
                            first_blk = False
                            P.op("dve", lambda e: e.tensor_tensor_scan(out=rv(gre), data0=rv(rt), data1=rv(xre), initial=i_re, op0=ALU.mult, op1=ALU.add),
                                 reads=["s_rt", "s_xre", "s_ini"], writes=["s_gre"])
                            P.op("dve", lambda e: e.tensor_tensor_scan(out=rv(gim), data0=rv(rt), data1=rv(xim), initial=i_im, op0=ALU.mult, op1=ALU.add),
                                 reads=["s_rt", "s_xim", "s_ini"], writes=["s_gim"])
                            Erf = Ere[:, ::-1] if d == 1 else Ere[:]
                            Eif = Eim[:, ::-1] if d == 1 else Eim[:]
                            P.op("dve", lambda e: e.tensor_tensor(out=xre[:], in0=gre[:], in1=Erf, op=ALU.mult), reads=["s_gre", "s_E"], writes=["s_xre"])
                            P.op("dve", lambda e: e.tensor_tensor(out=ta[:], in0=gim[:], in1=Eif, op=ALU.mult), reads=["s_gim", "s_E"], writes=["s_ta"])
                            P.op("dve", lambda e: e.tensor_tensor(out=xre[:], in0=xre[:], in1=ta[:], op=ALU.add), reads=["s_xre", "s_ta"], writes=["s_xre"])
                            P.op("dve", lambda e: e.tensor_tensor(out=xim[:], in0=gim[:], in1=Erf, op=ALU.mult), reads=["s_gim", "s_E"], writes=["s_xim"])
                            P.op("dve", lambda e: e.tensor_tensor(out=tb[:], in0=gre[:], in1=Eif, op=ALU.mult), reads=["s_gre", "s_E"], writes=["s_tb"])
                            P.op("dve", lambda e: e.tensor_tensor(out=xim[:], in0=xim[:], in1=tb[:], op=ALU.subtract), reads=["s_xim", "s_tb"], writes=["s_xim"])
                            edge = 0 if d == 1 else TB - 1
                            P.op("dve", lambda e: e.tensor_copy(out=tin[:, 0:1], in_=xre[:, edge:edge + 1]), reads=["s_xre"], writes=["s_tin"])
                            P.op("dve", lambda e: e.tensor_copy(out=tin[:, 1:2], in_=xim[:, edge:edge + 1]), reads=["s_xim"], writes=["s_tin"])
                            for sbk in range(TB // 512):
                                sl = slice(sbk * 512, (sbk + 1) * 512)
                                py, kpy = psY.next()
                                P.op("pe", lambda e: e.matmul(py[0:32, :], lhsT=cpr[:, 0, :], rhs=xre[:, sl], start=True, stop=False),
                                     reads=["s_cpr", "s_cpr1", "s_xre"], writes=[kpy])
                                P.op("pe", lambda e: e.matmul(py[0:32, :], lhsT=cpr[:, 1, :], rhs=xim[:, sl], start=False, stop=True),
                                     reads=["s_cpr", "s_cpr1", "s_xim"], writes=[kpy])
                                yv, kyv = yev.next()
                                P.op("act", lambda e: e.copy(out=yv[:], in_=py[0:32, :]), reads=[kpy], writes=[kyv])
                                P.op("pool", lambda e: e.dma_start(out=YS[d, 32 * T:32 * T + 32, t0 + sbk * 512:t0 + (sbk + 1) * 512], in_=yv[:]),
                                     reads=[kyv], dma=True)
            P.barrier()
        with ExitStack() as s2:
            def sb(name, shape, dt=F32):
                return self.sbt(s2, name, shape, dt)
            pc = sb("s_pc", [128, 4, 2])
            gw = sb("s_gw", [128, 4, 4, 128], BF16)
            yf = sb("s_yf", [128, 4, 512]); yb = sb("s_yb", [128, 4, 512]); uu = sb("s_uu", [128, 4, 512])
            yg = sb("s_yg", [128, 4, 512], BF16)
            sgm = Rot([sb("s_sg%d" % i, [128, 512]) for i in range(2)], "s_sg")
            yo = sb("s_yo", [128, 4, 512], BF16)
            psA = Rot(self.psum[0:4], "psA")
            P.op("sp", lambda e: e.dma_start(out=pc[:], in_=self.s5_pc[j]), writes=["s_pc"], dma=True)
            P.op("sp", lambda e: e.dma_start(out=gw[:], in_=self.wb[("glu", j)].rearrange("jc p ic w -> p jc ic w")), writes=["s_gw"], dma=True)
            for t0 in range(0, L, 512):
                rowv = lambda ap: ap.rearrange("(c p) t -> p c t", p=128)
                P.op("sp", lambda e: e.dma_start(out=yf[:], in_=rowv(YS[0, :, t0:t0 + 512])), writes=["s_yf"], dma=True)
                P.op("sp", lambda e: e.dma_start(out=yb[:], in_=rowv(YS[1, :, t0:t0 + 512])), writes=["s_yb"], dma=True)
                P.op("sp", lambda e: e.dma_start(out=uu[:], in_=rowv(self.PJ[0:512, t0:t0 + 512])), writes=["s_uu"], dma=True)
                P.op("pool", lambda e: e.tensor_tensor(out=yf[:], in0=yf[:], in1=yb[:], op=ALU.add), reads=["s_yf", "s_yb"], writes=["s_yf"])
                for c in range(4):
                    P.op("dve", lambda e: e.scalar_tensor_tensor(out=yf[:, c, :], in0=uu[:, c, :], scalar=pc[:, c, 0:1], op0=ALU.mult,
                                                                 in1=yf[:, c, :], op1=ALU.add), reads=["s_uu", "s_yf", "s_pc"], writes=["s_yf"])
                P.op("act", lambda e: e.activation(out=yg[:].rearrange("p a b -> p (a b)"), in_=yf[:].rearrange("p a b -> p (a b)"),
                                                   func=AF.Gelu_apprx_tanh), reads=["s_yf"], writes=["s_yg"])
                for jc in range(4):
                    pa, kpa = psA.next()
                    for ic in range(4):
                        P.op("pe", lambda e: e.matmul(pa[:], lhsT=gw[:, jc, ic, :], rhs=yg[:, ic, :], start=(ic == 0), stop=(ic == 3)),
                             reads=["s_gw", "s_yg"], writes=[kpa])
                    sg_, ksg = sgm.next()
                    P.op("act", lambda e: e.activation(out=sg_[:], in_=pa[:], func=AF.Sigmoid, bias=pc[:, jc, 1:2]), reads=[kpa, "s_pc"], writes=[ksg])
                    P.op("dve", lambda e: e.tensor_tensor(out=yo[:, jc, :], in0=sg_[:], in1=yg[:, jc, :], op=ALU.mult), reads=[ksg, "s_yg"], writes=["s_yo"])
                P.op("pool", lambda e: e.dma_start(out=self.YM[0:512, t0:t0 + 512].rearrange("(c p) t -> p c t", p=128), in_=yo[:]),
                     reads=["s_yo"], dma=True)

    def rwkv(self, L, j):
        nc, P = self.nc, self.P
        TM = self.RWTM
        YT = self.YTM
        GC, BC = self.RWG, self.RWB
        with ExitStack() as s2:
            def sb(name, shape, dt=F32):
                return self.sbt(s2, name, shape, dt)
            mu = sb("r_mu", [128, 14]); omu = sb("r_omu", [128, 14]); hmu = sb("r_hmu", [128, 14])
            pc = sb("r_pc", [128, 4, 8]); omka = sb("r_omka", [128, 4]); nw0 = sb("r_nw0", [128, 4, 2])
            lw = sb("r_lw", [128, 2, 512])
            bones = sb("r_bones", [128, 128])
            ppad = Rot([sb("r_ppad%d" % i, [128, 514]) for i in range(3)], "r_ppad")
            nsum = Rot([sb("r_nsum%d" % i, [128, 512]) for i in range(2)], "r_nsum")
            rr = sb("r_r", [128, 4, 512]); kk_ = sb("r_k", [128, 4, 512]); vv = sb("r_v", [128, 4, 512])
            lo = sb("r_lo", [128, 512]); xg = sb("r_xg", [128, 512]); tw = sb("r_tw", [128, 512]); sgx = sb("r_sgx", [128, 512])
            aa = sb("r_a", [128, 4, 512]); gg = sb("r_g", [128, 4, 512]); wf = sb("r_wf", [128, 4, 512]); wbk = sb("r_wb", [128, 4, 512])
            An = sb("r_An", [128, 4, 512]); Bn = sb("r_Bn", [128, 4, 512]); k2 = sb("r_k2", [128, 4, 512]); bon = sb("r_bon", [128, 4, 512])
            t1 = sb("r_t1", [128, 512]); t2 = sb("r_t2", [128, 512]); t3 = sb("r_t3", [128, 512])
            tok = Rot([sb("r_tok%d" % i, [128, 512]) for i in range(3)], "r_tok")
            psA = Rot(self.psum[0:6], "psA")
            psT = Rot(self.psum[6:8], "psT")
            P.op("sp", lambda e: e.dma_start(out=mu[:], in_=self.rw_mu[j]), writes=["r_mu"], dma=True)
            P.op("sp", lambda e: e.dma_start(out=pc[:], in_=self.rw_pc[j]), writes=["r_pc"], dma=True)
            P.op("sp", lambda e: e.dma_start(out=lw[:], in_=self.rw_lw[j]), writes=["r_lw"], dma=True)
            P.op("dve", lambda e: e.tensor_scalar(out=omu[:], in0=mu[:], scalar1=-1.0, scalar2=1.0, op0=ALU.mult, op1=ALU.add), reads=["r_mu"], writes=["r_omu"])
            P.op("dve", lambda e: e.tensor_scalar(out=hmu[:], in0=mu[:], scalar1=0.5, scalar2=None, op0=ALU.mult), reads=["r_mu"], writes=["r_hmu"])
            P.op("dve", lambda e: e.tensor_scalar(out=omka[:], in0=pc[:, :, 4], scalar1=-1.0, scalar2=1.0, op0=ALU.mult, op1=ALU.add), reads=["r_pc"], writes=["r_omka"])
            P.op("dve", lambda e: e.tensor_scalar(out=nw0[:], in0=pc[:, :, 0:2], scalar1=-1.0, scalar2=None, op0=ALU.mult), reads=["r_pc"], writes=["r_nw0"])
            P.op("dve", lambda e: e.memset(bones[:], 0.0), writes=["r_bones"])
            P.op("dve", lambda e: e.memset(bones[0:64, 0:64], 1.0), writes=["r_bones"])
            P.op("dve", lambda e: e.memset(bones[64:128, 64:128], 1.0), writes=["r_bones"])
            row_tiles = [(512 + 128 * i, 128) for i in range(12)] + [(512 + 1536, 96), (512 + 1632, 64)]
            dests = [(rr, i) for i in range(4)] + [(kk_, i) for i in range(4)] + [(vv, i) for i in range(4)] + [(lo, None), (xg, None)]
            for t0 in range(0, L, 512):
                lo_t = max(t0 - 1, 0)
                hi_t = min(t0 + 513, L)
                for ti, ((r0, nr), (dst, di)) in enumerate(zip(row_tiles, dests)):
                    pp, kp = ppad.next()
                    if t0 == 0:
                        P.op("pool", lambda e: e.memset(pp[:, 0:1], 0.0), writes=[kp])
                    if t0 + 512 == L:
                        P.op("pool", lambda e: e.memset(pp[:, 513:514], 0.0), writes=[kp])
                    P.op("sp", lambda e: e.dma_start(out=pp[0:nr, lo_t - (t0 - 1):hi_t - (t0 - 1)], in_=self.PJ[r0:r0 + nr, lo_t:hi_t]),
                         writes=[kp], dma=True)
                    ns, kn = nsum.next()
                    P.op("pool", lambda e: e.tensor_tensor(out=ns[0:nr, :], in0=pp[0:nr, 0:512], in1=pp[0:nr, 2:514], op=ALU.add), reads=[kp], writes=[kn])
                    P.op("pool", lambda e: e.tensor_scalar(out=ns[0:nr, :], in0=ns[0:nr, :], scalar1=hmu[0:nr, ti:ti + 1], scalar2=None, op0=ALU.mult),
                         reads=[kn, "r_hmu"], writes=[kn])
                    dap = dst[0:nr, :] if di is None else dst[0:nr, di, :]
                    dkey = "r_dst%d" % ti
                    P.op("dve", lambda e: e.scalar_tensor_tensor(out=dap, in0=pp[0:nr, 1:513], scalar=omu[0:nr, ti:ti + 1], op0=ALU.mult,
                                                                 in1=ns[0:nr, :], op1=ALU.add), reads=[kp, kn, "r_omu"], writes=[dkey])
                allk = ["r_dst%d" % i for i in range(14)]
                P.op("act", lambda e: e.activation(out=tw[0:64, :], in_=lo[0:64, :], func=AF.Tanh), reads=allk, writes=["r_tw"])
                P.op("act", lambda e: e.activation(out=sgx[0:64, :], in_=xg[0:64, :], func=AF.Sigmoid), reads=allk, writes=["r_sgx"])
                for ct in range(4):
                    cs_ = slice(ct * 128, (ct + 1) * 128)
                    pa, kpa = psA.next()
                    P.op("pe", lambda e: e.matmul(pa[:], lhsT=lw[64:96, 0, cs_], rhs=lo[64:96, :], start=True, stop=True), reads=["r_lw"] + allk, writes=[kpa])
                    P.op("act", lambda e: e.activation(out=aa[:, ct, :], in_=pa[:], func=AF.Sigmoid, bias=pc[:, ct, 2:3]), reads=[kpa, "r_pc"], writes=[("r_a", ct)])
                    pa, kpa = psA.next()
                    P.op("pe", lambda e: e.matmul(pa[:], lhsT=lw[0:64, 1, cs_], rhs=sgx[0:64, :], start=True, stop=True), reads=["r_lw", "r_sgx"], writes=[kpa])
                    P.op("act", lambda e: e.copy(out=gg[:, ct, :], in_=pa[:]), reads=[kpa], writes=[("r_g", ct)])
                    for d, wdst in ((0, wf), (1, wbk)):
                        pa, kpa = psA.next()
                        P.op("pe", lambda e: e.matmul(pa[:], lhsT=lw[32 * d:32 * d + 32, 0, cs_], rhs=tw[32 * d:32 * d + 32, :], start=True, stop=True),
                             reads=["r_lw", "r_tw"], writes=[kpa])
                        wk = ("r_w", d, ct)
                        P.op("act", lambda e: e.activation(out=wdst[:, ct, :], in_=pa[:], func=AF.Exp, scale=-1.0, bias=nw0[:, ct, d:d + 1]), reads=[kpa, "r_nw0"], writes=[wk])
                        P.op("act", lambda e: e.activation(out=wdst[:, ct, :], in_=wdst[:, ct, :], func=AF.Ln, bias=1.0), reads=[wk], writes=[wk])
                        P.op("act", lambda e: e.activation(out=wdst[:, ct, :], in_=wdst[:, ct, :], func=AF.Exp, scale=-1.0, bias=-0.5), reads=[wk], writes=[wk])
                        P.op("act", lambda e: e.activation(out=wdst[:, ct, :], in_=wdst[:, ct, :], func=AF.Exp, scale=-1.0), reads=[wk], writes=[wk])
                    P.op("dve", lambda e: e.tensor_scalar(out=t1[:], in0=kk_[:, ct, :], scalar1=pc[:, ct, 3:4], scalar2=None, op0=ALU.mult), reads=allk + ["r_pc"], writes=["r_t1"])
                    P.op("pool", lambda e: e.tensor_tensor(out=t2[:], in0=t1[:], in1=t1[:], op=ALU.mult), reads=["r_t1"], writes=["r_t2"])
                    pa, kpa = psA.next()
                    P.op("pe", lambda e: e.matmul(pa[:], lhsT=bones[:], rhs=t2[:], start=True, stop=True), reads=["r_bones", "r_t2"], writes=[kpa])
                    P.op("act", lambda e: e.activation(out=t3[:], in_=pa[:], func=AF.Sqrt), reads=[kpa], writes=["r_t3"])
                    P.op("dve", lambda e: e.tensor_scalar(out=t3[:], in0=t3[:], scalar1=1e-12, scalar2=None, op0=ALU.max), reads=["r_t3"], writes=["r_t3"])
                    P.op("dve", lambda e: e.reciprocal(out=t3[:], in_=t3[:]), reads=["r_t3"], writes=["r_t3"])
                    P.op("dve", lambda e: e.tensor_tensor(out=t1[:], in0=t1[:], in1=t3[:], op=ALU.mult), reads=["r_t1", "r_t3"], writes=["r_t1"])
                    P.op("pool", lambda e: e.tensor_scalar(out=An[:, ct, :], in0=t1[:], scalar1=-1.0, scalar2=None, op0=ALU.mult), reads=["r_t1"], writes=[("r_An", ct)])
                    P.op("dve", lambda e: e.tensor_tensor(out=Bn[:, ct, :], in0=t1[:], in1=aa[:, ct, :], op=ALU.mult), reads=["r_t1", ("r_a", ct)], writes=[("r_Bn", ct)])
                    P.op("dve", lambda e: e.tensor_scalar(out=t2[:], in0=aa[:, ct, :], scalar1=pc[:, ct, 4:5], scalar2=omka[:, ct:ct + 1], op0=ALU.mult, op1=ALU.add),
                         reads=[("r_a", ct), "r_pc", "r_omka"], writes=["r_t2"])
                    P.op("dve", lambda e: e.tensor_tensor(out=k2[:, ct, :], in0=kk_[:, ct, :], in1=t2[:], op=ALU.mult), reads=allk + ["r_t2"], writes=[("r_k2", ct)])
                    P.op("pool", lambda e: e.tensor_tensor(out=t3[:], in0=rr[:, ct, :], in1=k2[:, ct, :], op=ALU.mult), reads=allk + [("r_k2", ct)], writes=["r_t3"])
                    P.op("dve", lambda e: e.tensor_scalar(out=t3[:], in0=t3[:], scalar1=pc[:, ct, 5:6], scalar2=None, op0=ALU.mult), reads=["r_t3", "r_pc"], writes=["r_t3"])
                    pa, kpa = psA.next()
                    P.op("pe", lambda e: e.matmul(pa[:], lhsT=bones[:], rhs=t3[:], start=True, stop=True), reads=["r_bones", "r_t3"], writes=[kpa])
                    P.op("dve", lambda e: e.tensor_tensor(out=bon[:, ct, :], in0=pa[:], in1=vv[:, ct, :], op=ALU.mult), reads=[kpa] + allk, writes=[("r_bon", ct)])
                cm = lambda ap: ap.rearrange("(c p) t -> p c t", p=128)
                P.op("pool", lambda e: e.dma_start(out=cm(GC[:, t0:t0 + 512]), in_=gg[:]), reads=[("r_g", c) for c in range(4)], dma=True)
                P.op("pool", lambda e: e.dma_start(out=cm(BC[:, t0:t0 + 512]), in_=bon[:]), reads=[("r_bon", c) for c in range(4)], dma=True)
                srcs = [(rr, allk), (k2, [("r_k2", c) for c in range(4)]), (vv, allk), (An, [("r_An", c) for c in range(4)]),
                        (Bn, [("r_Bn", c) for c in range(4)]), (wf, [("r_w", 0, c) for c in range(4)]), (wbk, [("r_w", 1, c) for c in range(4)])]
                for ai, (src, skeys) in enumerate(srcs):
                    for sbk in range(4):
                        pt, kpt = psT.next()
                        for ct in range(4):
                            P.op("pe", lambda e: e.matmul(pt[:, ct * 128:(ct + 1) * 128], lhsT=src[:, ct, sbk * 128:(sbk + 1) * 128], rhs=self.ident[:],
                                                          start=True, stop=True), reads=skeys + ["ident"], writes=[kpt])
                        tk, ktk = tok.next()
                        P.op("act", lambda e: e.copy(out=tk[:], in_=pt[:]), reads=[kpt], writes=[ktk])
                        P.op("pool", lambda e: e.dma_start(out=TM[ai, t0 + sbk * 128:t0 + (sbk + 1) * 128, :], in_=tk[:]), reads=[ktk], dma=True)
            P.barrier()
        NS = 16
        with ExitStack() as s2:
            def sb(name, shape, dt=F32):
                return self.sbt(s2, name, shape, dt)
            rep = sb("q_rep", [16, 128]); iot = sb("q_iot", [16, 128], I32); iof = sb("q_iof", [16, 128]); iog = sb("q_iog", [16, 128])
            S = sb("q_S", [128, 8, 64]); tmp2 = sb("q_tmp2", [128, 8, 64]); U = sb("q_U", [128, 8])
            tmpR = Rot([sb("q_tmp%d" % i, [128, 8, 64]) for i in range(4)], "q_tmp")
            pend = None
            cmpR = Rot([sb("q_cmp%d" % i, [16, 5, NS * 64]) for i in range(2)], "q_cmp")
            opR = Rot([sb("q_op%d" % i, [128, 5, NS, 64]) for i in range(2)], "q_op")
            vR = Rot([sb("q_v%d" % i, [128, NS, 8]) for i in range(2)], "q_v")
            yR = Rot([sb("q_y%d" % i, [128, NS, 8]) for i in range(2)], "q_y")
            vkR = Rot([sb("q_vk%d" % i, [128, 8, 64]) for i in range(4)], "q_vk")
            psR = Rot(self.psum[0:8], "psR")
            P.op("pool", lambda e: e.iota(iot[:], pattern=[[1, 128]], base=0, channel_multiplier=-8), writes=["q_iot"])
            P.op("dve", lambda e: e.tensor_copy(out=iof[:], in_=iot[:]), reads=["q_iot"], writes=["q_iof"])
            P.op("dve", lambda e: e.tensor_scalar(out=iog[:], in0=iof[:], scalar1=0.0, scalar2=None, op0=ALU.is_ge), reads=["q_iof"], writes=["q_iog"])
            P.op("dve", lambda e: e.tensor_scalar(out=iof[:], in0=iof[:], scalar1=8.0, scalar2=None, op0=ALU.is_lt), reads=["q_iof"], writes=["q_iof"])
            P.op("dve", lambda e: e.tensor_tensor(out=rep[:], in0=iof[:], in1=iog[:], op=ALU.mult), reads=["q_iof", "q_iog"], writes=["q_rep"])
            P.op("dve", lambda e: e.memset(S[:].rearrange("p a b -> p (a b)"), 0.0), writes=["q_S"])
            arr_of = [3, None, 4, 1, 0]
            for i0 in range(0, L, NS):
                cmp_, kc_ = cmpR.next()
                opt, ko = opR.next()
                vt, kv = vR.next()
                yt_, ky = yR.next()
                for d in range(2):
                    for oi in range(5):
                        ai = arr_of[oi] if oi != 1 else (5 + d)
                        if d == 0:
                            src = TM[ai, i0:i0 + NS, :]
                        else:
                            src = TM[ai, L - i0 - NS:L - i0, :][::-1, :]
                        P.op("sp", lambda e: e.dma_start(out=cmp_[8 * d:8 * d + 8, oi, :].rearrange("h (s k) -> h s k", k=64),
                                                         in_=src.rearrange("s (h k) -> h s k", k=64)), writes=[(kc_, d, oi)], dma=True)
                    if d == 0:
                        vsrc = TM[2, i0:i0 + NS, :]
                    else:
                        vsrc = TM[2, L - i0 - NS:L - i0, :][::-1, :]
                    P.op("act", lambda e: e.dma_start(out=vt[64 * d:64 * d + 64, :, :], in_=vsrc.rearrange("s (hv vi) -> hv s vi", vi=8)),
                         writes=[(kv, d)], dma=True)
                for oi in range(5):
                    for q4 in range(NS * 64 // 512):
                        pr, kpr = psR.next()
                        P.op("pe", lambda e: e.matmul(pr[:], lhsT=rep[:], rhs=cmp_[:, oi, q4 * 512:(q4 + 1) * 512], start=True, stop=True),
                             reads=["q_rep", (kc_, 0, oi), (kc_, 1, oi)], writes=[kpr])
                        P.op("act", lambda e: e.copy(out=opt[:, oi, q4 * 8:(q4 + 1) * 8, :].rearrange("p s k -> p (s k)"), in_=pr[:]),
                             reads=[kpr], writes=[(ko, oi)])
                opk = [(ko, oi) for oi in range(5)]
                for s_ in range(NS):
                    bc = lambda oi: opt[:, oi, s_, :].unsqueeze(1).broadcast_to([128, 8, 64])
                    vk, kvk = vkR.next()
                    for vi_ in range(8):
                        P.op("act", lambda e: e.activation(out=vk[:, vi_, :], in_=opt[:, 3, s_, :], func=AF.Copy, scale=vt[:, s_, vi_:vi_ + 1]),
                             reads=[(kv, 0), (kv, 1)] + opk, writes=[(kvk, vi_)])
                    kvk = [(kvk, vi_) for vi_ in range(8)]
                    tA, ktA = tmpR.next()
                    C_ = 600
                    P.op("dve", lambda e: e.tensor_tensor(out=tA[:], in0=S[:], in1=bc(0), op=ALU.mult), reads=["q_S"] + opk, writes=[ktA], cost=C_)
                    P.op("dve", lambda e: e.tensor_tensor(out=S[:], in0=S[:], in1=bc(1), op=ALU.mult), reads=["q_S"] + opk, writes=["q_S"], cost=C_)
                    P.op("dve", lambda e: e.tensor_reduce(out=U[:], in_=tA[:], axis=AX.X, op=ALU.add), reads=[ktA], writes=["q_U"], cost=C_)
                    P.op("dve", lambda e: e.tensor_tensor(out=S[:], in0=S[:], in1=vk[:], op=ALU.add), reads=["q_S"] + kvk, writes=["q_S"], cost=C_)
                    P.op("dve", lambda e: e.tensor_tensor(out=tmp2[:], in0=U[:].unsqueeze(2).broadcast_to([128, 8, 64]), in1=bc(2), op=ALU.mult),
                         reads=["q_U"] + opk, writes=["q_tmp2"], cost=C_)
                    if pend is not None:
                        pt_, pkt, pyt, pky, ps_ = pend
                        P.op("dve", lambda e: e.tensor_reduce(out=pyt[:, ps_, :], in_=pt_[:], axis=AX.X, op=ALU.add), reads=[pkt], writes=[pky], cost=C_)
                        pend = None
                    P.op("dve", lambda e: e.tensor_tensor(out=S[:], in0=S[:], in1=tmp2[:], op=ALU.add), reads=["q_S", "q_tmp2"], writes=["q_S"], cost=C_)
                    tR, ktR = tmpR.next()
                    P.op("dve", lambda e: e.tensor_tensor(out=tR[:], in0=S[:], in1=bc(4), op=ALU.mult), reads=["q_S"] + opk, writes=[ktR], cost=C_)
                    pend = (tR, ktR, yt_, ky, s_)
                if pend is not None:
                    pt_, pkt, pyt, pky, ps_ = pend
                    P.op("dve", lambda e: e.tensor_reduce(out=pyt[:, ps_, :], in_=pt_[:], axis=AX.X, op=ALU.add), reads=[pkt], writes=[pky])
                    pend = None
                for d in range(2):
                    if d == 0:
                        dst = YT[0, i0:i0 + NS, :]
                    else:
                        dst = YT[1, L - i0 - NS:L - i0, :][::-1, :]
                    P.op("pool", lambda e: e.dma_start(out=dst.rearrange("s (hv vi) -> hv s vi", vi=8), in_=yt_[64 * d:64 * d + 64, :, :]),
                         reads=[ky], dma=True)
            P.barrier()
        with ExitStack() as s2:
            def sb(name, shape, dt=F32):
                return self.sbt(s2, name, shape, dt)
            pc = sb("z_pc", [128, 4, 8]); bones = sb("z_bones", [128, 128])
            ya = Rot([sb("z_ya%d" % i, [128, 512]) for i in range(2)], "z_ya")
            yb_ = Rot([sb("z_yb%d" % i, [128, 512]) for i in range(2)], "z_yb")
            ycm = sb("z_ycm", [128, 4, 512]); gt = sb("z_g", [128, 4, 512]); bt_ = sb("z_b", [128, 4, 512])
            mean = sb("z_mean", [128, 512]); cen = sb("z_cen", [128, 512]); sq = sb("z_sq", [128, 512]); rstd = sb("z_rstd", [128, 512])
            yo = sb("z_yo", [128, 4, 512], BF16)
            psT = Rot(self.psum[0:4], "psT")
            psA = Rot(self.psum[4:8], "psA")
            P.op("sp", lambda e: e.dma_start(out=pc[:], in_=self.rw_pc[j]), writes=["z_pc"], dma=True)
            P.op("dve", lambda e: e.memset(bones[:], 0.0), writes=["z_bones"])
            P.op("dve", lambda e: e.memset(bones[0:64, 0:64], 1.0 / 64.0), writes=["z_bones"])
            P.op("dve", lambda e: e.memset(bones[64:128, 64:128], 1.0 / 64.0), writes=["z_bones"])
            cm = lambda ap: ap.rearrange("(c p) t -> p c t", p=128)
            for t0 in range(0, L, 512):
                P.op("sp", lambda e: e.dma_start(out=gt[:], in_=cm(GC[:, t0:t0 + 512])), writes=["z_g"], dma=True)
                P.op("sp", lambda e: e.dma_start(out=bt_[:], in_=cm(BC[:, t0:t0 + 512])), writes=["z_b"], dma=True)
                for sbk in range(4):
                    a_, ka = ya.next()
                    b_, kb = yb_.next()
                    P.op("sp", lambda e: e.dma_start(out=a_[:], in_=YT[0, t0 + sbk * 128:t0 + (sbk + 1) * 128, :]), writes=[ka], dma=True)
                    P.op("sp", lambda e: e.dma_start(out=b_[:], in_=YT[1, t0 + sbk * 128:t0 + (sbk + 1) * 128, :]), writes=[kb], dma=True)
                    P.op("pool", lambda e: e.tensor_tensor(out=a_[:], in0=a_[:], in1=b_[:], op=ALU.add), reads=[ka, kb], writes=[ka])
                    pt, kpt = psT.next()
                    for ct in range(4):
                        P.op("pe", lambda e: e.matmul(pt[:, ct * 128:(ct + 1) * 128], lhsT=a_[:, ct * 128:(ct + 1) * 128], rhs=self.ident[:], start=True, stop=True),
                             reads=[ka, "ident"], writes=[kpt])
                    P.op("act", lambda e: e.copy(out=ycm[:, :, sbk * 128:(sbk + 1) * 128], in_=pt[:].rearrange("p (c t) -> p c t", c=4)),
                         reads=[kpt], writes=[("z_ycm", sbk)])
                yk = [("z_ycm", q) for q in range(4)]
                for ct in range(4):
                    pa, kpa = psA.next()
                    P.op("pe", lambda e: e.matmul(pa[:], lhsT=bones[:], rhs=ycm[:, ct, :], start=True, stop=True), reads=["z_bones"] + yk, writes=[kpa])
                    P.op("dve", lambda e: e.tensor_tensor(out=cen[:], in0=ycm[:, ct, :], in1=pa[:], op=ALU.subtract), reads=yk + [kpa], writes=["z_cen"])
                    P.op("pool", lambda e: e.tensor_tensor(out=sq[:], in0=cen[:], in1=cen[:], op=ALU.mult), reads=["z_cen"], writes=["z_sq"])
                    pa2, kpa2 = psA.next()
                    P.op("pe", lambda e: e.matmul(pa2[:], lhsT=bones[:], rhs=sq[:], start=True, stop=True), reads=["z_bones", "z_sq"], writes=[kpa2])
                    P.op("dve", lambda e: e.tensor_scalar(out=rstd[:], in0=pa2[:], scalar1=64e-5, scalar2=None, op0=ALU.add), reads=[kpa2], writes=["z_rstd"])
                    P.op("act", lambda e: e.activation(out=rstd[:], in_=rstd[:], func=AF.Sqrt), reads=["z_rstd"], writes=["z_rstd"])
                    P.op("dve", lambda e: e.reciprocal(out=rstd[:], in_=rstd[:]), reads=["z_rstd"], writes=["z_rstd"])
                    P.op("dve", lambda e: e.tensor_tensor(out=cen[:], in0=cen[:], in1=rstd[:], op=ALU.mult), reads=["z_cen", "z_rstd"], writes=["z_cen"])
                    P.op("dve", lambda e: e.tensor_scalar(out=cen[:], in0=cen[:], scalar1=pc[:, ct, 6:7], scalar2=pc[:, ct, 7:8], op0=ALU.mult, op1=ALU.add),
                         reads=["z_cen", "z_pc"], writes=["z_cen"])
                    P.op("pool", lambda e: e.tensor_tensor(out=cen[:], in0=cen[:], in1=bt_[:, ct, :], op=ALU.add), reads=["z_cen", "z_b"], writes=["z_cen"])
                    P.op("dve", lambda e: e.tensor_tensor(out=yo[:, ct, :], in0=cen[:], in1=gt[:, ct, :], op=ALU.mult), reads=["z_cen", "z_g"], writes=["z_yo"])
                P.op("pool", lambda e: e.dma_start(out=cm(self.YM[512:1024, t0:t0 + 512]), in_=yo[:]), reads=["z_yo"], dma=True)

    def _mod_reduce(self, eng, t, q, key):
        P = self.P
        P.op(eng, lambda e: e.tensor_scalar(out=t[1], in0=t[0], scalar1=1.0 / q, scalar2=MAGIC, op0=ALU.mult, op1=ALU.add), reads=[key], writes=[key + "_t"])
        P.op(eng, lambda e: e.tensor_scalar(out=t[1], in0=t[1], scalar1=MAGIC, scalar2=-float(q), op0=ALU.subtract, op1=ALU.mult), reads=[key + "_t"], writes=[key + "_t"])
        P.op(eng, lambda e: e.tensor_tensor(out=t[0], in0=t[0], in1=t[1], op=ALU.add), reads=[key, key + "_t"], writes=[key])

    def gen_dft(self, L):
        nc, P = self.nc, self.P
        nt = L // 128
        q = L // 32
        DF = self.dscr("DF_%d" % L, [2, nt, 128, nt, 128], BF16)
        DI = self.dscr("DI_%d" % L, [2, nt, 128, nt, 128], BF16)
        self.dft[L] = (DF, DI)
        sc = (TWO_PI / (4.0 * L)) * 0.999999
        with ExitStack() as s2:
            def sb(name, shape, dt=F32):
                return self.sbt(s2, name, shape, dt)
            ii = sb("g_ii", [128, 128], I32)
            pcol = sb("g_pcol", [128, 1]); p2col = sb("g_p2col", [128, 1]); jrow = sb("g_jrow", [128, 128]); crow = sb("g_crow", [128, nt])
            m1 = sb("g_m1", [128, nt, 128])
            P.op("pool", lambda e: e.iota(ii[:, 0:1], pattern=[[0, 1]], base=0, channel_multiplier=1), writes=["g_ii"])
            P.op("dve", lambda e: e.tensor_copy(out=pcol[:], in_=ii[:, 0:1]), reads=["g_ii"], writes=["g_pcol"])
            P.op("dve", lambda e: e.tensor_scalar(out=p2col[:], in0=pcol[:], scalar1=2.0, scalar2=1.0, op0=ALU.mult, op1=ALU.add), reads=["g_pcol"], writes=["g_p2col"])
            P.op("pool", lambda e: e.iota(ii[:], pattern=[[1, 128]], base=0, channel_multiplier=0), reads=["g_pcol"], writes=["g_ii"])
            P.op("dve", lambda e: e.tensor_copy(out=jrow[:], in_=ii[:]), reads=["g_ii"], writes=["g_jrow"])
            P.op("dve", lambda e: e.tensor_copy(out=crow[:], in_=jrow[:, 0:nt]), reads=["g_jrow"], writes=["g_crow"])
            P.op("dve", lambda e: e.tensor_tensor(out=m1[:], in0=crow[:].unsqueeze(2).broadcast_to([128, nt, 128]),
                                                  in1=jrow[:].unsqueeze(1).broadcast_to([128, nt, 128]), op=ALU.mult), reads=["g_crow", "g_jrow"], writes=["g_m1"])
            P.op("dve", lambda e: e.tensor_scalar(out=m1[:].rearrange("p a b -> p (a b)"), in0=m1[:].rearrange("p a b -> p (a b)"), scalar1=256.0, scalar2=None, op0=ALU.mult),
                 reads=["g_m1"], writes=["g_m1"])
            sets = {}
            for eng in ("dve",):
                sets[eng] = dict(
                    f2=sb("g_f2" + eng, [128, 128]), f1=sb("g_f1" + eng, [128, 128]), col=sb("g_col" + eng, [128, 2]),
                    G=sb("g_G" + eng, [128, nt, 128]), Gt=sb("g_Gt" + eng, [128, nt, 128]), A=sb("g_A" + eng, [128, nt, 128]),
                    o=[sb("g_o%d%s" % (i, eng), [128, nt, 128], BF16) for i in range(2)])
            fl = lambda ap: ap.rearrange("p a b -> p (a b)")
            for g in range(nt):
                for inv in (0, 1):
                    eng = "dve"
                    S_ = sets[eng]
                    k = "g_" + eng
                    if inv == 0:
                        fc = g
                        P.op(eng, lambda e: e.tensor_scalar(out=S_["f2"][:], in0=jrow[:], scalar1=2.0, scalar2=float(256 * fc + 1), op0=ALU.mult, op1=ALU.add), reads=["g_jrow"], writes=[k + "f2"])
                        P.op(eng, lambda e: e.tensor_scalar(out=S_["f1"][:], in0=S_["f2"][:], scalar1=pcol[:, 0:1], scalar2=None, op0=ALU.mult), reads=[k + "f2", "g_pcol"], writes=[k + "f1"])
                        P.op(eng, lambda e: e.tensor_tensor(out=S_["G"][:], in0=S_["f2"][:].unsqueeze(1).broadcast_to([128, nt, 128]),
                                                            in1=crow[:].unsqueeze(2).broadcast_to([128, nt, 128]), op=ALU.mult), reads=[k + "f2", "g_crow"], writes=[k + "G"])
                        self._mod_reduce(eng, (fl(S_["G"][:]), fl(S_["Gt"][:])), q, k + "G")
                        P.op(eng, lambda e: e.tensor_scalar(out=fl(S_["G"][:]), in0=fl(S_["G"][:]), scalar1=128.0, scalar2=None, op0=ALU.mult), reads=[k + "G"], writes=[k + "G"])
                        P.op(eng, lambda e: e.tensor_tensor(out=S_["G"][:], in0=S_["G"][:], in1=S_["f1"][:].unsqueeze(1).broadcast_to([128, nt, 128]), op=ALU.add),
                             reads=[k + "G", k + "f1"], writes=[k + "G"])
                    else:
                        tc = g
                        P.op(eng, lambda e: e.tensor_scalar(out=S_["col"][:, 0:1], in0=p2col[:], scalar1=float(tc), scalar2=None, op0=ALU.mult), reads=["g_p2col"], writes=[k + "col"])
                        self._mod_reduce(eng, (S_["col"][:, 0:1], S_["col"][:, 1:2]), q, k + "col")
                        P.op(eng, lambda e: e.tensor_scalar(out=S_["col"][:, 0:1], in0=S_["col"][:, 0:1], scalar1=128.0, scalar2=None, op0=ALU.mult), reads=[k + "col"], writes=[k + "col"])
                        P.op(eng, lambda e: e.tensor_scalar(out=S_["f1"][:], in0=jrow[:], scalar1=p2col[:, 0:1], scalar2=S_["col"][:, 0:1], op0=ALU.mult, op1=ALU.add),
                             reads=["g_jrow", "g_p2col", k + "col"], writes=[k + "f1"])
                        P.op(eng, lambda e: e.tensor_tensor(out=S_["G"][:], in0=m1[:], in1=S_["f1"][:].unsqueeze(1).broadcast_to([128, nt, 128]), op=ALU.add),
                             reads=["g_m1", k + "f1"], writes=[k + "G"])
                    for trig in (0, 1):
                        P.op(eng, lambda e: e.tensor_scalar(out=fl(S_["A"][:]), in0=fl(S_["G"][:]), scalar1=float(L if trig == 0 else 0), scalar2=None, op0=ALU.add),
                             reads=[k + "G"], writes=[k + "A"])
                        self._mod_reduce(eng, (fl(S_["A"][:]), fl(S_["Gt"][:])), 4 * L, k + "A")
                        ob = S_["o"][trig]
                        ok = k + "o%d" % trig
                        P.op("act", lambda e: e.activation(out=fl(ob[:]), in_=fl(S_["A"][:]), func=AF.Sin, scale=sc), reads=[k + "A"], writes=[ok])
                        dst = (DF if inv == 0 else DI)[trig, g]
                        P.op("sp", lambda e: e.dma_start(out=dst, in_=ob[:]), reads=[ok], dma=True)
            P.barrier()

    def hyena(self, L, j):
        nc, P = self.nc, self.P
        nt = L // 128
        DF, DI = self.dft[L]
        HC = self.HYC
        HZ = self.HYZ
        KT = self.HYK
        KF = self.HYF
        with ExitStack() as s2:
            def sb(name, shape, dt=F32):
                return self.sbt(s2, name, shape, dt)
            hc = sb("h_hc", [128, 12, 4])
            pp = Rot([sb("h_pp%d" % i, [128, 2050]) for i in range(2)], "h_pp")
            oo = Rot([sb("h_oo%d" % i, [128, 2048]) for i in range(2)], "h_oo")
            TBc = min(L, 2048)
            P.op("sp", lambda e: e.dma_start(out=hc[:], in_=self.hy_cols[j]), writes=["h_hc"], dma=True)
            for rt_ in range(12):
                for t0 in range(0, L, TBc):
                    p_, kp = pp.next()
                    o_, ko = oo.next()
                    lo_t = max(t0 - 1, 0); hi_t = min(t0 + TBc + 1, L)
                    if t0 == 0:
                        P.op("pool", lambda e: e.memset(p_[:, 0:1], 0.0), writes=[kp])
                    if t0 + TBc == L:
                        P.op("pool", lambda e: e.memset(p_[:, TBc + 1:TBc + 2], 0.0), writes=[kp])
                    P.op("sp", lambda e: e.dma_start(out=p_[:, lo_t - (t0 - 1):hi_t - (t0 - 1)], in_=self.PJ[1024 + rt_ * 128:1024 + (rt_ + 1) * 128, lo_t:hi_t]), writes=[kp], dma=True)
                    eng = "dve"
                    P.op(eng, lambda e: e.tensor_scalar(out=o_[:, 0:TBc], in0=p_[:, 0:TBc], scalar1=hc[:, rt_, 0:1], scalar2=hc[:, rt_, 3:4], op0=ALU.mult, op1=ALU.add),
                         reads=[kp, "h_hc"], writes=[ko])
                    for q_ in (1, 2):
                        P.op(eng, lambda e: e.scalar_tensor_tensor(out=o_[:, 0:TBc], in0=p_[:, q_:q_ + TBc], scalar=hc[:, rt_, q_:q_ + 1], op0=ALU.mult, in1=o_[:, 0:TBc], op1=ALU.add),
                             reads=[kp, ko, "h_hc"], writes=[ko])
                    P.op("pool", lambda e: e.dma_start(out=HC[rt_ // 4, (rt_ % 4) * 128:(rt_ % 4 + 1) * 128, t0:t0 + TBc], in_=o_[:, 0:TBc]), reads=[ko], dma=True)
            P.barrier()
        with ExitStack() as s2:
            def sb(name, shape, dt=F32):
                return self.sbt(s2, name, shape, dt)
            w1 = sb("f_w1", [33, 64]); w2 = sb("f_w2", [64, 64]); w3 = sb("f_w3", [64, 2048]); fc_ = sb("f_fc", [64, 6])
            zp = sb("f_zp", [33, 512]); h1 = sb("f_h1", [64, 512]); h2 = sb("f_h2", [64, 512]); tq = sb("f_tq", [64, 512])
            drow = sb("f_drow", [128, 512]); tcol = sb("f_tcol", [128, nt]); dec = sb("f_dec", [128, 512])
            kf = sb("f_kf", [128, 2048]); kb = Rot([sb("f_kb%d" % i, [128, 2048], BF16) for i in range(2)], "f_kb"); ka = sb("f_ka", [128, 2048], BF16)
            rn = sb("f_rn", [128, 1024])
            psK = Rot(self.psum[0:3], "psK")
            psH = Rot(self.psum[3:4], "psH")
            psNm = self.psum[4:8]
            P.op("sp", lambda e: e.dma_start(out=w1[:], in_=self.hy_w1[j]), writes=["f_w"], dma=True)
            P.op("sp", lambda e: e.dma_start(out=w2[:], in_=self.hy_w2[j]), writes=["f_w"], dma=True)
            P.op("sp", lambda e: e.dma_start(out=w3[:], in_=self.hy_w3[j]), writes=["f_w"], dma=True)
            P.op("sp", lambda e: e.dma_start(out=fc_[:], in_=self.hy_fcols[j]), writes=["f_fc"], dma=True)
            P.op("sp", lambda e: e.dma_start(out=drow[:], in_=self.hy_drow[:, :]), writes=["f_drow"], dma=True)
            P.op("sp", lambda e: e.dma_start(out=tcol[:], in_=self.hy_ntcol[L][:, :]), writes=["f_tcol"], dma=True)
            for i_ in range(2):
                P.op("dve", lambda e: e.tensor_tensor(out=fc_[:, 4 + i_:5 + i_], in0=fc_[:, i_:i_ + 1], in1=fc_[:, 2 + i_:3 + i_], op=ALU.mult), reads=["f_fc"], writes=["f_fc"])
            for tb in range(nt):
                if tb % 4 == 0:
                    c0 = tb * 128
                    P.op("sp", lambda e: e.dma_start(out=zp[:], in_=self.hy_zpos[L][:, c0:c0 + 512]), writes=["f_zp"], dma=True)
                    ph, kph = psH.next()
                    P.op("pe", lambda e: e.matmul(ph[0:64, :], lhsT=w1[:], rhs=zp[:], start=True, stop=True), reads=["f_w", "f_zp"], writes=[kph])
                    P.op("dve", lambda e: e.tensor_scalar(out=h1[:], in0=ph[0:64, :], scalar1=fc_[:, 2:3], scalar2=fc_[:, 4:5], op0=ALU.mult, op1=ALU.add), reads=[kph, "f_fc"], writes=["f_h1"])
                    self._range_reduce(h1[:], tq[:], "f_h1", "f_tq", 512)
                    P.op("act", lambda e: e.activation(out=h1[:], in_=h1[:], func=AF.Sin), reads=["f_h1"], writes=["f_h1"])
                    ph, kph = psH.next()
                    P.op("pe", lambda e: e.matmul(ph[0:64, :], lhsT=w2[:], rhs=h1[:], start=True, stop=True), reads=["f_w", "f_h1"], writes=[kph])
                    P.op("dve", lambda e: e.tensor_scalar(out=h2[:], in0=ph[0:64, :], scalar1=fc_[:, 3:4], scalar2=fc_[:, 5:6], op0=ALU.mult, op1=ALU.add), reads=[kph, "f_fc"], writes=["f_h2"])
                    self._range_reduce(h2[:], tq[:], "f_h2", "f_tq", 512)
                    P.op("act", lambda e: e.activation(out=h2[:], in_=h2[:], func=AF.Sin), reads=["f_h2"], writes=["f_h2"])
                cc = (tb % 4) * 128
                P.op("act", lambda e: e.activation(out=dec[:], in_=drow[:], func=AF.Exp, scale=tcol[:, tb:tb + 1]), reads=["f_drow", "f_tcol"], writes=["f_dec"])
                for g4 in range(4):
                    pk, kpk = psK.next()
                    P.op("pe", lambda e: e.matmul(pk[:], lhsT=h2[:, cc:cc + 128], rhs=w3[:, g4 * 512:(g4 + 1) * 512], start=True, stop=True), reads=["f_h2", "f_w"], writes=[kpk])
                    P.op("dve", lambda e: e.tensor_tensor(out=kf[:, g4 * 512:(g4 + 1) * 512], in0=pk[:], in1=dec[:], op=ALU.mult), reads=[kpk, "f_dec"], writes=[("f_kf", g4)])
                kfk = [("f_kf", g4) for g4 in range(4)]
                if tb == 0:
                    P.op("dve", lambda e: e.memset(kf[0:1, 1024:2048], 0.0), reads=kfk, writes=kfk)
                kb_, kkb = kb.next()
                P.op("pool", lambda e: e.tensor_copy(out=kb_[:], in_=kf[:]), reads=kfk, writes=[kkb])
                P.op("pool", lambda e: e.dma_start(out=KT[tb * 128:(tb + 1) * 128, :], in_=kb_[:]), reads=[kkb], dma=True)
                P.op("act", lambda e: e.activation(out=ka[:], in_=kf[:], func=AF.Abs), reads=kfk, writes=["f_ka"])
                for g4 in range(4):
                    P.op("pe", lambda e: e.matmul(psNm[g4][:], lhsT=self.ones_bf[:], rhs=ka[:, g4 * 512:(g4 + 1) * 512], start=(tb == 0), stop=(tb == nt - 1)),
                         reads=["f_ka", "ones_bf"], writes=[("psNm", g4)])
            for o in range(2):
                P.op("dve", lambda e: e.tensor_copy(out=rn[:, o * 512:(o + 1) * 512], in_=psNm[o][:]), reads=[("psNm", o)], writes=["f_rn"])
                P.op("dve", lambda e: e.tensor_tensor(out=rn[:, o * 512:(o + 1) * 512], in0=rn[:, o * 512:(o + 1) * 512], in1=psNm[2 + o][:], op=ALU.add),
                     reads=["f_rn", ("psNm", 2 + o)], writes=["f_rn"])
            P.op("dve", lambda e: e.reciprocal(out=rn[:], in_=rn[:]), reads=["f_rn"], writes=["f_rn"])
            P.op("pool", lambda e: e.dma_start(out=self.HYRN[:, :], in_=rn[:]), reads=["f_rn"], dma=True)
            P.barrier()
        with ExitStack() as s2:
            def sb(name, shape, dt=F32):
                return self.sbt(s2, name, shape, dt)
            rn = sb("k_rn", [128, 1024])
            kt = sb("k_kt", [128, nt, 2, 256], BF16)
            wc = Rot([sb("k_wc%d" % i, [128, nt, 128], BF16) for i in range(2)], "k_wc")
            ws = Rot([sb("k_ws%d" % i, [128, nt, 128], BF16) for i in range(2)], "k_ws")
            ko_ = Rot([sb("k_ko%d" % i, [128, 2, 256]) for i in range(2)], "k_ko")
            ps4 = [Rot(self.psum[2 * i:2 * i + 2], "psF%d" % i) for i in range(4)]
            P.op("sp", lambda e: e.dma_start(out=rn[:], in_=self.HYRN[:, :]), writes=["k_rn"], dma=True)
            for o in range(2):
                for hf in range(2):
                    for d in range(2):
                        c0 = d * 1024 + o * 512 + hf * 256
                        P.op("sp", lambda e: e.dma_start(out=kt[:, :, d, :], in_=KT[0:L, c0:c0 + 256].rearrange("(tc p) c -> p tc c", p=128)), writes=[("k_kt", d)], dma=True)
                    for fc in range(nt):
                        wc_, kwc = wc.next(); ws_, kws = ws.next()
                        P.op("sp", lambda e: e.dma_start(out=wc_[:], in_=DF[0, fc]), writes=[kwc], dma=True)
                        P.op("act", lambda e: e.dma_start(out=ws_[:], in_=DF[1, fc]), writes=[kws], dma=True)
                        accC, kaC = ps4[0].next()
                        accS, kaS = ps4[1].next()
                        for tc in range(nt):
                            rhs_ = kt[:, tc, :, :].rearrange("p d c -> p (d c)")
                            P.op("pe", lambda e: e.matmul(accC[:], lhsT=wc_[:, tc, :], rhs=rhs_, start=(tc == 0), stop=(tc == nt - 1)),
                                 reads=[kwc, ("k_kt", 0), ("k_kt", 1)], writes=[kaC])
                            P.op("pe", lambda e: e.matmul(accS[:], lhsT=ws_[:, tc, :], rhs=rhs_, start=(tc == 0), stop=(tc == nt - 1)),
                                 reads=[kws, ("k_kt", 0), ("k_kt", 1)], writes=[kaS])
                        o_, kko = ko_.next()
                        rs = rn[:, o * 512 + hf * 256:o * 512 + hf * 256 + 256]
                        P.op("dve", lambda e: e.tensor_copy(out=o_[:, 0, :], in_=accC[:, 0:256]), reads=[kaC], writes=[kko])
                        P.op("dve", lambda e: e.tensor_tensor(out=o_[:, 0, :], in0=o_[:, 0, :], in1=accC[:, 256:512], op=ALU.add), reads=[kko, kaC], writes=[kko])
                        P.op("dve", lambda e: e.tensor_tensor(out=o_[:, 0, :], in0=o_[:, 0, :], in1=rs, op=ALU.mult), reads=[kko, "k_rn"], writes=[kko])
                        P.op("dve", lambda e: e.tensor_copy(out=o_[:, 1, :], in_=accS[:, 256:512]), reads=[kaS], writes=[kko])
                        P.op("dve", lambda e: e.tensor_tensor(out=o_[:, 1, :], in0=o_[:, 1, :], in1=accS[:, 0:256], op=ALU.subtract), reads=[kko, kaS], writes=[kko])
                        P.op("dve", lambda e: e.tensor_tensor(out=o_[:, 1, :], in0=o_[:, 1, :], in1=rs, op=ALU.mult), reads=[kko, "k_rn"], writes=[kko])
                        P.op("pool", lambda e: e.dma_start(out=KF[o, hf, fc], in_=o_[:]), reads=[kko], dma=True)
            P.barrier()
        with ExitStack() as s2:
            def sb(name, shape, dt=F32):
                return self.sbt(s2, name, shape, dt)
            hb = sb("c_hb", [128, 4, 2])
            zt = sb("c_zt", [128, nt, 256], BF16)
            Pp = sb("c_P", [128, nt, 2, 256], BF16)
            wc = Rot([sb("c_wc%d" % i, [128, nt, 128], BF16) for i in range(2)], "c_wc")
            ws = Rot([sb("c_ws%d" % i, [128, nt, 128], BF16) for i in range(2)], "c_ws")
            kfR = Rot([sb("c_kf%d" % i, [128, 2, 256]) for i in range(2)], "c_kf")
            zin = Rot([sb("c_zin%d" % i, [128, 512]) for i in range(2)], "c_zin")
            zb = Rot([sb("c_zb%d" % i, [128, 512], BF16) for i in range(2)], "c_zb")
            ta_ = sb("c_ta", [128, 256]); tb_ = sb("c_tb", [128, 256])
            ytk = Rot([sb("c_ytk%d" % i, [128, 256]) for i in range(2)], "c_ytk")
            ycm = sb("c_ycm", [128, 2, 512]); gx = sb("c_gx", [128, 512]); zc = sb("c_zc", [128, 512]); ob = sb("c_ob", [128, 512], BF16)
            psX = [Rot(self.psum[0:2], "psXr"), Rot(self.psum[2:4], "psXs")]
            psY = Rot(self.psum[4:6], "psY")
            psT = Rot(self.psum[6:8], "psT")
            P.op("sp", lambda e: e.dma_start(out=hb[:], in_=self.hy_bias[j]), writes=["c_hb"], dma=True)
            for o in range(2):
                for hf in range(2):
                    zsrc = HC[0] if o == 0 else HZ
                    for ct2 in range(2):
                        r0 = hf * 256 + ct2 * 128
                        for t0 in range(0, L, 512):
                            zi, kzi = zin.next()
                            P.op("sp", lambda e: e.dma_start(out=zi[:], in_=zsrc[r0:r0 + 128, t0:t0 + 512]), writes=[kzi], dma=True)
                            zb_, kzb = zb.next()
                            P.op("pool", lambda e: e.tensor_copy(out=zb_[:], in_=zi[:]), reads=[kzi], writes=[kzb])
                            pt, kpt = psT.next()
                            for sbk in range(4):
                                P.op("pe", lambda e: e.matmul(pt[:, sbk * 128:(sbk + 1) * 128], lhsT=zi[:, sbk * 128:(sbk + 1) * 128], rhs=self.ident[:], start=True, stop=True),
                                     reads=[kzi, "ident"], writes=[kpt])
                            P.op("act", lambda e: e.copy(out=zt[:, t0 // 128:t0 // 128 + 4, ct2 * 128:(ct2 + 1) * 128], in_=pt[:].rearrange("p (a b) -> p a b", a=4)),
                                 reads=[kpt], writes=["c_zt"])
                    for fc in range(nt):
                        wc_, kwc = wc.next(); ws_, kws = ws.next()
                        P.op("sp", lambda e: e.dma_start(out=wc_[:], in_=DF[0, fc]), writes=[kwc], dma=True)
                        P.op("act", lambda e: e.dma_start(out=ws_[:], in_=DF[1, fc]), writes=[kws], dma=True)
                        kf_, kkf = kfR.next()
                        P.op("pool", lambda e: e.dma_start(out=kf_[:], in_=KF[o, hf, fc]), writes=[kkf], dma=True)
                        xr, kxr = psX[0].next(); xs, kxs = psX[1].next()
                        for tc in range(nt):
                            P.op("pe", lambda e: e.matmul(xr[:, 0:256], lhsT=wc_[:, tc, :], rhs=zt[:, tc, :], start=(tc == 0), stop=(tc == nt - 1)), reads=[kwc, "c_zt"], writes=[kxr])
                            P.op("pe", lambda e: e.matmul(xs[:, 0:256], lhsT=ws_[:, tc, :], rhs=zt[:, tc, :], start=(tc == 0), stop=(tc == nt - 1)), reads=[kws, "c_zt"], writes=[kxs])
                        P.op("dve", lambda e: e.tensor_tensor(out=ta_[:], in0=xr[:, 0:256], in1=kf_[:, 0, :], op=ALU.mult), reads=[kxr, kkf], writes=["c_ta"])
                        P.op("dve", lambda e: e.tensor_tensor(out=tb_[:], in0=xs[:, 0:256], in1=kf_[:, 1, :], op=ALU.mult), reads=[kxs, kkf], writes=["c_tb"])
                        P.op("dve", lambda e: e.tensor_tensor(out=Pp[:, fc, 0, :], in0=ta_[:], in1=tb_[:], op=ALU.add), reads=["c_ta", "c_tb"], writes=[("c_P", fc)])
                        P.op("dve", lambda e: e.tensor_tensor(out=ta_[:], in0=xs[:, 0:256], in1=kf_[:, 0, :], op=ALU.mult), reads=[kxs, kkf], writes=["c_ta"])
                        P.op("dve", lambda e: e.tensor_tensor(out=tb_[:], in0=xr[:, 0:256], in1=kf_[:, 1, :], op=ALU.mult), reads=[kxr, kkf], writes=["c_tb"])
                        P.op("dve", lambda e: e.tensor_tensor(out=Pp[:, fc, 1, :], in0=ta_[:], in1=tb_[:], op=ALU.subtract), reads=["c_ta", "c_tb"], writes=[("c_P", fc)])
                    pk = [("c_P", fc) for fc in range(nt)]
                    for tc in range(nt):
                        wc_, kwc = wc.next(); ws_, kws = ws.next()
                        P.op("sp", lambda e: e.dma_start(out=wc_[:], in_=DI[0, tc]), writes=[kwc], dma=True)
                        P.op("act", lambda e: e.dma_start(out=ws_[:], in_=DI[1, tc]), writes=[kws], dma=True)
                        py, kpy = psY.next()
                        for fc in range(nt):
                            P.op("pe", lambda e: e.matmul(py[:, 0:256], lhsT=wc_[:, fc, :], rhs=Pp[:, fc, 0, :], start=(fc == 0), stop=False), reads=[kwc] + pk, writes=[kpy])
                            P.op("pe", lambda e: e.matmul(py[:, 0:256], lhsT=ws_[:, fc, :], rhs=Pp[:, fc, 1, :], start=False, stop=(fc == nt - 1)), reads=[kws] + pk, writes=[kpy])
                        yk_, kyk = ytk.next()
                        P.op("act", lambda e: e.activation(out=yk_[:], in_=py[:, 0:256], func=AF.Copy, scale=1.0 / L), reads=[kpy], writes=[kyk])
                        pt, kpt = psT.next()
                        for ct2 in range(2):
                            P.op("pe", lambda e: e.matmul(pt[:, ct2 * 128:(ct2 + 1) * 128], lhsT=yk_[:, ct2 * 128:(ct2 + 1) * 128], rhs=self.ident[:], start=True, stop=True),
                                 reads=[kyk, "ident"], writes=[kpt])
                        P.op("act", lambda e: e.copy(out=ycm[:, :, (tc % 4) * 128:(tc % 4 + 1) * 128], in_=pt[:, 0:256].rearrange("p (a b) -> p a b", a=2)),
                             reads=[kpt], writes=[("c_ycm", tc % 4)])
                        if tc % 4 == 3:
                            t0 = (tc - 3) * 128
                            for ct2 in range(2):
                                r0 = hf * 256 + ct2 * 128
                                ctg = hf * 2 + ct2
                                P.op("sp", lambda e: e.dma_start(out=zc[:], in_=zsrc[r0:r0 + 128, t0:t0 + 512]), writes=["c_zc"], dma=True)
                                P.op("sp", lambda e: e.dma_start(out=gx[:], in_=HC[1 + o, r0:r0 + 128, t0:t0 + 512]), writes=["c_gx"], dma=True)
                                P.op("dve", lambda e: e.scalar_tensor_tensor(out=zc[:], in0=zc[:], scalar=hb[:, ctg, o:o + 1], op0=ALU.mult, in1=ycm[:, ct2, :], op1=ALU.add),
                                     reads=["c_zc",AllK.y, "c_hb"] if False else ["c_zc", "c_hb"] + [("c_ycm", q_) for q_ in range(4)], writes=["c_zc"])
                                if o == 0:
                                    P.op("pool", lambda e: e.tensor_tensor(out=zc[:], in0=zc[:], in1=gx[:], op=ALU.mult), reads=["c_zc", "c_gx"], writes=["c_zc"])
                                    P.op("pool", lambda e: e.dma_start(out=HZ[r0:r0 + 128, t0:t0 + 512], in_=zc[:]), reads=["c_zc"], dma=True)
                                else:
                                    P.op("pool", lambda e: e.tensor_tensor(out=ob[:], in0=zc[:], in1=gx[:], op=ALU.mult), reads=["c_zc", "c_gx"], writes=["c_ob"])
                                    P.op("pool", lambda e: e.dma_start(out=self.YM[512 + r0:512 + r0 + 128, t0:t0 + 512], in_=ob[:]), reads=["c_ob"], dma=True)
                    P.barrier()

    def dense_pass(self, xin, yout, L, l):
        nc, P = self.nc, self.P
        first = (l == 0)
        last = (l == self.depth)
        NH = TT // 512
        with ExitStack() as s2:
            def sb(name, shape, dt=F32):
                return self.sbt(s2, name, shape, dt)
            xt = sb("d_xt", [128, 8, TT])
            xn = sb("d_xn", [128, 8, TT], BF16)
            rstd = sb("d_rstd", [128, TT])
            hm = sb("d_hm", [128, 22, TT], BF16)
            sg = Rot([sb("d_sg%d" % i, [128, 512]) for i in range(2)], "d_sg")
            wgR = Rot([sb("d_wg%d" % i, [128, 8, 256], BF16) for i in range(3)], "d_wg")
            wuR = Rot([sb("d_wu%d" % i, [128, 8, 256], BF16) for i in range(3)], "d_wu")
            wdR = Rot([sb("d_wd%d" % i, [128, 22, 128], BF16) for i in range(2)], "d_wd")
            wpR = Rot([sb("d_wp%d" % i, [128, 8, 128], BF16) for i in range(3)], "d_wp")
            pjs = Rot([sb("d_pj%d" % i, [128, 512]) for i in range(2)], "d_pj")
            xtok = sb("d_xtok", [128, D])
            psG = Rot(self.psum[0:2], "psG")
            psU = Rot(self.psum[2:4], "psU")
            psA = Rot(self.psum[4:6], "psA")
            psN = Rot(self.psum[6:8], "psN")
            hs = lambda h: slice(h * 512, (h + 1) * 512)
            hmk = [("d_hm", c) for c in range(22)]

            def rmsnorm():
                P.op("act", lambda e: e.activation(out=hm[:, 0:8, :].rearrange("p a b -> p (a b)"),
                                                   in_=xt[:].rearrange("p a b -> p (a b)"), func=AF.Square),
                     reads=["d_xt"], writes=hmk[0:8])
                for h in range(NH):
                    pn, kn = psN.next()
                    for kc in range(8):
                        P.op("pe", lambda e: e.matmul(pn[:], lhsT=self.ones_bf[:], rhs=hm[:, kc, hs(h)], start=(kc == 0), stop=(kc == 7)),
                             reads=[("d_hm", kc), "ones_bf"], writes=[kn])
                    P.op("dve", lambda e: e.tensor_scalar(out=rstd[:, hs(h)], in0=pn[:], scalar1=1.0 / D, scalar2=EPS,
                                                          op0=ALU.mult, op1=ALU.add), reads=[kn], writes=["d_rstd"])
                P.op("act", lambda e: e.activation(out=rstd[:], in_=rstd[:], func=AF.Sqrt), reads=["d_rstd"], writes=["d_rstd"])
                P.op("dve", lambda e: e.reciprocal(out=rstd[:], in_=rstd[:]), reads=["d_rstd"], writes=["d_rstd"])

            def normed(which, out_t, key):
                rmsnorm()
                for kc in range(8):
                    P.op("dve", lambda e: e.scalar_tensor_tensor(out=out_t[:, kc, :], in0=xt[:, kc, :],
                                                                 scalar=self.gcols[:, which, kc:kc + 1], op0=ALU.mult,
                                                                 in1=rstd[:], op1=ALU.mult),
                         reads=["d_xt", "d_rstd", "gcols"], writes=[key])

            def ffn(f, lay):
                normed((0 if f == 1 else 2) * DEPTH + lay, xn, "d_xn")
                for mp in range(11):
                    wg, kg = wgR.next()
                    wu, ku = wuR.next()
                    P.op("sp", lambda e: e.dma_start(out=wg[:], in_=self.wb[("g", f, lay)][mp]), writes=[kg], dma=True)
                    P.op("sp", lambda e: e.dma_start(out=wu[:], in_=self.wb[("u", f, lay)][mp]), writes=[ku], dma=True)
                    for mi in range(2):
                        mc = 2 * mp + mi
                        for h in range(NH):
                            pg, kpg = psG.next()
                            pu, kpu = psU.next()
                            for kc in range(8):
                                P.op("pe", lambda e: e.matmul(pg[:], lhsT=wg[:, kc, mi * 128:(mi + 1) * 128], rhs=xn[:, kc, hs(h)],
                                                              start=(kc == 0), stop=(kc == 7)), reads=[kg, "d_xn"], writes=[kpg])
                            for kc in range(8):
                                P.op("pe", lambda e: e.matmul(pu[:], lhsT=wu[:, kc, mi * 128:(mi + 1) * 128], rhs=xn[:, kc, hs(h)],
                                                              start=(kc == 0), stop=(kc == 7)), reads=[ku, "d_xn"], writes=[kpu])
                            s_, ks = sg.next()
                            P.op("act", lambda e: e.activation(out=s_[:], in_=pg[:], func=AF.Silu), reads=[kpg], writes=[ks])
                            P.op("dve", lambda e: e.tensor_tensor(out=hm[:, mc, hs(h)], in0=s_[:], in1=pu[:], op=ALU.mult),
                                 reads=[ks, kpu], writes=[("d_hm", mc)])
                for dc in range(8):
                    wd, kd = wdR.next()
                    P.op("sp", lambda e: e.dma_start(out=wd[:], in_=self.wb[("d", f, lay)][dc]), writes=[kd], dma=True)
                    for h in range(NH):
                        pa, kpa = psA.next()
                        for fc in range(22):
                            P.op("pe", lambda e: e.matmul(pa[:], lhsT=wd[:, fc, :], rhs=hm[:, fc, hs(h)], start=(fc == 0), stop=(fc == 21)),
                                 reads=[kd, ("d_hm", fc)], writes=[kpa])
                        P.op("dve", lambda e: e.scalar_tensor_tensor(out=xt[:, dc, hs(h)], in0=pa[:], scalar=0.5, op0=ALU.mult,
                                                                     in1=xt[:, dc, hs(h)], op1=ALU.add),
                             reads=[kpa, "d_xt"], writes=["d_xt"])

            for ti in range(L // TT):
                t0 = ti * TT
                if first:
                    for b4 in range(TT // 128):
                        P.op("sp", lambda e: e.dma_start(out=xtok[:], in_=xin[t0 + b4 * 128:t0 + (b4 + 1) * 128, :]),
                             writes=["d_xtok"], dma=True)
                        for half in range(2):
                            pn, kn = psN.next()
                            for q in range(4):
                                dc = half * 4 + q
                                P.op("pe", lambda e: e.matmul(pn[:, q * 128:(q + 1) * 128], lhsT=xtok[:, dc * 128:(dc + 1) * 128],
                                                              rhs=self.ident[:], start=True, stop=True),
                                     reads=["d_xtok", "ident"], writes=[kn])
                            P.op("act", lambda e: e.copy(out=xt[:, half * 4:half * 4 + 4, b4 * 128:(b4 + 1) * 128],
                                                         in_=pn[:].rearrange("p (a b) -> p a b", a=4)),
                                 reads=[kn], writes=["d_xt"])
                else:
                    P.op("sp", lambda e: e.dma_start(out=xt[:], in_=self.XT.rearrange("(dc p) t -> p dc t", p=128)[:, :, t0:t0 + TT]),
                         writes=["d_xt"], dma=True)
                    lay = l - 1
                    P.op("sp", lambda e: e.dma_start(out=xn[:], in_=self.YM.rearrange("(dc p) t -> p dc t", p=128)[:, :, t0:t0 + TT]),
                         writes=["d_xn"], dma=True)
                    for dc in range(8):
                        wp, kp = wpR.next()
                        P.op("sp", lambda e: e.dma_start(out=wp[:], in_=self.wb[("out", lay)][dc]), writes=[kp], dma=True)
                        for h in range(NH):
                            pa, kpa = psA.next()
                            for kc in range(8):
                                P.op("pe", lambda e: e.matmul(pa[:], lhsT=wp[:, kc, :], rhs=xn[:, kc, hs(h)], start=(kc == 0), stop=(kc == 7)),
                                     reads=[kp, "d_xn"], writes=[kpa])
                            P.op("dve", lambda e: e.tensor_tensor(out=xt[:, dc, hs(h)], in0=pa[:], in1=xt[:, dc, hs(h)], op=ALU.add),
                                 reads=[kpa, "d_xt"], writes=["d_xt"])
                    ffn(2, lay)
                if not last:
                    ffn(1, l)
                    normed(1 * DEPTH + l, xn, "d_xn")
                    cin = AB_IN if l % 2 == 0 else CD_IN
                    for ci in range((cin + 127) // 128):
                        w = min(128, cin - ci * 128)
                        wp, kp = wpR.next()
                        P.op("sp", lambda e: e.dma_start(out=wp[:], in_=self.wb[("in", l)][ci]), writes=[kp], dma=True)
                        for h in range(NH):
                            pa, kpa = psA.next()
                            for kc in range(8):
                                P.op("pe", lambda e: e.matmul(pa[0:w, :], lhsT=wp[:, kc, 0:w], rhs=xn[:, kc, hs(h)], start=(kc == 0), stop=(kc == 7)),
                                     reads=[kp, "d_xn"], writes=[kpa])
                            pj, kj = pjs.next()
                            P.op("act", lambda e: e.copy(out=pj[0:w, :], in_=pa[0:w, :]), reads=[kpa], writes=[kj])
                            P.op("pool", lambda e: e.dma_start(out=self.PJ[ci * 128:ci * 128 + w, t0 + h * 512:t0 + (h + 1) * 512], in_=pj[0:w, :]),
                                 reads=[kj], dma=True)
                    P.op("pool", lambda e: e.dma_start(out=self.XT.rearrange("(dc p) t -> p dc t", p=128)[:, :, t0:t0 + TT], in_=xt[:]),
                         reads=["d_xt"], dma=True)
                else:
                    rmsnorm()
                    for kc in range(8):
                        P.op("dve", lambda e: e.scalar_tensor_tensor(out=xt[:, kc, :], in0=xt[:, kc, :],
                                                                     scalar=self.gcols[:, 3 * DEPTH, kc:kc + 1], op0=ALU.mult,
                                                                     in1=rstd[:], op1=ALU.mult),
                             reads=["d_xt", "d_rstd", "gcols"], writes=["d_xt"])
                    for b4 in range(TT // 128):
                        for half in range(2):
                            pn, kn = psN.next()
                            for q in range(4):
                                dc = half * 4 + q
                                P.op("pe", lambda e: e.matmul(pn[:, q * 128:(q + 1) * 128], lhsT=xt[:, dc, b4 * 128:(b4 + 1) * 128],
                                                              rhs=self.ident[:], start=True, stop=True),
                                     reads=["d_xt", "ident"], writes=[kn])
                            P.op("act", lambda e: e.copy(out=xtok[:, half * 512:(half + 1) * 512], in_=pn[:]),
                                 reads=[kn], writes=["d_xtok"])
                        P.op("pool", lambda e: e.dma_start(out=yout[t0 + b4 * 128:t0 + (b4 + 1) * 128, :], in_=xtok[:]),
                             reads=["d_xtok"], dma=True)


_CACHE = {}


def _host_layout(inputs):
    f = lambda a: np.ascontiguousarray(np.asarray(a, dtype=np.float32))
    shared = {}
    for nm in ("ffn1_w_gate", "ffn1_w_up", "ffn1_w_down", "ffn2_w_gate", "ffn2_w_up", "ffn2_w_down",
               "ab_w_in", "ab_w_out", "cd_w_in", "cd_w_out"):
        shared[nm] = f(inputs[nm])
    norms = np.concatenate([f(inputs["ffn1_norm"]), f(inputs["mix_norm"]), f(inputs["ffn2_norm"]),
                            f(inputs["final_norm"])[None, :]], axis=0)
    shared["norms"] = np.ascontiguousarray(norms.reshape(3 * DEPTH + 1, 8, 128).transpose(2, 0, 1))
    cw, cb = f(inputs["lru_conv_w"]), f(inputs["lru_conv_b"])
    lam, ba, bx = f(inputs["lru_lambda"]), f(inputs["lru_ba"]), f(inputs["lru_bx"])
    colv = [cw[:, 0], cw[:, 1], cw[:, 2], cw[:, 3], cb]
    for d in range(2):
        colv += [lam[:, d], ba[:, d], bx[:, d]]
    colv = np.stack(colv, axis=-1)
    shared["lru_cols"] = np.ascontiguousarray(colv.reshape(2, 4, 128, 11).transpose(0, 2, 1, 3))
    wa, wx = f(inputs["lru_wa"]), f(inputs["lru_wx"])
    wbd = np.zeros((2, 4, 128, 4, 128), np.float32)
    for d in range(2):
        for wi, wsrc in enumerate((wa, wx)):
            for ct in range(4):
                for hh in range(2):
                    wbd[:, ct, hh * 64:(hh + 1) * 64, 2 * d + wi, hh * 64:(hh + 1) * 64] = wsrc[:, d, 2 * ct + hh]
    shared["lru_wbd"] = wbd
    lre, lim, lst = f(inputs["s5_lambda_re"]), f(inputs["s5_lambda_im"]), f(inputs["s5_log_step"])
    prm = np.stack([lre, lim, np.broadcast_to(lst[..., None], lre.shape)], axis=-1)
    prm = prm.reshape(2, 2, 16, 2, 64, 3).transpose(0, 3, 4, 2, 1, 5)
    shared["s5_prm"] = np.ascontiguousarray(prm.reshape(2, 128, 32, 3))
    bre, bim = f(inputs["s5_b_re"]), f(inputs["s5_b_im"])
    bt = np.zeros((2, 16, 128, 2, 2, 128), np.float32)
    for ri, bsrc in enumerate((bre, bim)):
        for g in range(32):
            T, gp, gl = g // 2, (g % 8) // 2, g % 2
            r0 = gp * 32 + gl * 16
            bt[:, T, r0:r0 + 16, :, ri, gl * 64:(gl + 1) * 64] = bsrc[:, :, g].transpose(0, 3, 1, 2)
    shared["s5_bt"] = bt
    cre_, cim_ = f(inputs["s5_c_re"]), f(inputs["s5_c_im"])
    ct = np.zeros((2, 16, 2, 128, 2, 32), np.float32)
    for ri, csrc in enumerate((cre_, cim_)):
        for g in range(32):
            T, gl = g // 2, g % 2
            ct[:, T, :, gl * 64:(gl + 1) * 64, ri, gl * 16:(gl + 1) * 16] = csrc[:, :, g].transpose(0, 1, 3, 2)
    shared["s5_ct"] = ct
    pc = np.stack([f(inputs["s5_d"]), f(inputs["s5_glu_b"])], axis=-1)
    shared["s5_pc"] = np.ascontiguousarray(pc.reshape(2, 4, 128, 2).transpose(0, 2, 1, 3))
    shared["s5_glu_w"] = f(inputs["s5_glu_w"])
    mu = f(inputs["rw_mu"])
    mut = np.zeros((2, 14, 128), np.float32)
    mut[:, 0:12] = mu[:, 0:1536].reshape(2, 12, 128)
    mut[:, 12, 0:96] = mu[:, 1536:1632]
    mut[:, 13, 0:64] = mu[:, 1632:1696]
    shared["rw_mu"] = np.ascontiguousarray(mut.transpose(0, 2, 1))
    w0 = f(inputs["rw_w0"])
    pcs = np.stack([w0[:, 0], w0[:, 1], f(inputs["rw_a0"]), f(inputs["rw_k_k"]), f(inputs["rw_k_a"]),
                    f(inputs["rw_r_k"]).reshape(2, 512), f(inputs["rw_ln_w"]), f(inputs["rw_ln_b"])], axis=-1)
    shared["rw_pc"] = np.ascontiguousarray(pcs.reshape(2, 4, 128, 8).transpose(0, 2, 1, 3))
    lwt = np.zeros((2, 128, 2, 512), np.float32)
    wup = f(inputs["rw_w_up"])
    lwt[:, 0:32, 0] = wup[:, 0]
    lwt[:, 32:64, 0] = wup[:, 1]
    lwt[:, 64:96, 0] = f(inputs["rw_a_up"])
    lwt[:, 0:64, 1] = f(inputs["rw_g_up"])
    shared["rw_lw"] = lwt
    hcw, hcb = f(inputs["hy_conv_w"]), f(inputs["hy_conv_b"])
    hcols = np.stack([hcw[:, 0], hcw[:, 1], hcw[:, 2], hcb], axis=-1)
    shared["hy_cols"] = np.ascontiguousarray(hcols.reshape(2, 12, 128, 4).transpose(0, 2, 1, 3))
    shared["hy_w1"] = f(inputs["hy_f_w1"]); shared["hy_w2"] = f(inputs["hy_f_w2"]); shared["hy_w3"] = f(inputs["hy_f_w3"])
    fq = f(inputs["hy_f_freq"])
    z64 = np.zeros((2, 64), np.float32)
    shared["hy_fcols"] = np.ascontiguousarray(np.stack([f(inputs["hy_f_b1"]), f(inputs["hy_f_b2"]), fq[:, 0], fq[:, 1], z64, z64], axis=-1))
    hbias = f(inputs["hy_bias"])
    shared["hy_bias"] = np.ascontiguousarray(hbias.reshape(2, 2, 4, 128).transpose(0, 3, 2, 1))
    return shared


def _hy_consts(L):
    t = np.linspace(0.0, 1.0, L, dtype=np.float32)[:, None]
    bands = 16
    freqs = np.linspace(1e-4, bands - 1, bands, dtype=np.float32)[None, :]
    wpos = (np.float32(2.0 * math.pi / L) * np.arange(L, dtype=np.float32))[:, None]
    z = np.concatenate([t, np.cos(freqs * wpos), -np.sin(freqs * wpos)], axis=-1).astype(np.float32)
    ntcol = np.ascontiguousarray((-t[:, 0]).reshape(L // 128, 128).T)
    return np.ascontiguousarray(z.T), ntcol


def _hy_drow():
    max_decay = math.log(1e-2) / 0.3
    min_decay = math.log(1e-2) / 1.5
    deltas = np.abs(np.linspace(min_decay, max_decay, 512, dtype=np.float32))
    return np.ascontiguousarray(np.broadcast_to(deltas[None, :], (128, 512))).astype(np.float32)


def run(inputs, LP, LS, enable, depth=DEPTH, ncores=8):
    key = (LP, LS, tuple(sorted(enable.items())), depth)
    if key not in _CACHE:
        _CACHE[key] = Builder(LP, LS, enable, depth).build()
    nc = _CACHE[key]
    shared = _host_layout(inputs)
    shared["hy_drow"] = _hy_drow()
    for LL in sorted(set((LP, LS))):
        zT, ntc = _hy_consts(LL)
        shared["hy_zpos_%d" % LL] = zT
        shared["hy_ntcol_%d" % LL] = ntc
    xp = np.asarray(inputs["x_prompt"], dtype=np.float32)
    xs = np.asarray(inputs["x_sample"], dtype=np.float32)
    in_maps = []
    for c in range(ncores):
        m = dict(shared)
        m["xp"] = np.ascontiguousarray(xp[c])
        m["xs"] = np.ascontiguousarray(xs[c % 2])
        in_maps.append(m)
    res = run_bass_kernel_spmd(nc, in_maps, core_ids=list(range(ncores)))
    yp = np.stack([res.results[c]["yp"] for c in range(ncores)], axis=0).astype(np.float32)
    ys = np.stack([res.results[c]["ys"] for c in range(2)], axis=0).astype(np.float32)
    return yp, ys


ENABLE = dict(s5=True, rwkv=True, lru=True, hyena=True)


def kernel(**inputs):
    return run(inputs, 4096, 8192, ENABLE)
```
